# Optimizing a Trainium2 kernel written in Bass

```python
import jax, jax.numpy as jnp
from jax import lax
import numpy as np

D_MODEL = 1024
BATCH = 8
SEQ = 2048
DEPTH = 2

CHUNK = 64
Q_BLOCK = 128
EPS = 1e-6
NEG = -1e30
MIX_WIDTH = 1024
POOL_WIDTH = 512
POOL_GROUPS = 4
POOL_GROUP_DIM = POOL_WIDTH // POOL_GROUPS
POOL_WINDOWS = (2, 4, 8, 16)
HEAD_DIM = 64
DSA_HEADS = 8
DSA_WIDTH = DSA_HEADS * HEAD_DIM
IDX_HEADS = 4
IDX_DIM = 64
TOPK_MAX = 256
ROPE_THETA = 500000.0
ROPE_DIM = HEAD_DIM // 4
FOX_HEADS = 8
FOX_WIDTH = FOX_HEADS * HEAD_DIM
SGU_WIDTH = 512
SGU_GROUPS = 4
SGU_GROUP_DIM = SGU_WIDTH // SGU_GROUPS
SGU_CHUNK = 128
D_FF = 2816
CONV_W = 3
PLE_DIM = 256
N_EVEN = (DEPTH + 1) // 2
N_ODD = DEPTH // 2
EVEN_IN = POOL_WIDTH + DSA_WIDTH + 2 * HEAD_DIM + IDX_HEADS * IDX_DIM + IDX_DIM + IDX_HEADS
ODD_IN = 3 * FOX_WIDTH + FOX_HEADS + 2 * SGU_WIDTH

kernel_name = 'hybrid_chunk_causal_pool_dsa_fox_sgu'


def rmsnorm(x, g):
    xf = x.astype(jnp.float32)
    y = xf * lax.rsqrt(jnp.mean(xf * xf, axis=-1, keepdims=True) + EPS) * g.astype(jnp.float32)
    return y.astype(x.dtype)


def partial_rope(x, pos):
    half = ROPE_DIM // 2
    inv = ROPE_THETA ** (-jnp.arange(half, dtype=jnp.float32) / half)
    ang = pos.astype(jnp.float32)[..., None] * inv
    cos = jnp.cos(ang)[:, :, None, :]
    sin = jnp.sin(ang)[:, :, None, :]
    xf = x.astype(jnp.float32)
    x1, x2 = xf[..., :half], xf[..., half:ROPE_DIM]
    out = jnp.concatenate([x1 * cos - x2 * sin, x2 * cos + x1 * sin, xf[..., ROPE_DIM:]], axis=-1)
    return out.astype(x.dtype)


def pool_mixer(z, w_pool, s_pool):
    B, S, _ = z.shape
    zf = z.astype(jnp.float32)
    cs = jnp.concatenate([jnp.zeros_like(zf[:, :1]), jnp.cumsum(zf, axis=1)], axis=1)
    t = jnp.arange(S)
    outs = []
    for g, w in enumerate(POOL_WINDOWS):
        sl = slice(g * POOL_GROUP_DIM, (g + 1) * POOL_GROUP_DIM)
        lo = jnp.maximum(t + 1 - w, 0)
        cnt = (t + 1 - lo).astype(jnp.float32)
        cg = cs[..., sl]
        mean = (cg[:, 1:] - cg[:, lo]) / cnt[None, :, None]
        outs.append(mean - zf[..., sl])
    d = jnp.stack(outs, axis=2).astype(z.dtype)
    y = jnp.einsum('bsgc,gcd->bsgd', d, w_pool).reshape(B, S, POOL_WIDTH)
    return y * s_pool


def dsa_mixer(q, k, v, iq, ik, iw):
    B, S = q.shape[:2]
    n_sel = min(TOPK_MAX, S // 4)
    nb = S // Q_BLOCK
    s_chunk = jnp.arange(S) // CHUNK

    def block(n):
        start = n * Q_BLOCK
        qb = lax.dynamic_slice_in_dim(q, start, Q_BLOCK, axis=1)
        iqb = lax.dynamic_slice_in_dim(iq, start, Q_BLOCK, axis=1)
        iwb = lax.dynamic_slice_in_dim(iw, start, Q_BLOCK, axis=1)
        t_chunk = (start + jnp.arange(Q_BLOCK)) // CHUNK
        visible = s_chunk[None, :] <= t_chunk[:, None]
        dots = jax.nn.relu(jnp.einsum('bqhe,bse->bqsh', iqb, ik).astype(jnp.float32) * IDX_DIM ** -0.5)
        score = jnp.einsum('bqsh,bqh->bqs', dots, iwb.astype(jnp.float32) * IDX_HEADS ** -0.5)
        score = jnp.where(visible[None], score, NEG)
        _, idx = lax.top_k(score, n_sel)
        valid = (idx // CHUNK) <= t_chunk[None, :, None]
        k_g = jax.vmap(lambda a, i: a[i])(k, idx)
        v_g = jax.vmap(lambda a, i: a[i])(v, idx)
        logits = jnp.einsum('bqhd,bqkd->bqhk', qb, k_g).astype(jnp.float32) * HEAD_DIM ** -0.5
        logits = jnp.where(valid[:, :, None, :], logits, NEG)
        probs = jax.nn.softmax(logits, axis=-1).astype(v.dtype)
        return jnp.einsum('bqhk,bqkd->bqhd', probs, v_g)

    out = lax.map(block, jnp.arange(nb))
    return jnp.moveaxis(out, 0, 1).reshape(B, S, DSA_WIDTH)


def fox_mixer(q, k, v, f_raw, b_f):
    B, S, H, _ = q.shape
    logf = jax.nn.log_sigmoid(f_raw.astype(jnp.float32) + b_f.astype(jnp.float32))
    cum = jnp.moveaxis(jnp.cumsum(logf, axis=1), 1, 2)
    s_idx = jnp.arange(S)
    nb = S // Q_BLOCK

    def block(n):
        start = n * Q_BLOCK
        qb = lax.dynamic_slice_in_dim(q, start, Q_BLOCK, axis=1)
        cq = lax.dynamic_slice_in_dim(cum, start, Q_BLOCK, axis=2)
        t = start + jnp.arange(Q_BLOCK)
        logits = jnp.einsum('bqhd,bshd->bhqs', qb, k).astype(jnp.float32) * HEAD_DIM ** -0.5
        logits = logits + cq[..., None] - cum[:, :, None, :]
        logits = jnp.where((s_idx[None, :] <= t[:, None])[None, None], logits, NEG)
        probs = jax.nn.softmax(logits, axis=-1).astype(v.dtype)
        return jnp.einsum('bhqs,bshd->bqhd', probs, v)

    out = lax.map(block, jnp.arange(nb))
    return jnp.moveaxis(out, 0, 1).reshape(B, S, FOX_WIDTH)


def sgu_mixer(zs, ln_g, ln_b, w_s, b_s):
    B, S, _ = zs.shape
    a = jax.nn.gelu(zs)
    u, v = jnp.split(a, 2, axis=-1)
    vf = v.astype(jnp.float32)
    mu = jnp.mean(vf, axis=-1, keepdims=True)
    var = jnp.mean((vf - mu) ** 2, axis=-1, keepdims=True)
    v = ((vf - mu) * lax.rsqrt(var + EPS) * ln_g + ln_b).astype(zs.dtype)
    vc = v.reshape(B, S // SGU_CHUNK, SGU_CHUNK, SGU_GROUPS, SGU_GROUP_DIM)
    ci = jnp.arange(SGU_CHUNK) // CHUNK
    mask = ci[None, :] <= ci[:, None]
    w = jnp.where(mask[None], w_s, 0)
    mixed = jnp.einsum('gij,bnjgc->bnigc', w, vc) + b_s.T[None, None, :, :, None]
    return u * mixed.reshape(B, S, SGU_WIDTH)


def even_mixer(hn, positions, w_in, w_out, pool_w, pool_scale):
    B, S, _ = hn.shape
    z = hn @ w_in
    cuts = [int(c) for c in np.cumsum([POOL_WIDTH, DSA_WIDTH, HEAD_DIM, HEAD_DIM, IDX_HEADS * IDX_DIM, IDX_DIM])]
    z_pool, z_q, z_k, z_v, z_iq, z_ik, z_iw = jnp.split(z, cuts, axis=-1)
    y_pool = pool_mixer(z_pool, pool_w, pool_scale)
    q = partial_rope(z_q.reshape(B, S, DSA_HEADS, HEAD_DIM), positions)
    k = partial_rope(z_k[:, :, None, :], positions)[:, :, 0]
    iq = partial_rope(z_iq.reshape(B, S, IDX_HEADS, IDX_DIM), positions)
    ik = partial_rope(z_ik[:, :, None, :], positions)[:, :, 0]
    y_dsa = dsa_mixer(q, k, z_v, iq, ik, z_iw)
    return jnp.concatenate([y_pool, y_dsa], axis=-1) @ w_out


def odd_mixer(hn, w_in, w_out, b_f, ln_g, ln_b, w_s, b_s):
    B, S, _ = hn.shape
    z = hn @ w_in
    cuts = [FOX_WIDTH, 2 * FOX_WIDTH, 3 * FOX_WIDTH, 3 * FOX_WIDTH + FOX_HEADS]
    z_q, z_k, z_v, z_f, z_sgu = jnp.split(z, cuts, axis=-1)
    shp = (B, S, FOX_HEADS, HEAD_DIM)
    y_fox = fox_mixer(z_q.reshape(shp), z_k.reshape(shp), z_v.reshape(shp), z_f, b_f)
    y_sgu = sgu_mixer(z_sgu, ln_g, ln_b, w_s, b_s)
    return jnp.concatenate([y_fox, y_sgu], axis=-1) @ w_out


def conv_ffn(hn, w_in, conv_w, conv_b, w_out):
    S = hn.shape[1]
    a = hn @ w_in
    ap = jnp.pad(a, ((0, 0), (CONV_W - 1, 0), (0, 0)))
    c = sum(ap[:, i:i + S] * conv_w[i] for i in range(CONV_W)) + conv_b
    g, u = jnp.split(c, 2, axis=-1)
    return (jax.nn.gelu(g) * u) @ w_out


def setup_inputs(seed: int = 0) -> dict:
    key = jax.random.key(seed)
    ks = jax.random.split(key, 24)

    def nrm(i, shape, scale):
        return jax.random.normal(ks[i], shape, jnp.float32) * scale

    x = nrm(0, (BATCH, SEQ, D_MODEL), 1.0)
    p = nrm(1, (DEPTH, BATCH, SEQ, PLE_DIM), 1.0)
    offsets = jax.random.randint(ks[2], (BATCH, 1), 0, 16) * CHUNK
    positions = (offsets + jnp.arange(SEQ, dtype=jnp.int32)[None, :]).astype(jnp.int32)
    return {
        'x': x,
        'p': p,
        'positions': positions,
        'norm_mix': 1.0 + nrm(3, (DEPTH, D_MODEL), 0.1),
        'norm_ffn': 1.0 + nrm(4, (DEPTH, D_MODEL), 0.1),
        'norm_ple': 1.0 + nrm(5, (DEPTH, D_MODEL), 0.1),
        'ffn_w_in': nrm(6, (DEPTH, D_MODEL, 2 * D_FF), D_MODEL ** -0.5),
        'ffn_conv_w': nrm(7, (DEPTH, CONV_W, 2 * D_FF), CONV_W ** -0.5),
        'ffn_conv_b': nrm(8, (DEPTH, 2 * D_FF), 0.01),
        'ffn_w_out': nrm(9, (DEPTH, D_FF, D_MODEL), D_FF ** -0.5),
        'ple_w_proj': nrm(10, (DEPTH, PLE_DIM, D_MODEL), PLE_DIM ** -0.5),
        'ple_w_gate': nrm(11, (DEPTH, D_MODEL, D_MODEL), D_MODEL ** -0.5),
        'norm_final': 1.0 + nrm(12, (D_MODEL,), 0.1),
        'ev_w_in': nrm(13, (N_EVEN, D_MODEL, EVEN_IN), D_MODEL ** -0.5),
        'ev_w_out': nrm(14, (N_EVEN, MIX_WIDTH, D_MODEL), MIX_WIDTH ** -0.5),
        'pool_w': nrm(15, (N_EVEN, POOL_GROUPS, POOL_GROUP_DIM, POOL_GROUP_DIM), POOL_GROUP_DIM ** -0.5),
        'pool_scale': 1.0 + nrm(16, (N_EVEN, POOL_WIDTH), 0.1),
        'od_w_in': nrm(17, (N_ODD, D_MODEL, ODD_IN), D_MODEL ** -0.5),
        'od_w_out': nrm(18, (N_ODD, MIX_WIDTH, D_MODEL), MIX_WIDTH ** -0.5),
        'fox_b_f': 3.0 + nrm(19, (N_ODD, FOX_HEADS), 0.5),
        'sgu_ln_g': 1.0 + nrm(20, (N_ODD, SGU_WIDTH), 0.1),
        'sgu_ln_b': nrm(21, (N_ODD, SGU_WIDTH), 0.01),
        'sgu_w': nrm(22, (N_ODD, SGU_GROUPS, SGU_CHUNK, SGU_CHUNK), CHUNK ** -0.5),
        'sgu_b': 1.0 + nrm(23, (N_ODD, SGU_GROUPS, SGU_CHUNK), 0.1),
    }


def reference(x, p, positions, norm_mix, norm_ffn, norm_ple, ffn_w_in, ffn_conv_w, ffn_conv_b,
              ffn_w_out, ple_w_proj, ple_w_gate, norm_final, ev_w_in, ev_w_out, pool_w, pool_scale,
              od_w_in, od_w_out, fox_b_f, sgu_ln_g, sgu_ln_b, sgu_w, sgu_b):
    h = x
    for i in range(DEPTH):
        j = i // 2
        hn = rmsnorm(h, norm_mix[i])
        if i % 2 == 0:
            h = h + even_mixer(hn, positions, ev_w_in[j], ev_w_out[j], pool_w[j], pool_scale[j])
        else:
            h = h + odd_mixer(hn, od_w_in[j], od_w_out[j], fox_b_f[j], sgu_ln_g[j], sgu_ln_b[j],
                              sgu_w[j], sgu_b[j])
        h = h + conv_ffn(rmsnorm(h, norm_ffn[i]), ffn_w_in[i], ffn_conv_w[i], ffn_conv_b[i], ffn_w_out[i])
        gate = jax.nn.sigmoid(rmsnorm(h, norm_ple[i]) @ ple_w_gate[i])
        h = h + (p[i] @ ple_w_proj[i]) * gate
    return rmsnorm(h, norm_final)
```

```python
import numpy as np
from contextlib import ExitStack
import concourse.bass as bass
import concourse.mybir as mybir
from concourse.bass_utils import run_bass_kernel_spmd

F32 = mybir.dt.float32
BF16 = mybir.dt.bfloat16
I32 = mybir.dt.int32
AF = mybir.ActivationFunctionType
ALU = mybir.AluOpType

S = 2048
D = 1024
NT = S // 128
NC = D // 128
EPS = 1e-6
STAGES = {"mix0", "ffn0", "ple0", "mix1", "ffn1", "ple1", "noacttopk"}
DEBUG_OUT = None


class T:
    def __init__(self, name):
        self.name = name
        self.last_w = None
        self.readers = {}
        self.dsem = None


class KB:
    def __init__(self, nc, es):
        self.nc = nc
        self.es = es
        self.E = {"pe": nc.tensor, "act": nc.scalar, "dve": nc.vector, "pool": nc.gpsimd, "sp": nc.sync}
        self.sem = {}
        for e in ["pe", "act", "dve", "pool"]:
            self.sem[e] = es.enter_context(nc.semaphore("s_" + e))
        self.cnt = {e: 0 for e in self.sem}
        self.waited = {e: {} for e in self.E}
        self.semobj = {}
        self.dpool = []
        for i in range(90):
            s = es.enter_context(nc.semaphore("d%d" % i))
            self.dpool.append(s)
            self.semobj[s.num] = s
        for e in self.sem:
            self.semobj[self.sem[e].num] = self.sem[e]
        self.dcount = {s.num: 0 for s in self.dpool}
        self.dfree = list(self.dpool)
        self.phase_dsems = []

    def t(self, name, dma=False):
        o = T(name)
        if dma:
            o.dsem = self.dfree.pop()
            self.phase_dsems.append(o.dsem)
        return o

    def release_phase_dsems(self):
        for s in self.phase_dsems:
            self.dfree.append(s)
        self.phase_dsems = []

    def _wait(self, eng, ev):
        semnum, val = ev
        w = self.waited[eng]
        if w.get(semnum, 0) >= val:
            return
        self.E[eng].wait_ge(self.semobj[semnum], val)
        w[semnum] = val

    def _deps(self, eng, reads, writes):
        mysem = self.sem[eng].num if eng in self.sem else None
        for t in reads:
            if t.last_w is not None:
                if t.last_w[0] == mysem and eng == "pe":
                    continue
                self._wait(eng, t.last_w)
        for t in writes:
            if t.last_w is not None and t.last_w[0] != mysem:
                self._wait(eng, t.last_w)
            for sn, v in t.readers.items():
                if sn != mysem:
                    self._wait(eng, (sn, v))

    def _record(self, ev, reads, writes):
        for t in reads:
            if t.readers.get(ev[0], 0) < ev[1]:
                t.readers[ev[0]] = ev[1]
        for t in writes:
            t.last_w = ev
            t.readers = {}

    def op(self, eng, fn, reads=(), writes=()):
        self._deps(eng, reads, writes)
        ins = fn()
        self.cnt[eng] += 1
        ins.then_inc(self.sem[eng], 1)
        ev = (self.sem[eng].num, self.cnt[eng])
        self._record(ev, reads, writes)
        return ev

    def group(self, eng, fns, reads=(), writes=()):
        self._deps(eng, reads, writes)
        ins = None
        for fn in fns:
            ins = fn()
        self.cnt[eng] += 1
        ins.then_inc(self.sem[eng], 1)
        ev = (self.sem[eng].num, self.cnt[eng])
        self._record(ev, reads, writes)
        return ev

    def dma(self, q, out, in_, obj, reads=(), writes=()):
        self._deps(q, reads, writes)
        self.dcount[obj.dsem.num] += 1
        self.E[q].dma_start(out=out, in_=in_).then_inc(obj.dsem, 16)
        ev = (obj.dsem.num, 16 * self.dcount[obj.dsem.num])
        self._record(ev, reads, writes)
        return ev

    def barrier(self):
        evs = [(self.sem[e].num, self.cnt[e]) for e in self.sem if self.cnt[e] > 0]
        for s in self.dpool:
            if self.dcount[s.num] > 0:
                evs.append((s.num, 16 * self.dcount[s.num]))
        for e in self.E:
            for ev in evs:
                if e in self.sem and ev[0] == self.sem[e].num:
                    continue
                self._wait(e, ev)


def build_program():
    nc = bass.Bass("TRN2", target_bir_lowering=False)
    dr = {}

    def din(name, shape, dt=F32):
        dr[name] = nc.dram_tensor(name, list(shape), dt, kind="ExternalInput").ap()
        return dr[name]

    x_d = din("x", [S, D])
    p_d = din("p", [2, S, 256])
    pos_d = din("pos", [1, S], I32)
    gains_d = din("gains", [128, 7, 8])
    ident_d = din("ident", [128, 128])
    cwb_d = din("cwb", [2, 128, 44, 4])
    win_d = din("win", [2, 22, 128, 2 * NC * 128])
    wout_d = din("wout", [2, 2, NC, 128, 11 * 128])
    wgate_d = din("wgate", [2, NC, 128, NC * 128])
    wproj_d = din("wproj", [2, 128, NC * 2 * 128])
    odw_d = din("odw", [5, 128, NC * 512])
    odw_d2 = din("odw2", [128, NC * 640])
    odwo_d = din("odwo", [128, NC * D])
    odqk_d = din("odqk", [8, 128, NC * 128])
    lng_d = din("lng", [128, 512])
    lnb_d = din("lnb", [128, 512])
    sguw_d = din("sguw", [128, 4, 128])
    sgumask_d = din("sgumask", [128, 128])
    sgub_d = din("sgub", [128, 4, 512])
    foxbf_d = din("foxbf", [8, 1])
    cmask_d = din("cmask", [128, 128])
    evw_d = din("evw", [25, 128, NC * 128])
    evwo_d = din("evwo", [128, NC * D])
    poolw_d = din("poolw", [128, 4, 128])
    poolc_d = din("poolc", [128, 68])
    posr_d = din("posr", [128, S], I32)
    ropec_d = din("ropec", [128, 2])
    pert_d = din("pert", [128, S])
    out_d = nc.dram_tensor("out", [S, D], F32, kind="ExternalOutput").ap()

    with ExitStack() as es:
        kb = KB(nc, es)
        E = kb.E

        def sb(name, shape, dt, stack=es):
            return stack.enter_context(nc.sbuf_tensor("sb_" + name, list(shape), dt))

        hT = sb("hT", [128, NC, S], F32)
        hT_t = [kb.t("hT%d" % c) for c in range(NC)]
        gains = sb("gains", [128, 7, 8], F32)
        gains_t = kb.t("gains", dma=True)
        ident_f = sb("ident_f", [128, 128], F32)
        ident_f_t = kb.t("ident_f", dma=True)
        ident_b = sb("ident_b", [128, 128], BF16)
        ident_b_t = kb.t("ident_b", dma=True)
        ones_b = sb("ones_b", [128, 128], BF16)
        ones_b_t = kb.t("ones_b")
        one_c = sb("one_c", [128, 1], F32)
        onec_t = kb.t("onec")
        zeros_b = sb("zeros_b", [128, 512], BF16)
        zeros_t = kb.t("zeros")
        eps_c = sb("eps_c", [128, 1], F32)
        eps_t = kb.t("eps")
        banks = [es.enter_context(nc.psum_tensor("bank%d" % i, [128, 512], F32)) for i in range(8)]
        bank_t = [kb.t("bank%d" % i) for i in range(8)]

        kb.dma("sp", gains[:], gains_d[:, :, :], gains_t, writes=[gains_t])
        kb.dma("sp", ident_f[:], ident_d[:, :], ident_f_t, writes=[ident_f_t])
        kb.dma("pool", ident_b[:], ident_d[:, :], ident_b_t, writes=[ident_b_t])
        kb.op("dve", lambda: nc.vector.memset(ones_b[:], 1.0), writes=[ones_b_t])
        kb.op("dve", lambda: nc.vector.memset(eps_c[:], EPS), writes=[eps_t])
        kb.op("dve", lambda: nc.vector.memset(zeros_b[:], 0.0), writes=[zeros_t])
        kb.op("dve", lambda: nc.vector.memset(one_c[:], 1.0), writes=[onec_t])
        zero_c = sb("zero_c", [128, 1], F32)
        zeroc_t = kb.t("zeroc")
        tauc = sb("tauc", [128, 1], F32)
        tauc_t = kb.t("tauc")
        kb.op("dve", lambda: nc.vector.memset(zero_c[:], 0.0), writes=[zeroc_t])
        kb.op("dve", lambda: nc.vector.memset(tauc[:], 32.0 / (2 ** 29)), writes=[tauc_t])

        rr = {"i": 0}

        def alt(engs):
            rr["i"] += 1
            return engs[rr["i"] % len(engs)]

        with ExitStack() as ph:
            xs = [sb("xs%d" % i, [128, D], F32, ph) for i in range(2)]
            xs_t = [kb.t("xs%d" % i, dma=True) for i in range(2)]
            for tt in range(NT):
                b = tt % 2
                kb.dma("sp", xs[b][:], x_d[tt * 128:(tt + 1) * 128, :], xs_t[b], writes=[xs_t[b]])
                for half in range(2):
                    bi = (2 * tt + half) % 4
                    bk = banks[bi]
                    fns = []
                    for k in range(4):
                        c = half * 4 + k
                        fns.append(lambda k=k, c=c, bk=bk, b=b: nc.tensor.transpose(
                            out=bk[:, k * 128:(k + 1) * 128], in_=xs[b][:, c * 128:(c + 1) * 128], identity=ident_f[:]))
                    kb.group("pe", fns, reads=[xs_t[b], ident_f_t], writes=[bank_t[bi]])
                    eng = alt(["dve", "act"])
                    dst = hT[:, half * 4:(half + 1) * 4, tt * 128:(tt + 1) * 128]
                    src = bk[:, :].rearrange("p (k n) -> p k n", k=4)
                    if eng == "dve":
                        kb.op("dve", lambda dst=dst, src=src: nc.vector.tensor_copy(out=dst, in_=src),
                              reads=[bank_t[bi]], writes=hT_t[half * 4:(half + 1) * 4])
                    else:
                        kb.op("act", lambda dst=dst, src=src: nc.scalar.copy(out=dst, in_=src),
                              reads=[bank_t[bi]], writes=hT_t[half * 4:(half + 1) * 4])
            kb.barrier()
            kb.release_phase_dsems()

        def rmsnorm(ph_unused, gidx, outT, outT_t):
            with ExitStack() as ph:
                sqb = [sb("sqb%d_%d" % (gidx, i), [128, S], BF16, ph) for i in range(2)]
                sqb_t = [kb.t("sqb%d" % i) for i in range(2)]
                rstd = sb("rstd%d" % gidx, [128, S], F32, ph)
                rstd_t = [kb.t("rstd%d" % i) for i in range(4)]
                for c in range(NC):
                    b = c % 2
                    kb.op("act", lambda c=c, b=b: nc.scalar.activation(out=sqb[b][:], in_=hT[:, c, :], func=AF.Square),
                          reads=[hT_t[c]], writes=[sqb_t[b]])
                    fns = []
                    for tc in range(4):
                        fns.append(lambda tc=tc, c=c, b=b: nc.tensor.matmul(
                            banks[tc][:, :], lhsT=ones_b[:], rhs=sqb[b][:, tc * 512:(tc + 1) * 512],
                            start=(c == 0), stop=(c == NC - 1)))
                    kb.group("pe", fns, reads=[sqb_t[b], ones_b_t], writes=bank_t[0:4])
                for tc in range(4):
                    sl = slice(tc * 512, (tc + 1) * 512)
                    kb.op("act", lambda tc=tc, sl=sl: nc.scalar.activation(
                        out=rstd[:, sl], in_=banks[tc][:, :], func=AF.Sqrt, scale=1.0 / D, bias=eps_c[:]),
                        reads=[bank_t[tc], eps_t], writes=[rstd_t[tc]])
                    kb.op("dve", lambda sl=sl: nc.vector.reciprocal(out=rstd[:, sl], in_=rstd[:, sl]),
                          reads=[rstd_t[tc]], writes=[rstd_t[tc]])
                for c in range(NC):
                    kb.op("dve", lambda c=c: nc.vector.scalar_tensor_tensor(
                        out=outT[:, c, :], in0=hT[:, c, :], scalar=gains[:, gidx, c:c + 1], in1=rstd[:, :],
                        op0=ALU.mult, op1=ALU.mult),
                        reads=[hT_t[c], gains_t] + rstd_t, writes=[outT_t[c]])
                kb.barrier()

        def ffn_phase(l):
            with ExitStack() as ph:
                hn = sb("hn_f%d" % l, [128, NC, S], BF16, ph)
                hn_t = [kb.t("hn%d" % c) for c in range(NC)]
                cwb = sb("cwb%d" % l, [128, 44, 4], F32, ph)
                cwb_t = kb.t("cwb", dma=True)
                kb.dma("sp", cwb[:], cwb_d[l], cwb_t, writes=[cwb_t])
                win = [sb("win%d_%d" % (l, i), [128, 2, NC, 128], BF16, ph) for i in range(3)]
                win_t = [kb.t("win%d" % i, dma=True) for i in range(3)]
                wob = [sb("wob%d_%d" % (l, i), [128, 11, 128], BF16, ph) for i in range(2)]
                wob_t = [kb.t("wob%d" % i, dma=True) for i in range(2)]
                act = sb("act%d" % l, [128, 11, S], BF16, ph)
                act_t = [kb.t("act%d" % i) for i in range(11)]

                def load_win(j):
                    b = j % 3
                    kb.dma("pool", win[b][:].rearrange("p s c n -> p (s c n)"), win_d[l, j], win_t[b], writes=[win_t[b]])

                load_win(0)
                load_win(1)
                rmsnorm(ph, 3 * l + 1, hn, hn_t)
                A = [[sb("A%d_%d_%d" % (l, st, s_), [128, 1026], F32, ph) for s_ in range(2)] for st in range(2)]
                A_t = [[kb.t("A%d%d" % (st, s_)) for s_ in range(2)] for st in range(2)]
                C = [[sb("C%d_%d_%d" % (l, st, s_), [128, 1024], F32, ph) for s_ in range(2)] for st in range(2)]
                C_t = [[kb.t("C%d%d" % (st, s_)) for s_ in range(2)] for st in range(2)]
                step = 0
                for grp in range(2):
                    for jj in range(11):
                        j = grp * 11 + jj
                        if j + 2 < 22:
                            load_win(j + 2)
                        wb = win[j % 3]
                        wt = win_t[j % 3]
                        for half in range(2):
                            st = step % 2
                            step += 1
                            for s_ in range(2):
                                ft = s_ * 22 + j
                                for tcl in range(2):
                                    bi = 4 * st + 2 * s_ + tcl
                                    t0 = half * 1024 + tcl * 512
                                    fns = [(lambda c=c, bi=bi, s_=s_, t0=t0, wb=wb: nc.tensor.matmul(
                                        banks[bi][:, :], lhsT=wb[:, s_, c, :], rhs=hn[:, c, t0:t0 + 512],
                                        start=(c == 0), stop=(c == NC - 1))) for c in range(NC)]
                                    kb.group("pe", fns, reads=[wt] + hn_t, writes=[bank_t[bi]])
                            for s_ in range(2):
                                ft = s_ * 22 + j
                                a = A[st][s_]
                                at = A_t[st][s_]
                                cc = C[st][s_]
                                ct = C_t[st][s_]
                                if half == 0:
                                    kb.op("pool", lambda a=a: nc.gpsimd.memset(a[:, 0:2], 0.0), writes=[at])
                                else:
                                    ap_ = A[1 - st][s_]
                                    kb.op("pool", lambda a=a, ap_=ap_: nc.gpsimd.tensor_copy(out=a[:, 0:2], in_=ap_[:, 1024:1026]),
                                          reads=[A_t[1 - st][s_]], writes=[at])
                                for tcl in range(2):
                                    bi = 4 * st + 2 * s_ + tcl
                                    kb.op("act", lambda a=a, bi=bi, tcl=tcl: nc.scalar.copy(
                                        out=a[:, 2 + tcl * 512:2 + (tcl + 1) * 512], in_=banks[bi][:, :]),
                                        reads=[bank_t[bi]], writes=[at])
                                    kb.op("act", lambda cc=cc, bi=bi, tcl=tcl, ft=ft: nc.scalar.activation(
                                        out=cc[:, tcl * 512:(tcl + 1) * 512], in_=banks[bi][:, :], func=AF.Identity,
                                        scale=cwb[:, ft, 2:3], bias=cwb[:, ft, 3:4]),
                                        reads=[bank_t[bi], cwb_t], writes=[ct])
                                kb.op("dve", lambda a=a, cc=cc, ft=ft: nc.vector.scalar_tensor_tensor(
                                    out=cc[:, :], in0=a[:, 1:1025], scalar=cwb[:, ft, 1:2], in1=cc[:, :],
                                    op0=ALU.mult, op1=ALU.add), reads=[at, ct, cwb_t], writes=[ct])
                                kb.op("dve", lambda a=a, cc=cc, ft=ft: nc.vector.scalar_tensor_tensor(
                                    out=cc[:, :], in0=a[:, 0:1024], scalar=cwb[:, ft, 0:1], in1=cc[:, :],
                                    op0=ALU.mult, op1=ALU.add), reads=[at, ct, cwb_t], writes=[ct])
                            cg = C[st][0]
                            cu = C[st][1]
                            kb.op("act", lambda cg=cg: nc.scalar.activation(out=cg[:, :], in_=cg[:, :], func=AF.Gelu_apprx_tanh),
                                  reads=[C_t[st][0]], writes=[C_t[st][0]])
                            kb.op("dve", lambda cg=cg, cu=cu, jj=jj, half=half: nc.vector.tensor_tensor(
                                out=act[:, jj, half * 1024:(half + 1) * 1024], in0=cg[:, :], in1=cu[:, :], op=ALU.mult),
                                reads=[C_t[st][0], C_t[st][1]], writes=[act_t[jj]])
                    for dt in range(NC):
                        b = dt % 2
                        kb.dma("pool", wob[b][:].rearrange("p j n -> p (j n)"), wout_d[l, grp, dt], wob_t[b], writes=[wob_t[b]])
                        for tc in range(4):
                            bi = (dt * 4 + tc) % 8
                            fns = [(lambda q=q, bi=bi, b=b, tc=tc, dt=dt: nc.tensor.matmul(
                                banks[bi][:, :], lhsT=wob[b][:, q, :], rhs=act[:, q, tc * 512:(tc + 1) * 512],
                                start=(q == 0), stop=(q == 10))) for q in range(11)]
                            kb.group("pe", fns, reads=[wob_t[b]] + act_t, writes=[bank_t[bi]])
                            kb.op("dve", lambda bi=bi, dt=dt, tc=tc: nc.vector.tensor_tensor(
                                out=hT[:, dt, tc * 512:(tc + 1) * 512], in0=hT[:, dt, tc * 512:(tc + 1) * 512],
                                in1=banks[bi][:, :], op=ALU.add), reads=[bank_t[bi], hT_t[dt]], writes=[hT_t[dt]])
                kb.barrier()
                kb.release_phase_dsems()

        def ple_phase(l):
            with ExitStack() as ph:
                hn = sb("hn_p%d" % l, [128, NC, S], BF16, ph)
                hn_t = [kb.t("hn%d" % c) for c in range(NC)]
                wg = sb("wg%d" % l, [128, NC, NC, 128], BF16, ph)
                wg_t = [kb.t("wg%d" % i, dma=True) for i in range(NC)]
                wp = sb("wp%d" % l, [128, NC, 2, 128], BF16, ph)
                wp_t = kb.t("wp", dma=True)
                pb = sb("pb%d" % l, [128, NT, 256], BF16, ph)
                pb_t = kb.t("pb", dma=True)
                pT = sb("pT%d" % l, [128, 2, S], BF16, ph)
                pT_t = kb.t("pT")
                gs = [sb("gs%d_%d" % (l, i), [128, 512], F32, ph) for i in range(2)]
                gs_t = [kb.t("gs%d" % i) for i in range(2)]
                kb.dma("pool", pb[:], p_d[l].rearrange("(tt p) f -> p tt f", p=128), pb_t, writes=[pb_t])
                kb.dma("pool", wp[:].rearrange("p d c n -> p (d c n)"), wproj_d[l], wp_t, writes=[wp_t])
                for dt in range(NC):
                    kb.dma("pool", wg[:, dt].rearrange("p c n -> p (c n)"), wgate_d[l, dt], wg_t[dt], writes=[wg_t[dt]])
                rmsnorm(ph, 3 * l + 2, hn, hn_t)
                for tt in range(NT):
                    bi = 4 + tt % 4
                    bkb = banks[bi][:, :].bitcast(BF16)
                    fns = [(lambda c2=c2, bkb=bkb, tt=tt: nc.tensor.transpose(
                        out=bkb[:, c2 * 128:(c2 + 1) * 128], in_=pb[:, tt, c2 * 128:(c2 + 1) * 128], identity=ident_b[:]))
                        for c2 in range(2)]
                    kb.group("pe", fns, reads=[pb_t, ident_b_t], writes=[bank_t[bi]])
                    eng = alt(["dve", "act"])
                    dst = pT[:, :, tt * 128:(tt + 1) * 128]
                    src = bkb[:, 0:256].rearrange("p (k n) -> p k n", k=2)
                    if eng == "dve":
                        kb.op("dve", lambda dst=dst, src=src: nc.vector.tensor_copy(out=dst, in_=src),
                              reads=[bank_t[bi]], writes=[pT_t])
                    else:
                        kb.op("act", lambda dst=dst, src=src: nc.scalar.copy(out=dst, in_=src),
                              reads=[bank_t[bi]], writes=[pT_t])
                it = 0
                for dt in range(NC):
                    for tc in range(4):
                        b = it % 2
                        it += 1
                        bg = 2 * b
                        bp = 2 * b + 1
                        sl = slice(tc * 512, (tc + 1) * 512)
                        fns = [(lambda c=c, bg=bg, dt=dt, sl=sl: nc.tensor.matmul(
                            banks[bg][:, :], lhsT=wg[:, dt, c, :], rhs=hn[:, c, sl], start=(c == 0), stop=(c == NC - 1)))
                            for c in range(NC)]
                        kb.group("pe", fns, reads=[wg_t[dt]] + hn_t, writes=[bank_t[bg]])
                        fns = [(lambda c2=c2, bp=bp, dt=dt, sl=sl: nc.tensor.matmul(
                            banks[bp][:, :], lhsT=wp[:, dt, c2, :], rhs=pT[:, c2, sl], start=(c2 == 0), stop=(c2 == 1)))
                            for c2 in range(2)]
                        kb.group("pe", fns, reads=[wp_t, pT_t], writes=[bank_t[bp]])
                        kb.op("act", lambda b=b, bg=bg: nc.scalar.activation(out=gs[b][:, :], in_=banks[bg][:, :], func=AF.Sigmoid),
                              reads=[bank_t[bg]], writes=[gs_t[b]])
                        kb.op("dve", lambda b=b, bp=bp: nc.vector.tensor_tensor(
                            out=gs[b][:, :], in0=gs[b][:, :], in1=banks[bp][:, :], op=ALU.mult),
                            reads=[gs_t[b], bank_t[bp]], writes=[gs_t[b]])
                        kb.op("pool", lambda b=b, dt=dt, sl=sl: nc.gpsimd.tensor_tensor(
                            out=hT[:, dt, sl], in0=hT[:, dt, sl], in1=gs[b][:, :], op=ALU.add),
                            reads=[gs_t[b], hT_t[dt]], writes=[hT_t[dt]])
                kb.barrier()
                kb.release_phase_dsems()

        def attn_group(h, qc, ph_bufs, k_lhsT, q_rhs, v_rhs, mask_fn, y_tm, y_tm_t, kin_t, vin_t):
            PT, PT_t, rden, rden_t, st = ph_bufs
            bo = 4 + (st["g"] % 2)
            st["g"] += 1
            obank = banks[bo]
            kb.group("pe", [lambda: nc.tensor.matmul(obank[:, 0:260], lhsT=zeros_b[:, 0:128], rhs=zeros_b[:, 0:260],
                                                      start=True, stop=False, skip_group_check=True)],
                     reads=[zeros_t], writes=[bank_t[bo]])
            nk = 4 * qc + 4
            for kt in range(nk):
                r = max(0, kt - 4 * qc)
                w = 512 - 128 * r
                c0 = qc * 512 + r * 128
                bs = 6 + (st["s"] % 2)
                pb_ = st["s"] % 3
                st["s"] += 1
                sbank = banks[bs]
                mk = mask_fn(kt, qc, r)
                fns = [lambda kt=kt, c0=c0, w=w, sbank=sbank, mk=mk: nc.tensor.matmul(
                    sbank[:, 0:w], lhsT=k_lhsT(kt), rhs=q_rhs(c0, c0 + w), start=True, stop=(mk is None))]
                rds = list(kin_t)
                if mk is not None:
                    mrhs, mw, mt = mk
                    fns.append(lambda sbank=sbank, mrhs=mrhs, mw=mw: nc.tensor.matmul(
                        sbank[:, 0:mw], lhsT=ident_b[:], rhs=mrhs, start=False, stop=True))
                    rds += [mt, ident_b_t]
                kb.group("pe", fns, reads=rds, writes=[bank_t[bs]])
                kb.op("act", lambda pb_=pb_, w=w, sbank=sbank: nc.scalar.activation(
                    out=PT[pb_][:, 0:w], in_=sbank[:, 0:w], func=AF.Exp, scale=0.125),
                    reads=[bank_t[bs]], writes=[PT_t[pb_]])
                fns = []
                for qs in range(r, 4):
                    last = (kt == nk - 1 and qs == 3)
                    fns.append(lambda qs=qs, r=r, pb_=pb_, kt=kt, last=last: nc.tensor.matmul(
                        obank[:, qs * 65:qs * 65 + 65], lhsT=PT[pb_][:, (qs - r) * 128:(qs - r + 1) * 128],
                        rhs=v_rhs(kt), start=False, stop=last, skip_group_check=True))
                kb.group("pe", fns, reads=[PT_t[pb_]] + list(vin_t), writes=[bank_t[bo]])
            rb = st["g"] % 2
            ov = obank[:, 0:260].rearrange("p (q e) -> p q e", e=65)
            kb.op("dve", lambda rb=rb, ov=ov: nc.vector.reciprocal(out=rden[rb][:, 0:4], in_=ov[:, :, 64]),
                  reads=[bank_t[bo]], writes=[rden_t[rb]])
            for qs in range(4):
                tt = qc * 4 + qs
                kb.op("dve", lambda qs=qs, tt=tt, rb=rb: nc.vector.tensor_scalar(
                    out=y_tm[:, tt, h * 64:(h + 1) * 64], in0=obank[:, qs * 65:qs * 65 + 64],
                    scalar1=rden[rb][:, qs:qs + 1], scalar2=None, op0=ALU.mult),
                    reads=[rden_t[rb]], writes=[y_tm_t, bank_t[bo]])

        def tm_to_fm(y_tm, y_tm_t, ymT, ymT_t, c_base):
            for tt in range(NT):
                bi = tt % 4
                bkb = banks[bi][:, :].bitcast(BF16)
                fns = [(lambda k=k, bkb=bkb, tt=tt: nc.tensor.transpose(
                    out=bkb[:, k * 128:(k + 1) * 128], in_=y_tm[:, tt, k * 128:(k + 1) * 128], identity=ident_b[:]))
                    for k in range(4)]
                kb.group("pe", fns, reads=[y_tm_t, ident_b_t], writes=[bank_t[bi]])
                dst = ymT[:, c_base:c_base + 4, tt * 128:(tt + 1) * 128]
                src = bkb[:, 0:512].rearrange("p (k n) -> p k n", k=4)
                eng = alt(["dve", "act"])
                if eng == "dve":
                    kb.op("dve", lambda dst=dst, src=src: nc.vector.tensor_copy(out=dst, in_=src),
                          reads=[bank_t[bi]], writes=ymT_t[c_base:c_base + 4])
                else:
                    kb.op("act", lambda dst=dst, src=src: nc.scalar.copy(out=dst, in_=src),
                          reads=[bank_t[bi]], writes=ymT_t[c_base:c_base + 4])

        def mixer_out(ph, ymT, ymT_t, wo_dram, nm, ym_fn=None):
            wo = sb("wo_mix" + nm, [128, NC, D], BF16, ph)
            wo_t = kb.t("wo_mix", dma=True)
            kb.dma("pool", wo[:].rearrange("p c n -> p (c n)"), wo_dram, wo_t, writes=[wo_t])
            for dt in range(NC):
                for tc in range(4):
                    bi = (dt * 4 + tc) % 4
                    sl = slice(tc * 512, (tc + 1) * 512)
                    fns = [(lambda c=c, bi=bi, dt=dt, sl=sl: nc.tensor.matmul(
                        banks[bi][:, :], lhsT=wo[:, c, dt * 128:(dt + 1) * 128], rhs=(ym_fn(c, sl) if ym_fn else ymT[:, c, sl]),
                        start=(c == 0), stop=(c == NC - 1))) for c in range(NC)]
                    kb.group("pe", fns, reads=[wo_t] + ymT_t, writes=[bank_t[bi]])
                    kb.op("dve", lambda bi=bi, dt=dt, sl=sl: nc.vector.tensor_tensor(
                        out=hT[:, dt, sl], in0=hT[:, dt, sl], in1=banks[bi][:, :], op=ALU.add),
                        reads=[bank_t[bi], hT_t[dt]], writes=[hT_t[dt]])

        def odd_phase(l):
            with ExitStack() as ph:
                hn = sb("hn_o", [128, NC, S], BF16, ph)
                hn_t = [kb.t("hn%d" % c) for c in range(NC)]
                ymT = sb("ymT_o", [128, NC, S], BF16, ph)
                ymT_t = [kb.t("ymT%d" % c) for c in range(NC)]
                rmsnorm(ph, 3 * l, hn, hn_t)
                with ExitStack() as p2:
                    wu = sb("wu", [128, NC, 512], BF16, p2)
                    wu_t = kb.t("wu", dma=True)
                    wv = sb("wsv", [128, NC, 512], BF16, p2)
                    wv_t = kb.t("wsv", dma=True)
                    kb.dma("pool", wv[:].rearrange("p c n -> p (c n)"), odw_d[4], wv_t, writes=[wv_t])
                    kb.dma("pool", wu[:].rearrange("p c n -> p (c n)"), odw_d[3], wu_t, writes=[wu_t])
                    vn = sb("vn_tm", [128, NT, 512], BF16, p2)
                    vn_t = [kb.t("vn%d" % i) for i in range(NT)]
                    lng = sb("lng", [128, 512], F32, p2)
                    lnb = sb("lnb", [128, 512], F32, p2)
                    ln_t = kb.t("ln", dma=True)
                    kb.dma("sp", lng[:], lng_d[:, :], ln_t, writes=[ln_t])
                    kb.dma("sp", lnb[:], lnb_d[:, :], ln_t, writes=[ln_t])
                    swT = sb("swT", [128, 4, 128], F32, p2)
                    smk = sb("smk", [128, 128], F32, p2)
                    sw_t = kb.t("sw", dma=True)
                    kb.dma("sp", swT[:], sguw_d[:, :, :], sw_t, writes=[sw_t])
                    kb.dma("sp", smk[:], sgumask_d[:, :], sw_t, writes=[sw_t])
                    wm = sb("wm", [128, 4, 128], BF16, p2)
                    wm_t = kb.t("wm")
                    for g in range(4):
                        kb.op("dve", lambda g=g: nc.vector.tensor_tensor(out=wm[:, g, :], in0=swT[:, g, :], in1=smk[:, :], op=ALU.mult),
                              reads=[sw_t], writes=[wm_t])
                    brep = sb("brep", [128, 4, 512], F32, p2)
                    brep_t = kb.t("brep", dma=True)
                    kb.dma("sp", brep[:], sgub_d[:, :, :], brep_t, writes=[brep_t])
                    vg = [sb("vg%d" % i, [128, 512], F32, p2) for i in range(2)]
                    vg_t = [kb.t("vg%d" % i) for i in range(2)]
                    stt = [sb("stt%d" % i, [128, 8], F32, p2) for i in range(2)]
                    stt_t = [kb.t("stt%d" % i) for i in range(2)]
                    for tt in range(NT):
                        b = tt % 2
                        bi = tt % 4
                        fns = [(lambda c=c, bi=bi, tt=tt: nc.tensor.matmul(
                            banks[bi][:, :], lhsT=hn[:, c, tt * 128:(tt + 1) * 128], rhs=wv[:, c, :],
                            start=(c == 0), stop=(c == NC - 1))) for c in range(NC)]
                        kb.group("pe", fns, reads=[wv_t] + hn_t, writes=[bank_t[bi]])
                        kb.op("act", lambda b=b, bi=bi: nc.scalar.activation(out=vg[b][:, :], in_=banks[bi][:, :], func=AF.Gelu_apprx_tanh),
                              reads=[bank_t[bi]], writes=[vg_t[b]])
                        kb.op("dve", lambda b=b: nc.vector.bn_stats(out=stt[b][:, 0:6], in_=vg[b][:, :]),
                              reads=[vg_t[b]], writes=[stt_t[b]])
                        kb.op("dve", lambda b=b: nc.vector.bn_aggr(out=stt[b][:, 6:8], in_=stt[b][:, 0:6]),
                              reads=[stt_t[b]], writes=[stt_t[b]])
                        kb.op("act", lambda b=b: nc.scalar.activation(out=stt[b][:, 7:8], in_=stt[b][:, 7:8], func=AF.Sqrt, bias=eps_c[:]),
                              reads=[stt_t[b], eps_t], writes=[stt_t[b]])
                        kb.op("dve", lambda b=b: nc.vector.reciprocal(out=stt[b][:, 7:8], in_=stt[b][:, 7:8]),
                              reads=[stt_t[b]], writes=[stt_t[b]])
                        kb.op("dve", lambda b=b: nc.vector.tensor_scalar(
                            out=vg[b][:, :], in0=vg[b][:, :], scalar1=stt[b][:, 6:7], scalar2=stt[b][:, 7:8],
                            op0=ALU.subtract, op1=ALU.mult), reads=[vg_t[b], stt_t[b]], writes=[vg_t[b]])
                        kb.op("pool", lambda b=b: nc.gpsimd.tensor_tensor(out=vg[b][:, :], in0=vg[b][:, :], in1=lng[:, :], op=ALU.mult),
                              reads=[vg_t[b], ln_t], writes=[vg_t[b]])
                        kb.op("pool", lambda b=b, tt=tt: nc.gpsimd.tensor_tensor(out=vn[:, tt, :], in0=vg[b][:, :], in1=lnb[:, :], op=ALU.add),
                              reads=[vg_t[b], ln_t], writes=[vn_t[tt]])
                    us = [sb("us%d" % i, [128, 512], F32, p2) for i in range(2)]
                    us_t = [kb.t("us%d" % i) for i in range(2)]
                    ms = [sb("ms%d" % i, [128, 512], F32, p2) for i in range(2)]
                    ms_t = [kb.t("ms%d" % i) for i in range(2)]
                    it = 0
                    for g in range(4):
                        for tc in range(4):
                            b = it % 2
                            it += 1
                            bu = 4 + 2 * b
                            bm = 5 + 2 * b
                            sl = slice(tc * 512, (tc + 1) * 512)
                            fns = [(lambda c=c, bu=bu, g=g, sl=sl: nc.tensor.matmul(
                                banks[bu][:, :], lhsT=wu[:, c, g * 128:(g + 1) * 128], rhs=hn[:, c, sl],
                                start=(c == 0), stop=(c == NC - 1))) for c in range(NC)]
                            kb.group("pe", fns, reads=[wu_t] + hn_t, writes=[bank_t[bu]])
                            fns = [(lambda q=q, bm=bm, g=g, tc=tc: nc.tensor.matmul(
                                banks[bm][:, q * 128:(q + 1) * 128], lhsT=vn[:, tc * 4 + q, g * 128:(g + 1) * 128],
                                rhs=wm[:, g, :], start=True, stop=True)) for q in range(4)]
                            kb.group("pe", fns, reads=[wm_t] + vn_t[tc * 4:tc * 4 + 4], writes=[bank_t[bm]])
                            kb.op("act", lambda b=b, bu=bu: nc.scalar.activation(out=us[b][:, :], in_=banks[bu][:, :], func=AF.Gelu_apprx_tanh),
                                  reads=[bank_t[bu]], writes=[us_t[b]])
                            kb.op("dve", lambda b=b, bm=bm, g=g: nc.vector.tensor_tensor(
                                out=ms[b][:, :], in0=brep[:, g, :], in1=banks[bm][:, :], op=ALU.add),
                                reads=[bank_t[bm], brep_t], writes=[ms_t[b]])
                            kb.op("dve", lambda b=b, g=g, sl=sl: nc.vector.tensor_tensor(
                                out=ymT[:, 4 + g, sl], in0=ms[b][:, :], in1=us[b][:, :], op=ALU.mult),
                                reads=[ms_t[b], us_t[b]], writes=[ymT_t[4 + g]])
                    kb.barrier()
                    kb.release_phase_dsems()
                y_tm = sb("yfox_tm", [128, NT, 512], BF16, ph)
                y_tm_t = kb.t("yfox_tm")
                vaug = sb("vaug", [128, NT, 8, 65], BF16, ph)
                vaug_t = kb.t("vaug")
                Pp = [sb("Pp%d" % i, [8, S], BF16, ph) for i in range(3)]
                Pp_t = kb.t("Pp")
                with ExitStack() as p2:
                    wvf = sb("wvf", [128, NC, 640], BF16, p2)
                    wvf_t = kb.t("wvf", dma=True)
                    kb.dma("pool", wvf[:].rearrange("p c n -> p (c n)"), odw_d2[:, :], wvf_t, writes=[wvf_t])
                    bfc = sb("bfc", [8, 2], F32, p2)
                    bfc_t = kb.t("bfc", dma=True)
                    kb.dma("sp", bfc[:, 0:1], foxbf_d[:, :], bfc_t, writes=[bfc_t])
                    kb.op("dve", lambda: nc.vector.tensor_scalar(out=bfc[:, 1:2], in0=bfc[:, 0:1], scalar1=-1.0, scalar2=None, op0=ALU.mult),
                          reads=[bfc_t], writes=[bfc_t])
                    L8 = sb("L8", [8, S], F32, p2)
                    one8 = sb("one8", [8, S], BF16, p2)
                    one8_t = kb.t("one8")
                    kb.op("pool", lambda: nc.gpsimd.memset(one8[:, :], 1.0), writes=[one8_t])
                    kb.op("pool", lambda: nc.gpsimd.memset(vaug[:].rearrange("p a b c -> p (a b c)"), 1.0), writes=[vaug_t])
                    for tt in range(NT):
                        bi = tt % 4
                        fns = [(lambda c=c, bi=bi, tt=tt: nc.tensor.matmul(
                            banks[bi][:, :], lhsT=hn[:, c, tt * 128:(tt + 1) * 128], rhs=wvf[:, c, 0:512],
                            start=(c == 0), stop=(c == NC - 1))) for c in range(NC)]
                        kb.group("pe", fns, reads=[wvf_t] + hn_t, writes=[bank_t[bi]])
                        src = banks[bi][:, :].rearrange("p (h e) -> p h e", e=64)
                        eng = alt(["dve", "act"])
                        if eng == "dve":
                            kb.op("dve", lambda tt=tt, src=src: nc.vector.tensor_copy(out=vaug[:, tt, :, 0:64], in_=src),
                                  reads=[bank_t[bi]], writes=[vaug_t])
                        else:
                            kb.op("act", lambda tt=tt, src=src: nc.scalar.copy(out=vaug[:, tt, :, 0:64], in_=src),
                                  reads=[bank_t[bi]], writes=[vaug_t])
                    lf = sb("lf", [8, S], F32, p2)
                    lf_t = kb.t("lf")
                    for tc in range(4):
                        bi = 4 + tc
                        sl = slice(tc * 512, (tc + 1) * 512)
                        fns = [(lambda c=c, bi=bi, sl=sl: nc.tensor.matmul(
                            banks[bi][:, :], lhsT=wvf[:, c, 512:640], rhs=hn[:, c, sl],
                            start=(c == 0), stop=(c == NC - 1))) for c in range(NC)]
                        kb.group("pe", fns, reads=[wvf_t] + hn_t, writes=[bank_t[bi]])
                        kb.op("act", lambda bi=bi, sl=sl: nc.scalar.activation(
                            out=lf[:, sl], in_=banks[bi][0:8, :], func=AF.Exp, scale=-1.0, bias=bfc[:, 1:2]),
                            reads=[bank_t[bi], bfc_t], writes=[lf_t])
                    kb.op("act", lambda: nc.scalar.activation(out=lf[:, :], in_=lf[:, :], func=AF.Ln, bias=one_c[0:8, :]),
                          reads=[lf_t, onec_t], writes=[lf_t])
                    kb.op("dve", lambda: nc.vector.tensor_tensor_scan(out=L8[:, :], data0=one8[:, :], data1=lf[:, :], initial=0.0,
                                                                        op0=ALU.mult, op1=ALU.add),
                          reads=[lf_t, one8_t], writes=[Pp_t])
                    kb.op("dve", lambda: nc.vector.tensor_scalar(out=L8[:, :], in0=L8[:, :], scalar1=8.0, scalar2=None, op0=ALU.mult),
                          reads=[Pp_t], writes=[Pp_t])
                    for i in range(3):
                        kb.op("dve", lambda i=i: nc.vector.tensor_copy(out=Pp[i][:, :], in_=L8[:, :]), reads=[Pp_t], writes=[Pp_t])
                        if i < 2:
                            kb.op("dve", lambda i=i: nc.vector.tensor_tensor(out=L8[:, :], in0=L8[:, :], in1=Pp[i][:, :], op=ALU.subtract),
                                  reads=[Pp_t], writes=[Pp_t])
                    kb.barrier()
                    kb.release_phase_dsems()
                with ExitStack() as p2:
                    wqk = [sb("wqk%d" % i, [128, NC, 128], BF16, p2) for i in range(2)]
                    wqk_t = [kb.t("wqk%d" % i, dma=True) for i in range(2)]
                    qa = [sb("qa%d" % i, [128, S], BF16, p2) for i in range(2)]
                    ka = [sb("ka%d" % i, [128, S], BF16, p2) for i in range(2)]
                    qa_t = [kb.t("qa%d" % i, dma=True) for i in range(2)]
                    ka_t = [kb.t("ka%d" % i, dma=True) for i in range(2)]
                    PT = [sb("PT%d" % i, [128, 512], BF16, p2) for i in range(3)]
                    PT_t = [kb.t("PT%d" % i) for i in range(3)]
                    rden = [sb("rden%d" % i, [128, 4], F32, p2) for i in range(2)]
                    rden_t = [kb.t("rden%d" % i) for i in range(2)]
                    cmask = sb("cmask", [128, 128], BF16, p2)
                    cmask_t = kb.t("cmask", dma=True)
                    kb.dma("pool", cmask[:], cmask_d[:, :], cmask_t, writes=[cmask_t])
                    st = {"g": 0, "s": 0}
                    bufs = (PT, PT_t, rden, rden_t, st)
                    for h in (range(8) if "nofox" not in STAGES else []):
                        hb = h % 2
                        kb.dma("pool", wqk[hb][:].rearrange("p c n -> p (c n)"), odqk_d[h], wqk_t[hb], writes=[wqk_t[hb]])
                        for tc in range(4):
                            bi = tc % 4
                            sl = slice(tc * 512, (tc + 1) * 512)
                            fns = [(lambda c=c, bi=bi, sl=sl: nc.tensor.matmul(
                                banks[bi][:, :], lhsT=wqk[hb][:, c, :], rhs=hn[:, c, sl],
                                start=(c == 0), stop=(c == NC - 1))) for c in range(NC)]
                            kb.group("pe", fns, reads=[wqk_t[hb]] + hn_t, writes=[bank_t[bi]])
                            kb.op("dve", lambda bi=bi, sl=sl: nc.vector.tensor_copy(out=qa[hb][0:64, sl], in_=banks[bi][0:64, :]),
                                  reads=[], writes=[qa_t[hb], bank_t[bi]])
                            kb.op("act", lambda bi=bi, sl=sl: nc.scalar.copy(out=ka[hb][0:64, sl], in_=banks[bi][64:128, :]),
                                  reads=[], writes=[ka_t[hb], bank_t[bi]])
                        kb.op("pool", lambda: nc.gpsimd.memset(qa[hb][64:128, :], 0.0), writes=[qa_t[hb]])
                        kb.op("pool", lambda: nc.gpsimd.memset(ka[hb][64:128, :], 0.0), writes=[ka_t[hb]])
                        kb.op("pool", lambda: nc.gpsimd.memset(qa[hb][64:70, :], 1.0), writes=[qa_t[hb]])
                        kb.op("pool", lambda: nc.gpsimd.memset(ka[hb][64:70, :], -1.0), writes=[ka_t[hb]])
                        for i in range(3):
                            kb.dma("sp", qa[hb][64 + i:65 + i, :], Pp[i][h:h + 1, :], qa_t[hb], reads=[Pp_t], writes=[qa_t[hb]])
                            kb.dma("sp", ka[hb][67 + i:68 + i, :], Pp[i][h:h + 1, :], ka_t[hb], reads=[Pp_t], writes=[ka_t[hb]])

                        def mask_fn(kt, qc, r):
                            if kt >= 4 * qc:
                                return (cmask[:, :], 128, cmask_t)
                            return None
                        for qc in range(4):
                            attn_group(h, qc, bufs,
                                       k_lhsT=lambda kt, hb=hb: ka[hb][:, kt * 128:(kt + 1) * 128],
                                       q_rhs=lambda c0, c1, hb=hb: qa[hb][:, c0:c1],
                                       v_rhs=lambda kt, h=h: vaug[:, kt, h, :],
                                       mask_fn=mask_fn, y_tm=y_tm, y_tm_t=y_tm_t,
                                       kin_t=[qa_t[hb], ka_t[hb]], vin_t=[vaug_t])
                    kb.barrier()
                    kb.release_phase_dsems()
                with ExitStack() as p2:
                    if "nofox" in STAGES:
                        kb.op("pool", lambda: nc.gpsimd.memset(y_tm[:].rearrange("p a b -> p (a b)"), 0.0), writes=[y_tm_t])
                    tm_to_fm(y_tm, y_tm_t, ymT, ymT_t, 0)
                    mixer_out(p2, ymT, ymT_t, odwo_d[:, :], "o")
                    kb.barrier()
                    kb.release_phase_dsems()

        def even_phase(l):
            R0 = 32.0
            NIT = 29
            with ExitStack() as ph:
                ypT = sb("ypT", [128, 4, S], BF16, ph)
                ypT_t = [kb.t("ypT%d" % c) for c in range(4)]
                qr = sb("qr", [128, 4, S], BF16, ph)
                qr_t = [kb.t("qr%d" % c) for c in range(4)]
                kAB = sb("kAB", [128, 2, S], BF16, ph)
                kAB_t = [kb.t("kAB%d" % c) for c in range(2)]
                iqr = sb("iqr", [128, 2, S], BF16, ph)
                iqr_t = [kb.t("iqr%d" % c) for c in range(2)]
                ikAB = sb("ikAB", [128, 2, S], BF16, ph)
                ikAB_t = [kb.t("ikAB%d" % c) for c in range(2)]
                vaug = sb("vaug_e", [128, NT, 65], BF16, ph)
                vaug_t = kb.t("vaug_e")
                iwt = sb("iwt", [128, NT, 4], F32, ph)
                iwt_t = kb.t("iwt")
                with ExitStack() as p1:
                    hn = sb("hn_e", [128, NC, S], BF16, p1)
                    hn_t = [kb.t("hn%d" % c) for c in range(NC)]
                    wt = [sb("ewt%d" % i, [128, NC, 128], BF16, p1) for i in range(4)]
                    wt_t = [kb.t("ewt%d" % i, dma=True) for i in range(4)]
                    wst = {"i": 0}

                    def load_w(idx):
                        b = wst["i"] % 4
                        wst["i"] += 1
                        kb.dma("pool", wt[b][:].rearrange("p c n -> p (c n)"), evw_d[idx], wt_t[b], writes=[wt_t[b]])
                        return wt[b], wt_t[b]

                    def proj_fm(w, w_t, bi, tc):
                        sl = slice(tc * 512, (tc + 1) * 512)
                        fns = [(lambda c=c: nc.tensor.matmul(banks[bi][:, :], lhsT=w[:, c, :], rhs=hn[:, c, sl],
                                                             start=(c == 0), stop=(c == NC - 1))) for c in range(NC)]
                        kb.group("pe", fns, reads=[w_t] + hn_t, writes=[bank_t[bi]])

                    rmsnorm(p1, 3 * l, hn, hn_t)
                    wv_, wv_t_ = load_w(24)
                    kb.op("pool", lambda: nc.gpsimd.memset(vaug[:].rearrange("p a b -> p (a b)"), 1.0), writes=[vaug_t])
                    for tt in range(NT):
                        bi = tt % 4
                        fns = [(lambda c=c, bi=bi, tt=tt: nc.tensor.matmul(
                            banks[bi][:, 0:128], lhsT=hn[:, c, tt * 128:(tt + 1) * 128], rhs=wv_[:, c, :],
                            start=(c == 0), stop=(c == NC - 1))) for c in range(NC)]
                        kb.group("pe", fns, reads=[wv_t_] + hn_t, writes=[bank_t[bi]])
                        kb.op("dve", lambda bi=bi, tt=tt: nc.vector.tensor_copy(out=vaug[:, tt, 0:64], in_=banks[bi][:, 0:64]),
                              reads=[], writes=[vaug_t, bank_t[bi]])
                        kb.op("dve", lambda bi=bi, tt=tt: nc.vector.tensor_scalar(
                            out=iwt[:, tt, :], in0=banks[bi][:, 64:68], scalar1=0.0625, scalar2=None, op0=ALU.mult),
                            reads=[], writes=[iwt_t, bank_t[bi]])
                    with ExitStack() as p2:
                        zp = [sb("zp%d" % i, [128, 16 + S], F32, p2) for i in range(2)]
                        zp_t = [kb.t("zp%d" % i) for i in range(2)]
                        z0 = sb("z0", [128, S], F32, p2)
                        z0_t = kb.t("z0")
                        dd = sb("dd", [128, S], BF16, p2)
                        dd_t = kb.t("dd")
                        d16 = sb("d16", [128, 16], F32, p2)
                        d16_t = kb.t("d16")
                        pw = sb("pw", [128, 4, 128], BF16, p2)
                        pw_t = kb.t("pw", dma=True)
                        kb.dma("pool", pw[:], poolw_d[:, :, :], pw_t, writes=[pw_t])
                        pcs = sb("pcs", [128, 4 + 64], F32, p2)
                        pcs_t = kb.t("pcs", dma=True)
                        kb.dma("sp", pcs[:], poolc_d[:, :], pcs_t, writes=[pcs_t])
                        for i in range(2):
                            kb.op("pool", lambda i=i: nc.gpsimd.memset(zp[i][:, 0:16], 0.0), writes=[zp_t[i]])
                        for g in range(4):
                            w_, w_t_ = load_w(g)
                            for tc in range(4):
                                bi = 4 + tc
                                proj_fm(w_, w_t_, bi, tc)
                                kb.op("act", lambda bi=bi, tc=tc: nc.scalar.copy(out=z0[:, tc * 512:(tc + 1) * 512], in_=banks[bi][:, :]),
                                      reads=[], writes=[z0_t, bank_t[bi]])
                            cur = None
                            nsteps = g + 1
                            for sidx in range(nsteps):
                                sh = 1 << sidx
                                o = zp[sidx % 2]
                                ot = zp_t[sidx % 2]
                                if sidx == 0:
                                    kb.op("pool", lambda: nc.gpsimd.tensor_copy(out=zp[1][:, 16:16 + S], in_=z0[:, :]),
                                          reads=[z0_t], writes=[zp_t[1]])
                                    src, srct = zp[1], zp_t[1]
                                else:
                                    src, srct = zp[(sidx + 1) % 2], zp_t[(sidx + 1) % 2]
                                eng = "pool" if sidx % 2 == 0 else "dve"
                                kb.op(eng, lambda o=o, src=src, sh=sh, eng=eng: E[eng].tensor_tensor(
                                    out=o[:, 16:16 + S], in0=src[:, 16:16 + S], in1=src[:, 16 - sh:16 - sh + S], op=ALU.add),
                                    reads=[srct], writes=[ot])
                                cur, curt = o, ot
                            wwin = float(2 << g)
                            kb.op("dve", lambda cur=cur, wwin=wwin: nc.vector.scalar_tensor_tensor(
                                out=dd[:, :], in0=cur[:, 16:16 + S], scalar=1.0 / wwin, in1=z0[:, :], op0=ALU.mult, op1=ALU.subtract),
                                reads=[curt, z0_t], writes=[dd_t])
                            kb.op("dve", lambda cur=cur, g=g: nc.vector.tensor_tensor(
                                out=d16[:, :], in0=cur[:, 16:32], in1=pcs[:, 4 + g * 16:4 + (g + 1) * 16], op=ALU.mult),
                                reads=[curt, pcs_t], writes=[d16_t])
                            kb.op("dve", lambda: nc.vector.tensor_tensor(out=dd[:, 0:16], in0=d16[:, :], in1=z0[:, 0:16], op=ALU.subtract),
                                  reads=[d16_t, z0_t, dd_t], writes=[dd_t])
                            for tc in range(4):
                                bi = tc
                                sl = slice(tc * 512, (tc + 1) * 512)
                                kb.group("pe", [lambda bi=bi, g=g, sl=sl: nc.tensor.matmul(
                                    banks[bi][:, :], lhsT=pw[:, g, :], rhs=dd[:, sl], start=True, stop=True)],
                                    reads=[pw_t, dd_t], writes=[bank_t[bi]])
                                kb.op("act", lambda bi=bi, g=g, sl=sl: nc.scalar.activation(
                                    out=ypT[:, g, sl], in_=banks[bi][:, :], func=AF.Identity, scale=pcs[:, g:g + 1]),
                                    reads=[pcs_t], writes=[ypT_t[g], bank_t[bi]])
                        kb.barrier()
                    with ExitStack() as p2:
                        if "e_stop1" not in STAGES:
                          cosT = sb("cosT", [128, S], F32, p2)
                          sinT = sb("sinT", [128, S], F32, p2)
                          tab_t = kb.t("ropetab")
                          posi = sb("posi", [128, S], I32, p2)
                          posi_t = kb.t("posi", dma=True)
                          kb.dma("sp", posi[:], posr_d[:, :], posi_t, writes=[posi_t])
                          rc = sb("ropec", [128, 2], F32, p2)
                          rc_t = kb.t("ropec", dma=True)
                          kb.dma("sp", rc[:], ropec_d[:, :], rc_t, writes=[rc_t])
                          ang = sb("ang", [128, S], F32, p2)
                          ang_t = kb.t("ang")
                          t1 = [sb("rt1_%d" % i, [128, 512], F32, p2) for i in range(2)]
                          t1_t = [kb.t("rt1_%d" % i) for i in range(2)]
                          t2 = [sb("rt2_%d" % i, [128, 512], F32, p2) for i in range(2)]
                          t2_t = [kb.t("rt2_%d" % i) for i in range(2)]
                          ki = posi
                          TWO_PI = 6.283185307179586
                          PI = 3.141592653589793
                          kb.op("dve", lambda: nc.vector.tensor_copy(out=ang[:, :], in_=posi[:, :]), reads=[posi_t], writes=[ang_t])
                          kb.op("dve", lambda: nc.vector.tensor_scalar(out=ang[:, :], in0=ang[:, :], scalar1=rc[:, 0:1], scalar2=None, op0=ALU.mult),
                                reads=[ang_t, rc_t], writes=[ang_t])
                          for (tab, phi) in ((sinT, 0.0), (cosT, PI / 2)):
                              kb.op("dve", lambda tab=tab, phi=phi: nc.vector.tensor_scalar(
                                  out=tab[:, :], in0=ang[:, :], scalar1=phi + PI, scalar2=1.0 / TWO_PI, op0=ALU.add, op1=ALU.mult),
                                  reads=[ang_t], writes=[tab_t])
                              kb.op("dve", lambda tab=tab: nc.vector.tensor_copy(out=ki[:, :], in_=tab[:, :]), reads=[tab_t], writes=[posi_t])
                              kb.op("dve", lambda tab=tab: nc.vector.tensor_copy(out=tab[:, :], in_=ki[:, :]), reads=[posi_t], writes=[tab_t])
                              kb.op("dve", lambda tab=tab: nc.vector.scalar_tensor_tensor(
                                  out=tab[:, :], in0=tab[:, :], scalar=-TWO_PI, in1=ang[:, :], op0=ALU.mult, op1=ALU.add),
                                  reads=[tab_t, ang_t], writes=[tab_t])
                              if phi != 0.0:
                                  kb.op("dve", lambda tab=tab, phi=phi: nc.vector.tensor_scalar(
                                      out=tab[:, :], in0=tab[:, :], scalar1=phi, scalar2=None, op0=ALU.add), reads=[tab_t], writes=[tab_t])
                              for half in range(4):
                                  sl = slice(half * 512, (half + 1) * 512)
                                  for (cmp_op, thr, corr) in ((ALU.is_gt, PI, -TWO_PI), (ALU.is_lt, -PI, TWO_PI)):
                                      kb.op("dve", lambda cmp_op=cmp_op, thr=thr, corr=corr, tab=tab, sl=sl: nc.vector.tensor_scalar(
                                          out=t1[0][:, :], in0=tab[:, sl], scalar1=thr, scalar2=corr, op0=cmp_op, op1=ALU.mult),
                                          reads=[tab_t], writes=[t1_t[0]])
                                      kb.op("dve", lambda tab=tab, sl=sl: nc.vector.tensor_tensor(
                                          out=tab[:, sl], in0=tab[:, sl], in1=t1[0][:, :], op=ALU.add),
                                          reads=[tab_t, t1_t[0]], writes=[tab_t])
                              kb.op("dve", lambda tab=tab: nc.vector.tensor_scalar(
                                  out=tab[:, :], in0=tab[:, :], scalar1=-3.14159, scalar2=3.14159, op0=ALU.max, op1=ALU.min),
                                  reads=[tab_t], writes=[tab_t])
                              kb.op("act", lambda tab=tab: nc.scalar.activation(out=tab[:, :], in_=tab[:, :], func=AF.Sin),
                                    reads=[tab_t], writes=[tab_t])
                          kb.op("dve", lambda: nc.vector.tensor_scalar(out=sinT[:, :], in0=sinT[:, :], scalar1=rc[:, 1:2], scalar2=None, op0=ALU.mult),
                                reads=[tab_t, rc_t], writes=[tab_t])
                          jobs = []
                          for j in range(4):
                              jobs.append((qr, j, qr_t[j], 4 + j, 8 + j))
                          jobs.append((kAB, 0, kAB_t[0], 12, 14))
                          jobs.append((kAB, 1, kAB_t[1], 13, 15))
                          jobs.append((iqr, 0, iqr_t[0], 16, 18))
                          jobs.append((iqr, 1, iqr_t[1], 17, 19))
                          jobs.append((ikAB, 0, ikAB_t[0], 20, 22))
                          jobs.append((ikAB, 1, ikAB_t[1], 21, 23))
                          it = 0
                          for (dst, dj, dst_t, wi, wpi) in jobs:
                              w_, w_t_ = load_w(wi)
                              wp_, wp_t_ = load_w(wpi)
                              for tc in range(4):
                                  b = it % 2
                                  it += 1
                                  bx = 2 * b
                                  bp = 2 * b + 1
                                  sl = slice(tc * 512, (tc + 1) * 512)
                                  proj_fm(w_, w_t_, bx, tc)
                                  proj_fm(wp_, wp_t_, bp, tc)
                                  kb.op("dve", lambda b=b, bx=bx, sl=sl: nc.vector.tensor_tensor(
                                      out=t1[b][:, :], in0=banks[bx][:, :], in1=cosT[:, sl], op=ALU.mult),
                                      reads=[tab_t], writes=[t1_t[b], bank_t[bx]])
                                  kb.op("dve", lambda b=b, bp=bp, sl=sl: nc.vector.tensor_tensor(
                                      out=t2[b][:, :], in0=banks[bp][:, :], in1=sinT[:, sl], op=ALU.mult),
                                      reads=[tab_t], writes=[t2_t[b], bank_t[bp]])
                                  kb.op("pool", lambda b=b, dst=dst, dj=dj, sl=sl: nc.gpsimd.tensor_tensor(
                                      out=dst[:, dj, sl], in0=t1[b][:, :], in1=t2[b][:, :], op=ALU.add),
                                      reads=[t1_t[b], t2_t[b]], writes=[dst_t])
                        kb.barrier()
                    kb.release_phase_dsems()
                y_tm = sb("ydsa_tm", [128, NT, 512], BF16, ph)
                y_tm_t = kb.t("ydsa_tm")
                with ExitStack() as p1:
                    sc = [sb("sc%d" % i, [128, S], F32, p1) for i in range(2)]
                    sc_t = [kb.t("sc%d" % i) for i in range(2)]
                    rtmp = sb("rtmp", [128, S], F32, p1)
                    rtmp_t = kb.t("rtmp")
                    junk = sb("junk", [128, S], BF16, p1)
                    mb = [sb("mb%d" % i, [128, S], BF16, p1) for i in range(2)]
                    mb_t = [kb.t("mb%d" % i) for i in range(2)]
                    pert = sb("pert", [128, S], F32, p1)
                    pert_t = kb.t("pert", dma=True)
                    kb.dma("sp", pert[:], pert_d[:, :], pert_t, writes=[pert_t])
                    MBT = sb("MBT", [128, NT, 512], BF16, p1)
                    MBT_t = kb.t("MBT")
                    PT = [sb("PTe%d" % i, [128, 512], BF16, p1) for i in range(3)]
                    PT_t = [kb.t("PTe%d" % i) for i in range(3)]
                    rden = [sb("rdene%d" % i, [128, 4], F32, p1) for i in range(2)]
                    rden_t = [kb.t("rdene%d" % i) for i in range(2)]
                    bs_ = [sb("bis%d" % i, [128, 8], F32, p1) for i in range(2)]
                    bs_t = [kb.t("bis%d" % i) for i in range(2)]
                    ncol = sb("ncol", [128, NT], F32, p1)
                    ncol_t = kb.t("ncol")
                    for qt in range(NT):
                        kb.op("pool", lambda qt=qt: nc.gpsimd.memset(ncol[:, qt:qt + 1], float(128 * (qt + 1) - 513)), writes=[ncol_t])
                    st = {"g": 0, "s": 0}
                    bufs = (PT, PT_t, rden, rden_t, st)
                    for qc in (range(4) if ("e_stop1" not in STAGES and "e_stop2" not in STAGES) else []):
                        for qs in range(4):
                            qt = 4 * qc + qs
                            n = 128 * (qt + 1)
                            b = qt % 2
                            s_ = sc[b]
                            s_t = sc_t[b]
                            nch = (n + 511) // 512
                            for h4 in range(4):
                                par = h4 % 2
                                for ch in range(nch):
                                    c0 = ch * 512
                                    w = min(512, n - c0)
                                    kb.group("pe", [lambda ch=ch, c0=c0, w=w, h4=h4, par=par, qt=qt: nc.tensor.matmul(
                                        banks[ch][:, 0:w], lhsT=iqr[:, h4 // 2, qt * 128:(qt + 1) * 128], rhs=ikAB[:, par, c0:c0 + w],
                                        start=True, stop=True)],
                                        reads=[iqr_t[h4 // 2], ikAB_t[par]], writes=[bank_t[ch]])
                                    kb.op("act", lambda ch=ch, c0=c0, w=w: nc.scalar.activation(
                                        out=rtmp[:, c0:c0 + w], in_=banks[ch][:, 0:w], func=AF.Relu),
                                        reads=[], writes=[rtmp_t, bank_t[ch]])
                                in1 = pert if h4 == 0 else s_
                                in1_t = pert_t if h4 == 0 else s_t
                                kb.op("dve", lambda s_=s_, in1=in1, h4=h4, qt=qt, n=n: nc.vector.scalar_tensor_tensor(
                                    out=s_[:, 0:n], in0=rtmp[:, 0:n], scalar=iwt[:, qt, h4:h4 + 1], in1=in1[:, 0:n],
                                    op0=ALU.mult, op1=ALU.add), reads=[rtmp_t, iwt_t, in1_t], writes=[s_t])
                            kb.op("dve", lambda s_=s_, n=n: nc.vector.memset(s_[0:64, n - 64:n], -1e30), reads=[s_t], writes=[s_t])
                            bt = bs_[b]
                            btt = bs_t[b]
                            use_act = (qt % 2 == 1) and ("noacttopk" not in STAGES)
                            if qt < 2:
                                kb.op("dve", lambda bt=bt: nc.vector.memset(bt[:, 3:4], -1e29), writes=[btt])
                            elif not use_act:
                                kb.op("dve", lambda bt=bt: nc.vector.memset(bt[:, 0:1], 0.0), writes=[btt])
                                for i in range(NIT):
                                    dl = R0 / (2 ** (i + 1))
                                    kb.op("dve", lambda s_=s_, n=n, bt=bt: nc.vector.tensor_scalar(
                                        out=junk[:, 0:n], in0=s_[:, 0:n], scalar1=bt[:, 0:1], scalar2=None, op0=ALU.is_gt, op1=ALU.add,
                                        accum_out=bt[:, 1:2]), reads=[s_t, btt], writes=[btt])
                                    kb.op("dve", lambda bt=bt, dl=dl: nc.vector.tensor_scalar(
                                        out=bt[:, 2:3], in0=bt[:, 1:2], scalar1=256.5, scalar2=2.0 * dl, op0=ALU.is_ge, op1=ALU.mult),
                                        reads=[btt], writes=[btt])
                                    kb.op("dve", lambda bt=bt, dl=dl: nc.vector.scalar_tensor_tensor(
                                        out=bt[:, 0:1], in0=bt[:, 2:3], scalar=-dl, in1=bt[:, 0:1], op0=ALU.add, op1=ALU.add),
                                        reads=[btt], writes=[btt])
                                kb.op("dve", lambda bt=bt: nc.vector.tensor_scalar(
                                    out=bt[:, 3:4], in0=bt[:, 0:1], scalar1=R0 / (2 ** NIT), scalar2=None, op0=ALU.add),
                                    reads=[btt], writes=[btt])
                            else:
                                kb.op("act", lambda bt=bt: nc.scalar.activation(out=bt[:, 0:1], in_=zero_c[:, 0:1], func=AF.Identity),
                                      reads=[zeroc_t], writes=[btt])
                                for i in range(NIT):
                                    dl = R0 / (2 ** (i + 1))
                                    kb.op("act", lambda s_=s_, n=n, bt=bt: nc.scalar.activation(
                                        out=junk[:, 0:n], in_=s_[:, 0:n], func=AF.Sign, bias=bt[:, 0:1], accum_out=bt[:, 1:2]),
                                        reads=[s_t, btt], writes=[btt])
                                    kb.op("act", lambda bt=bt, qt=qt: nc.scalar.activation(
                                        out=bt[:, 2:3], in_=bt[:, 1:2], func=AF.Sign, bias=ncol[:, qt:qt + 1]),
                                        reads=[btt, ncol_t], writes=[btt])
                                    kb.op("act", lambda bt=bt, dl=dl: nc.scalar.activation(
                                        out=bt[:, 0:1], in_=bt[:, 2:3], func=AF.Identity, scale=-dl, bias=bt[:, 0:1]),
                                        reads=[btt], writes=[btt])
                                kb.op("act", lambda bt=bt: nc.scalar.activation(
                                    out=bt[:, 3:4], in_=bt[:, 0:1], func=AF.Identity, scale=-1.0, bias=tauc[:, 0:1]),
                                    reads=[btt, tauc_t], writes=[btt])
                            m_ = mb[b]
                            m_t = mb_t[b]
                            kb.op("dve", lambda m_=m_, s_=s_, n=n, bt=bt: nc.vector.tensor_scalar(
                                out=m_[:, 0:n], in0=s_[:, 0:n], scalar1=bt[:, 3:4], scalar2=-30000.0, op0=ALU.is_le, op1=ALU.mult),
                                reads=[s_t, btt], writes=[m_t])
                            for k0 in range(0, qt + 1, 4):
                                kn = min(4, qt + 1 - k0)
                                bi = (k0 // 4) % 4
                                bkb = banks[bi][:, :].bitcast(BF16)
                                fns = [(lambda k=k, bkb=bkb, m_=m_, k0=k0: nc.tensor.transpose(
                                    out=bkb[:, k * 128:(k + 1) * 128], in_=m_[:, (k0 + k) * 128:(k0 + k + 1) * 128], identity=ident_b[:]))
                                    for k in range(kn)]
                                kb.group("pe", fns, reads=[m_t, ident_b_t], writes=[bank_t[bi]])
                                dst = MBT[:, k0:k0 + kn, qs * 128:(qs + 1) * 128]
                                src = bkb[:, 0:kn * 128].rearrange("p (k n) -> p k n", k=kn)
                                kb.op("dve", lambda dst=dst, src=src: nc.vector.tensor_copy(out=dst, in_=src),
                                      reads=[], writes=[MBT_t, bank_t[bi]])

                        def mask_fn(kt, qc_, r):
                            return (MBT[:, kt, r * 128:512], 512 - 128 * r, MBT_t)
                        for h in (range(8) if "e_stop3" not in STAGES else []):
                            attn_group(h, qc, bufs,
                                       k_lhsT=lambda kt, h=h: kAB[:, h % 2, kt * 128:(kt + 1) * 128],
                                       q_rhs=lambda c0, c1, h=h: qr[:, h // 2, c0:c1],
                                       v_rhs=lambda kt: vaug[:, kt, :],
                                       mask_fn=mask_fn, y_tm=y_tm, y_tm_t=y_tm_t,
                                       kin_t=[qr_t[h // 2], kAB_t[h % 2]], vin_t=[vaug_t])
                    kb.barrier()
                    kb.release_phase_dsems()
                with ExitStack() as p1:
                    ydT = sb("ydT", [128, 4, S], BF16, p1)
                    ydT_t = [kb.t("ydT%d" % c) for c in range(4)]
                    tm_to_fm(y_tm, y_tm_t, ydT, ydT_t, 0)
                    mixer_out(p1, None, ypT_t + ydT_t, evwo_d[:, :], "e",
                              ym_fn=lambda c, sl: (ypT[:, c, sl] if c < 4 else ydT[:, c - 4, sl]))
                    kb.barrier()
                    kb.release_phase_dsems()

        for l in range(2):
            if "mix%d" % l in STAGES:
                if l == 1:
                    odd_phase(l)
                else:
                    even_phase(l)
            if "ffn%d" % l in STAGES:
                ffn_phase(l)
            if "ple%d" % l in STAGES:
                ple_phase(l)

        with ExitStack() as ph:
            rmsnorm(ph, 6, hT, hT_t)
            ost = [sb("ost%d" % i, [128, D], F32, ph) for i in range(2)]
            ost_t = [kb.t("ost%d" % i, dma=True) for i in range(2)]
            for tt in range(NT):
                b = tt % 2
                for half in range(2):
                    bi = 4 + (2 * tt + half) % 4
                    bk = banks[bi]
                    fns = []
                    for k in range(4):
                        c = half * 4 + k
                        fns.append(lambda k=k, c=c, bk=bk: nc.tensor.transpose(
                            out=bk[:, k * 128:(k + 1) * 128], in_=hT[:, c, tt * 128:(tt + 1) * 128], identity=ident_f[:]))
                    kb.group("pe", fns, reads=hT_t[half * 4:(half + 1) * 4] + [ident_f_t], writes=[bank_t[bi]])
                    eng = alt(["dve", "act"])
                    dst = ost[b][:, half * 512:(half + 1) * 512]
                    if eng == "dve":
                        kb.op("dve", lambda dst=dst, bk=bk: nc.vector.tensor_copy(out=dst, in_=bk[:, :]),
                              reads=[bank_t[bi]], writes=[ost_t[b]])
                    else:
                        kb.op("act", lambda dst=dst, bk=bk: nc.scalar.copy(out=dst, in_=bk[:, :]),
                              reads=[bank_t[bi]], writes=[ost_t[b]])
                kb.dma("sp", out_d[tt * 128:(tt + 1) * 128, :], ost[b][:], ost_t[b], reads=[ost_t[b]])
            kb.barrier()
    return nc


def prep_inputs(inputs):
    f = lambda a: np.ascontiguousarray(np.asarray(a, dtype=np.float32))
    x = f(inputs["x"])
    p = f(inputs["p"])
    pos = np.ascontiguousarray(np.asarray(inputs["positions"], dtype=np.int32))
    gl = []
    for l in range(2):
        for nm in ("norm_mix", "norm_ffn", "norm_ple"):
            gl.append(f(inputs[nm])[l])
    gl.append(f(inputs["norm_final"]))
    gains = np.ascontiguousarray(np.stack([g.reshape(8, 128).T for g in gl], axis=1))
    shared = {"gains": gains, "ident": np.eye(128, dtype=np.float32)}
    ow = f(inputs["od_w_in"])[0]

    def kcn(w):
        n = w.shape[1]
        return np.ascontiguousarray(w.reshape(8, 128, n).transpose(1, 0, 2)).reshape(128, 8 * n)
    shared["odw"] = np.stack([kcn(ow[:, 0:512]), kcn(ow[:, 512:1024]), kcn(ow[:, 1024:1536]),
                              kcn(ow[:, 1544:2056]), kcn(ow[:, 2056:2568])], axis=0)
    shared["odw2"] = kcn(np.concatenate([ow[:, 1024:1536], ow[:, 1536:1544], np.zeros((1024, 120), np.float32)], axis=1))
    shared["odwo"] = kcn(f(inputs["od_w_out"])[0])
    shared["odqk"] = np.stack([kcn(np.concatenate([ow[:, h * 64:(h + 1) * 64], ow[:, 512 + h * 64:512 + (h + 1) * 64]], axis=1))
                               for h in range(8)], axis=0)
    shared["lng"] = np.ascontiguousarray(np.broadcast_to(f(inputs["sgu_ln_g"])[0][None, :], (128, 512)))
    shared["lnb"] = np.ascontiguousarray(np.broadcast_to(f(inputs["sgu_ln_b"])[0][None, :], (128, 512)))
    shared["sguw"] = np.ascontiguousarray(f(inputs["sgu_w"])[0].transpose(2, 0, 1))
    ci = np.arange(128) // 64
    shared["sgumask"] = np.ascontiguousarray((ci[:, None] <= ci[None, :]).astype(np.float32))
    sbb = f(inputs["sgu_b"])[0]
    shared["sgub"] = np.ascontiguousarray(np.broadcast_to(np.tile(sbb, (1, 4))[None, :, :], (128, 4, 512)))
    shared["foxbf"] = np.ascontiguousarray(f(inputs["fox_b_f"])[0].reshape(8, 1))
    ar = np.arange(128)
    shared["cmask"] = np.where(ar[:, None] <= ar[None, :], 0.0, -30000.0).astype(np.float32)
    ew = f(inputs["ev_w_in"])[0]

    def permh(w):
        nh = w.shape[1] // 64
        idx = []
        for h_ in range(nh):
            for d_ in range(64):
                pd = d_ + 8 if d_ < 8 else (d_ - 8 if d_ < 16 else d_)
                idx.append(h_ * 64 + pd)
        return w[:, idx]
    z64 = np.zeros((1024, 64), np.float32)
    qW = ew[:, 512:1024]
    kW = ew[:, 1024:1088]
    iqW = ew[:, 1152:1408]
    ikW = ew[:, 1408:1472]
    tl = [ew[:, g_ * 128:(g_ + 1) * 128] for g_ in range(4)]
    tl += [qW[:, j_ * 128:(j_ + 1) * 128] for j_ in range(4)]
    tl += [permh(qW)[:, j_ * 128:(j_ + 1) * 128] for j_ in range(4)]
    tl += [np.concatenate([kW, z64], 1), np.concatenate([z64, kW], 1),
           np.concatenate([permh(kW), z64], 1), np.concatenate([z64, permh(kW)], 1)]
    tl += [iqW[:, j_ * 128:(j_ + 1) * 128] for j_ in range(2)]
    tl += [permh(iqW)[:, j_ * 128:(j_ + 1) * 128] for j_ in range(2)]
    tl += [np.concatenate([ikW, z64], 1), np.concatenate([z64, ikW], 1),
           np.concatenate([permh(ikW), z64], 1), np.concatenate([z64, permh(ikW)], 1)]
    tl += [np.concatenate([ew[:, 1088:1152], ew[:, 1472:1476], np.zeros((1024, 60), np.float32)], 1)]
    shared["evw"] = np.stack([kcn(t_) for t_ in tl], axis=0)
    shared["evwo"] = kcn(f(inputs["ev_w_out"])[0])
    shared["poolw"] = np.ascontiguousarray(f(inputs["pool_w"])[0].transpose(1, 0, 2))
    pc = np.zeros((128, 68), np.float32)
    pc[:, 0:4] = f(inputs["pool_scale"])[0].reshape(4, 128).T
    for g_, w_ in enumerate((2, 4, 8, 16)):
        pc[:, 4 + g_ * 16:4 + (g_ + 1) * 16] = (1.0 / np.minimum(np.arange(16) + 1, w_)).astype(np.float32)[None, :]
    shared["poolc"] = pc
    inv = (np.float32(500000.0) ** (-np.arange(8, dtype=np.float32) / np.float32(8))).astype(np.float32)
    rc_ = np.zeros((128, 2), np.float32)
    for p_ in range(128):
        d_ = p_ % 64
        if d_ < 16:
            rc_[p_, 0] = inv[d_ % 8]
            rc_[p_, 1] = -1.0 if d_ < 8 else 1.0
    shared["ropec"] = rc_
    shared["pert"] = np.ascontiguousarray(np.broadcast_to((-(2.0 ** -22) * np.arange(S, dtype=np.float64)).astype(np.float32)[None, :], (128, S)))
    cw = f(inputs["ffn_conv_w"])
    cb = f(inputs["ffn_conv_b"])
    cwb = np.concatenate([cw.reshape(2, 3, 44, 128).transpose(0, 3, 2, 1), cb.reshape(2, 44, 128).transpose(0, 2, 1)[..., None]], axis=-1)
    shared["cwb"] = np.ascontiguousarray(cwb)
    wi = f(inputs["ffn_w_in"]).reshape(2, 8, 128, 2, 22, 128)
    shared["win"] = np.ascontiguousarray(wi.transpose(0, 4, 2, 3, 1, 5)).reshape(2, 22, 128, 2 * 8 * 128)
    wo = f(inputs["ffn_w_out"]).reshape(2, 2, 11, 128, 8, 128)
    shared["wout"] = np.ascontiguousarray(wo.transpose(0, 1, 4, 3, 2, 5)).reshape(2, 2, 8, 128, 11 * 128)
    wgt = f(inputs["ple_w_gate"]).reshape(2, 8, 128, 8, 128)
    shared["wgate"] = np.ascontiguousarray(wgt.transpose(0, 3, 2, 1, 4)).reshape(2, 8, 128, 8 * 128)
    wpj = f(inputs["ple_w_proj"]).reshape(2, 2, 128, 8, 128)
    shared["wproj"] = np.ascontiguousarray(wpj.transpose(0, 2, 3, 1, 4)).reshape(2, 128, 8 * 2 * 128)
    in_maps = []
    for b in range(8):
        m = dict(shared)
        m["x"] = np.ascontiguousarray(x[b])
        m["p"] = np.ascontiguousarray(p[:, b])
        m["pos"] = np.ascontiguousarray(pos[b:b + 1])
        m["posr"] = np.ascontiguousarray(np.broadcast_to(pos[b:b + 1], (128, S)))
        in_maps.append(m)
    return in_maps


_CACHE = {}


def kernel(**inputs):
    in_maps = prep_inputs(inputs)
    if "nc" not in _CACHE:
        _CACHE["nc"] = build_program()
    nc = _CACHE["nc"]
    ncores = int(_CACHE.get("dev_cores", 8))
    res = run_bass_kernel_spmd(nc, in_maps[:ncores], core_ids=list(range(ncores)))
    _CACHE["last_res"] = res
    out = np.stack([np.asarray(r["out"], dtype=np.float32) for r in res.results], axis=0)
    return out
```

```python
import numpy as np
from contextlib import ExitStack
import concourse.bass as bass
import concourse.mybir as mybir
from concourse.bass_utils import run_bass_kernel_spmd

F32 = mybir.dt.float32
BF16 = mybir.dt.bfloat16
I32 = mybir.dt.int32
AF = mybir.ActivationFunctionType
ALU = mybir.AluOpType

S = 2048
D = 1024
NT = S // 128
NC = D // 128
EPS = 1e-6
STAGES = {"mix0", "ffn0", "ple0", "mix1", "ffn1", "ple1"}
DEBUG_OUT = None


class T:
    def __init__(self, name):
        self.name = name
        self.last_w = None
        self.readers = {}
        self.dsem = None


class KB:
    def __init__(self, nc, es):
        self.nc = nc
        self.es = es
        self.E = {"pe": nc.tensor, "act": nc.scalar, "dve": nc.vector, "pool": nc.gpsimd, "sp": nc.sync}
        self.sem = {}
        for e in ["pe", "act", "dve", "pool"]:
            self.sem[e] = es.enter_context(nc.semaphore("s_" + e))
        self.cnt = {e: 0 for e in self.sem}
        self.waited = {e: {} for e in self.E}
        self.semobj = {}
        self.dpool = []
        for i in range(90):
            s = es.enter_context(nc.semaphore("d%d" % i))
            self.dpool.append(s)
            self.semobj[s.num] = s
        for e in self.sem:
            self.semobj[self.sem[e].num] = self.sem[e]
        self.dcount = {s.num: 0 for s in self.dpool}
        self.dfree = list(self.dpool)
        self.phase_dsems = []

    def t(self, name, dma=False):
        o = T(name)
        if dma:
            o.dsem = self.dfree.pop()
            self.phase_dsems.append(o.dsem)
        return o

    def release_phase_dsems(self):
        for s in self.phase_dsems:
            self.dfree.append(s)
        self.phase_dsems = []

    def _wait(self, eng, ev):
        semnum, val = ev
        w = self.waited[eng]
        if w.get(semnum, 0) >= val:
            return
        self.E[eng].wait_ge(self.semobj[semnum], val)
        w[semnum] = val

    def _deps(self, eng, reads, writes):
        mysem = self.sem[eng].num if eng in self.sem else None
        for t in reads:
            if t.last_w is not None:
                if t.last_w[0] == mysem and eng == "pe":
                    continue
                self._wait(eng, t.last_w)
        for t in writes:
            if t.last_w is not None and t.last_w[0] != mysem:
                self._wait(eng, t.last_w)
            for sn, v in t.readers.items():
                if sn != mysem:
                    self._wait(eng, (sn, v))

    def _record(self, ev, reads, writes):
        for t in reads:
            if t.readers.get(ev[0], 0) < ev[1]:
                t.readers[ev[0]] = ev[1]
        for t in writes:
            t.last_w = ev
            t.readers = {}

    def op(self, eng, fn, reads=(), writes=()):
        self._deps(eng, reads, writes)
        ins = fn()
        self.cnt[eng] += 1
        ins.then_inc(self.sem[eng], 1)
        ev = (self.sem[eng].num, self.cnt[eng])
        self._record(ev, reads, writes)
        return ev

    def group(self, eng, fns, reads=(), writes=()):
        self._deps(eng, reads, writes)
        ins = None
        for fn in fns:
            ins = fn()
        self.cnt[eng] += 1
        ins.then_inc(self.sem[eng], 1)
        ev = (self.sem[eng].num, self.cnt[eng])
        self._record(ev, reads, writes)
        return ev

    def dma(self, q, out, in_, obj, reads=(), writes=()):
        self._deps(q, reads, writes)
        self.dcount[obj.dsem.num] += 1
        self.E[q].dma_start(out=out, in_=in_).then_inc(obj.dsem, 16)
        ev = (obj.dsem.num, 16 * self.dcount[obj.dsem.num])
        self._record(ev, reads, writes)
        return ev

    def barrier(self):
        evs = [(self.sem[e].num, self.cnt[e]) for e in self.sem if self.cnt[e] > 0]
        for s in self.dpool:
            if self.dcount[s.num] > 0:
                evs.append((s.num, 16 * self.dcount[s.num]))
        for e in self.E:
            for ev in evs:
                if e in self.sem and ev[0] == self.sem[e].num:
                    continue
                self._wait(e, ev)


def build_program():
    nc = bass.Bass("TRN2", target_bir_lowering=False)
    dr = {}

    def din(name, shape, dt=F32):
        dr[name] = nc.dram_tensor(name, list(shape), dt, kind="ExternalInput").ap()
        return dr[name]

    x_d = din("x", [S, D])
    p_d = din("p", [2, S, 256])
    pos_d = din("pos", [1, S], I32)
    gains_d = din("gains", [128, 7, 8])
    ident_d = din("ident", [128, 128])
    cwb_d = din("cwb", [2, 128, 44, 4])
    win_d = din("win", [2, 22, 128, 2 * NC * 128])
    wout_d = din("wout", [2, 2, NC, 128, 11 * 128])
    wgate_d = din("wgate", [2, NC, 128, NC * 128])
    wproj_d = din("wproj", [2, 128, NC * 2 * 128])
    odw_d = din("odw", [5, 128, NC * 512])
    odw_d2 = din("odw2", [128, NC * 640])
    odwo_d = din("odwo", [128, NC * D])
    odqk_d = din("odqk", [8, 128, NC * 128])
    lng_d = din("lng", [128, 512])
    lnb_d = din("lnb", [128, 512])
    sguw_d = din("sguw", [128, 4, 128])
    sgumask_d = din("sgumask", [128, 128])
    sgub_d = din("sgub", [128, 4, 512])
    foxbf_d = din("foxbf", [8, 1])
    cmask_d = din("cmask", [128, 128])
    evw_d = din("evw", [25, 128, NC * 128])
    evwo_d = din("evwo", [128, NC * D])
    poolw_d = din("poolw", [128, 4, 128])
    poolc_d = din("poolc", [128, 68])
    posr_d = din("posr", [128, S], I32)
    ropec_d = din("ropec", [128, 2])
    pert_d = din("pert", [128, S])
    out_d = nc.dram_tensor("out", [S, D], F32, kind="ExternalOutput").ap()

    with ExitStack() as es:
        kb = KB(nc, es)
        E = kb.E

        def sb(name, shape, dt, stack=es):
            return stack.enter_context(nc.sbuf_tensor("sb_" + name, list(shape), dt))

        hT = sb("hT", [128, NC, S], F32)
        hT_t = [kb.t("hT%d" % c) for c in range(NC)]
        gains = sb("gains", [128, 7, 8], F32)
        gains_t = kb.t("gains", dma=True)
        ident_f = sb("ident_f", [128, 128], F32)
        ident_f_t = kb.t("ident_f", dma=True)
        ident_b = sb("ident_b", [128, 128], BF16)
        ident_b_t = kb.t("ident_b", dma=True)
        ones_b = sb("ones_b", [128, 128], BF16)
        ones_b_t = kb.t("ones_b")
        one_c = sb("one_c", [128, 1], F32)
        onec_t = kb.t("onec")
        zeros_b = sb("zeros_b", [128, 512], BF16)
        zeros_t = kb.t("zeros")
        eps_c = sb("eps_c", [128, 1], F32)
        eps_t = kb.t("eps")
        banks = [es.enter_context(nc.psum_tensor("bank%d" % i, [128, 512], F32)) for i in range(8)]
        bank_t = [kb.t("bank%d" % i) for i in range(8)]

        kb.dma("sp", gains[:], gains_d[:, :, :], gains_t, writes=[gains_t])
        kb.dma("sp", ident_f[:], ident_d[:, :], ident_f_t, writes=[ident_f_t])
        kb.dma("pool", ident_b[:], ident_d[:, :], ident_b_t, writes=[ident_b_t])
        kb.op("dve", lambda: nc.vector.memset(ones_b[:], 1.0), writes=[ones_b_t])
        kb.op("dve", lambda: nc.vector.memset(eps_c[:], EPS), writes=[eps_t])
        kb.op("dve", lambda: nc.vector.memset(zeros_b[:], 0.0), writes=[zeros_t])
        kb.op("dve", lambda: nc.vector.memset(one_c[:], 1.0), writes=[onec_t])
        zero_c = sb("zero_c", [128, 1], F32)
        zeroc_t = kb.t("zeroc")
        tauc = sb("tauc", [128, 1], F32)
        tauc_t = kb.t("tauc")
        kb.op("dve", lambda: nc.vector.memset(zero_c[:], 0.0), writes=[zeroc_t])
        kb.op("dve", lambda: nc.vector.memset(tauc[:], 8.0 / (2 ** 26)), writes=[tauc_t])

        rr = {"i": 0}

        def alt(engs):
            rr["i"] += 1
            return engs[rr["i"] % len(engs)]

        with ExitStack() as ph:
            xs = [sb("xs%d" % i, [128, D], F32, ph) for i in range(2)]
            xs_t = [kb.t("xs%d" % i, dma=True) for i in range(2)]
            for tt in range(NT):
                b = tt % 2
                kb.dma("sp", xs[b][:], x_d[tt * 128:(tt + 1) * 128, :], xs_t[b], writes=[xs_t[b]])
                for half in range(2):
                    bi = (2 * tt + half) % 4
                    bk = banks[bi]
                    fns = []
                    for k in range(4):
                        c = half * 4 + k
                        fns.append(lambda k=k, c=c, bk=bk, b=b: nc.tensor.transpose(
                            out=bk[:, k * 128:(k + 1) * 128], in_=xs[b][:, c * 128:(c + 1) * 128], identity=ident_f[:]))
                    kb.group("pe", fns, reads=[xs_t[b], ident_f_t], writes=[bank_t[bi]])
                    eng = alt(["dve", "act"])
                    dst = hT[:, half * 4:(half + 1) * 4, tt * 128:(tt + 1) * 128]
                    src = bk[:, :].rearrange("p (k n) -> p k n", k=4)
                    if eng == "dve":
                        kb.op("dve", lambda dst=dst, src=src: nc.vector.tensor_copy(out=dst, in_=src),
                              reads=[bank_t[bi]], writes=hT_t[half * 4:(half + 1) * 4])
                    else:
                        kb.op("act", lambda dst=dst, src=src: nc.scalar.copy(out=dst, in_=src),
                              reads=[bank_t[bi]], writes=hT_t[half * 4:(half + 1) * 4])
            kb.barrier()
            kb.release_phase_dsems()

        def rmsnorm(ph_unused, gidx, outT, outT_t):
            with ExitStack() as ph:
                sqb = [sb("sqb%d_%d" % (gidx, i), [128, S], BF16, ph) for i in range(2)]
                sqb_t = [kb.t("sqb%d" % i) for i in range(2)]
                rstd = sb("rstd%d" % gidx, [128, S], F32, ph)
                rstd_t = [kb.t("rstd%d" % i) for i in range(4)]
                for c in range(NC):
                    b = c % 2
                    kb.op("act", lambda c=c, b=b: nc.scalar.activation(out=sqb[b][:], in_=hT[:, c, :], func=AF.Square),
                          reads=[hT_t[c]], writes=[sqb_t[b]])
                    fns = []
                    for tc in range(4):
                        fns.append(lambda tc=tc, c=c, b=b: nc.tensor.matmul(
                            banks[tc][:, :], lhsT=ones_b[:], rhs=sqb[b][:, tc * 512:(tc + 1) * 512],
                            start=(c == 0), stop=(c == NC - 1)))
                    kb.group("pe", fns, reads=[sqb_t[b], ones_b_t], writes=bank_t[0:4])
                for tc in range(4):
                    sl = slice(tc * 512, (tc + 1) * 512)
                    kb.op("act", lambda tc=tc, sl=sl: nc.scalar.activation(
                        out=rstd[:, sl], in_=banks[tc][:, :], func=AF.Sqrt, scale=1.0 / D, bias=eps_c[:]),
                        reads=[bank_t[tc], eps_t], writes=[rstd_t[tc]])
                    kb.op("dve", lambda sl=sl: nc.vector.reciprocal(out=rstd[:, sl], in_=rstd[:, sl]),
                          reads=[rstd_t[tc]], writes=[rstd_t[tc]])
                for c in range(NC):
                    kb.op("dve", lambda c=c: nc.vector.scalar_tensor_tensor(
                        out=outT[:, c, :], in0=hT[:, c, :], scalar=gains[:, gidx, c:c + 1], in1=rstd[:, :],
                        op0=ALU.mult, op1=ALU.mult),
                        reads=[hT_t[c], gains_t] + rstd_t, writes=[outT_t[c]])
                kb.barrier()

        def ffn_phase(l):
            with ExitStack() as ph:
                hn = sb("hn_f%d" % l, [128, NC, S], BF16, ph)
                hn_t = [kb.t("hn%d" % c) for c in range(NC)]
                cwb = sb("cwb%d" % l, [128, 44, 4], F32, ph)
                cwb_t = kb.t("cwb", dma=True)
                kb.dma("sp", cwb[:], cwb_d[l], cwb_t, writes=[cwb_t])
                win = [sb("win%d_%d" % (l, i), [128, 2, NC, 128], BF16, ph) for i in range(3)]
                win_t = [kb.t("win%d" % i, dma=True) for i in range(3)]
                wob = [sb("wob%d_%d" % (l, i), [128, 11, 128], BF16, ph) for i in range(2)]
                wob_t = [kb.t("wob%d" % i, dma=True) for i in range(2)]
                act = sb("act%d" % l, [128, 11, S], BF16, ph)
                act_t = [kb.t("act%d" % i) for i in range(11)]

                def load_win(j):
                    b = j % 3
                    kb.dma("pool", win[b][:].rearrange("p s c n -> p (s c n)"), win_d[l, j], win_t[b], writes=[win_t[b]])

                load_win(0)
                load_win(1)
                rmsnorm(ph, 3 * l + 1, hn, hn_t)
                A = [[sb("A%d_%d_%d" % (l, st, s_), [128, 1026], F32, ph) for s_ in range(2)] for st in range(2)]
                A_t = [[kb.t("A%d%d" % (st, s_)) for s_ in range(2)] for st in range(2)]
                C = [[sb("C%d_%d_%d" % (l, st, s_), [128, 1024], F32, ph) for s_ in range(2)] for st in range(2)]
                C_t = [[kb.t("C%d%d" % (st, s_)) for s_ in range(2)] for st in range(2)]
                step = 0
                for grp in range(2):
                    for jj in range(11):
                        j = grp * 11 + jj
                        if j + 2 < 22:
                            load_win(j + 2)
                        wb = win[j % 3]
                        wt = win_t[j % 3]
                        for half in range(2):
                            st = step % 2
                            step += 1
                            for s_ in range(2):
                                ft = s_ * 22 + j
                                for tcl in range(2):
                                    bi = 4 * st + 2 * s_ + tcl
                                    t0 = half * 1024 + tcl * 512
                                    fns = [(lambda c=c, bi=bi, s_=s_, t0=t0, wb=wb: nc.tensor.matmul(
                                        banks[bi][:, :], lhsT=wb[:, s_, c, :], rhs=hn[:, c, t0:t0 + 512],
                                        start=(c == 0), stop=(c == NC - 1))) for c in range(NC)]
                                    kb.group("pe", fns, reads=[wt] + hn_t, writes=[bank_t[bi]])
                            for s_ in range(2):
                                ft = s_ * 22 + j
                                a = A[st][s_]
                                at = A_t[st][s_]
                                cc = C[st][s_]
                                ct = C_t[st][s_]
                                if half == 0:
                                    kb.op("pool", lambda a=a: nc.gpsimd.memset(a[:, 0:2], 0.0), writes=[at])
                                else:
                                    ap_ = A[1 - st][s_]
                                    kb.op("pool", lambda a=a, ap_=ap_: nc.gpsimd.tensor_copy(out=a[:, 0:2], in_=ap_[:, 1024:1026]),
                                          reads=[A_t[1 - st][s_]], writes=[at])
                                for tcl in range(2):
                                    bi = 4 * st + 2 * s_ + tcl
                                    kb.op("act", lambda a=a, bi=bi, tcl=tcl: nc.scalar.copy(
                                        out=a[:, 2 + tcl * 512:2 + (tcl + 1) * 512], in_=banks[bi][:, :]),
                                        reads=[bank_t[bi]], writes=[at])
                                    kb.op("act", lambda cc=cc, bi=bi, tcl=tcl, ft=ft: nc.scalar.activation(
                                        out=cc[:, tcl * 512:(tcl + 1) * 512], in_=banks[bi][:, :], func=AF.Identity,
                                        scale=cwb[:, ft, 2:3], bias=cwb[:, ft, 3:4]),
                                        reads=[bank_t[bi], cwb_t], writes=[ct])
                                kb.op("dve", lambda a=a, cc=cc, ft=ft: nc.vector.scalar_tensor_tensor(
                                    out=cc[:, :], in0=a[:, 1:1025], scalar=cwb[:, ft, 1:2], in1=cc[:, :],
                                    op0=ALU.mult, op1=ALU.add), reads=[at, ct, cwb_t], writes=[ct])
                                kb.op("dve", lambda a=a, cc=cc, ft=ft: nc.vector.scalar_tensor_tensor(
                                    out=cc[:, :], in0=a[:, 0:1024], scalar=cwb[:, ft, 0:1], in1=cc[:, :],
                                    op0=ALU.mult, op1=ALU.add), reads=[at, ct, cwb_t], writes=[ct])
                            cg = C[st][0]
                            cu = C[st][1]
                            kb.op("act", lambda cg=cg: nc.scalar.activation(out=cg[:, :], in_=cg[:, :], func=AF.Gelu_apprx_tanh),
                                  reads=[C_t[st][0]], writes=[C_t[st][0]])
                            kb.op("dve", lambda cg=cg, cu=cu, jj=jj, half=half: nc.vector.tensor_tensor(
                                out=act[:, jj, half * 1024:(half + 1) * 1024], in0=cg[:, :], in1=cu[:, :], op=ALU.mult),
                                reads=[C_t[st][0], C_t[st][1]], writes=[act_t[jj]])
                    for dt in range(NC):
                        b = dt % 2
                        kb.dma("pool", wob[b][:].rearrange("p j n -> p (j n)"), wout_d[l, grp, dt], wob_t[b], writes=[wob_t[b]])
                        for tc in range(4):
                            bi = (dt * 4 + tc) % 8
                            fns = [(lambda q=q, bi=bi, b=b, tc=tc, dt=dt: nc.tensor.matmul(
                                banks[bi][:, :], lhsT=wob[b][:, q, :], rhs=act[:, q, tc * 512:(tc + 1) * 512],
                                start=(q == 0), stop=(q == 10))) for q in range(11)]
                            kb.group("pe", fns, reads=[wob_t[b]] + act_t, writes=[bank_t[bi]])
                            kb.op("dve", lambda bi=bi, dt=dt, tc=tc: nc.vector.tensor_tensor(
                                out=hT[:, dt, tc * 512:(tc + 1) * 512], in0=hT[:, dt, tc * 512:(tc + 1) * 512],
                                in1=banks[bi][:, :], op=ALU.add), reads=[bank_t[bi], hT_t[dt]], writes=[hT_t[dt]])
                kb.barrier()
                kb.release_phase_dsems()

        def ple_phase(l):
            with ExitStack() as ph:
                hn = sb("hn_p%d" % l, [128, NC, S], BF16, ph)
                hn_t = [kb.t("hn%d" % c) for c in range(NC)]
                wg = sb("wg%d" % l, [128, NC, NC, 128], BF16, ph)
                wg_t = [kb.t("wg%d" % i, dma=True) for i in range(NC)]
                wp = sb("wp%d" % l, [128, NC, 2, 128], BF16, ph)
                wp_t = kb.t("wp", dma=True)
                pb = sb("pb%d" % l, [128, NT, 256], BF16, ph)
                pb_t = kb.t("pb", dma=True)
                pT = sb("pT%d" % l, [128, 2, S], BF16, ph)
                pT_t = kb.t("pT")
                gs = [sb("gs%d_%d" % (l, i), [128, 512], F32, ph) for i in range(2)]
                gs_t = [kb.t("gs%d" % i) for i in range(2)]
                kb.dma("pool", pb[:], p_d[l].rearrange("(tt p) f -> p tt f", p=128), pb_t, writes=[pb_t])
                kb.dma("pool", wp[:].rearrange("p d c n -> p (d c n)"), wproj_d[l], wp_t, writes=[wp_t])
                for dt in range(NC):
                    kb.dma("pool", wg[:, dt].rearrange("p c n -> p (c n)"), wgate_d[l, dt], wg_t[dt], writes=[wg_t[dt]])
                rmsnorm(ph, 3 * l + 2, hn, hn_t)
                for tt in range(NT):
                    bi = 4 + tt % 4
                    bkb = banks[bi][:, :].bitcast(BF16)
                    fns = [(lambda c2=c2, bkb=bkb, tt=tt: nc.tensor.transpose(
                        out=bkb[:, c2 * 128:(c2 + 1) * 128], in_=pb[:, tt, c2 * 128:(c2 + 1) * 128], identity=ident_b[:]))
                        for c2 in range(2)]
                    kb.group("pe", fns, reads=[pb_t, ident_b_t], writes=[bank_t[bi]])
                    eng = alt(["dve", "act"])
                    dst = pT[:, :, tt * 128:(tt + 1) * 128]
                    src = bkb[:, 0:256].rearrange("p (k n) -> p k n", k=2)
                    if eng == "dve":
                        kb.op("dve", lambda dst=dst, src=src: nc.vector.tensor_copy(out=dst, in_=src),
                              reads=[bank_t[bi]], writes=[pT_t])
                    else:
                        kb.op("act", lambda dst=dst, src=src: nc.scalar.copy(out=dst, in_=src),
                              reads=[bank_t[bi]], writes=[pT_t])
                it = 0
                for dt in range(NC):
                    for tc in range(4):
                        b = it % 2
                        it += 1
                        bg = 2 * b
                        bp = 2 * b + 1
                        sl = slice(tc * 512, (tc + 1) * 512)
                        fns = [(lambda c=c, bg=bg, dt=dt, sl=sl: nc.tensor.matmul(
                            banks[bg][:, :], lhsT=wg[:, dt, c, :], rhs=hn[:, c, sl], start=(c == 0), stop=(c == NC - 1)))
                            for c in range(NC)]
                        kb.group("pe", fns, reads=[wg_t[dt]] + hn_t, writes=[bank_t[bg]])
                        fns = [(lambda c2=c2, bp=bp, dt=dt, sl=sl: nc.tensor.matmul(
                            banks[bp][:, :], lhsT=wp[:, dt, c2, :], rhs=pT[:, c2, sl], start=(c2 == 0), stop=(c2 == 1)))
                            for c2 in range(2)]
                        kb.group("pe", fns, reads=[wp_t, pT_t], writes=[bank_t[bp]])
                        kb.op("act", lambda b=b, bg=bg: nc.scalar.activation(out=gs[b][:, :], in_=banks[bg][:, :], func=AF.Sigmoid),
                              reads=[bank_t[bg]], writes=[gs_t[b]])
                        kb.op("dve", lambda b=b, bp=bp: nc.vector.tensor_tensor(
                            out=gs[b][:, :], in0=gs[b][:, :], in1=banks[bp][:, :], op=ALU.mult),
                            reads=[gs_t[b], bank_t[bp]], writes=[gs_t[b]])
                        kb.op("pool", lambda b=b, dt=dt, sl=sl: nc.gpsimd.tensor_tensor(
                            out=hT[:, dt, sl], in0=hT[:, dt, sl], in1=gs[b][:, :], op=ALU.add),
                            reads=[gs_t[b], hT_t[dt]], writes=[hT_t[dt]])
                kb.barrier()
                kb.release_phase_dsems()

        def attn_group(h, qc, ph_bufs, k_lhsT, q_rhs, v_rhs, mask_fn, y_tm, y_tm_t, kin_t, vin_t):
            PT, PT_t, rden, rden_t, st = ph_bufs
            bo = 4 + (st["g"] % 2)
            st["g"] += 1
            obank = banks[bo]
            kb.group("pe", [lambda: nc.tensor.matmul(obank[:, 0:260], lhsT=zeros_b[:, 0:128], rhs=zeros_b[:, 0:260],
                                                      start=True, stop=False, skip_group_check=True)],
                     reads=[zeros_t], writes=[bank_t[bo]])
            nk = 4 * qc + 4
            for kt in range(nk):
                r = max(0, kt - 4 * qc)
                w = 512 - 128 * r
                c0 = qc * 512 + r * 128
                bs = 6 + (st["s"] % 2)
                pb_ = st["s"] % 3
                st["s"] += 1
                sbank = banks[bs]
                mk = mask_fn(kt, qc, r)
                fns = [lambda kt=kt, c0=c0, w=w, sbank=sbank, mk=mk: nc.tensor.matmul(
                    sbank[:, 0:w], lhsT=k_lhsT(kt), rhs=q_rhs(c0, c0 + w), start=True, stop=(mk is None))]
                rds = list(kin_t)
                if mk is not None:
                    mrhs, mw, mt = mk
                    fns.append(lambda sbank=sbank, mrhs=mrhs, mw=mw: nc.tensor.matmul(
                        sbank[:, 0:mw], lhsT=ident_b[:], rhs=mrhs, start=False, stop=True))
                    rds += [mt, ident_b_t]
                kb.group("pe", fns, reads=rds, writes=[bank_t[bs]])
                kb.op("act", lambda pb_=pb_, w=w, sbank=sbank: nc.scalar.activation(
                    out=PT[pb_][:, 0:w], in_=sbank[:, 0:w], func=AF.Exp, scale=0.125),
                    reads=[bank_t[bs]], writes=[PT_t[pb_]])
                fns = []
                for qs in range(r, 4):
                    last = (kt == nk - 1 and qs == 3)
                    fns.append(lambda qs=qs, r=r, pb_=pb_, kt=kt, last=last: nc.tensor.matmul(
                        obank[:, qs * 65:qs * 65 + 65], lhsT=PT[pb_][:, (qs - r) * 128:(qs - r + 1) * 128],
                        rhs=v_rhs(kt), start=False, stop=last, skip_group_check=True))
                kb.group("pe", fns, reads=[PT_t[pb_]] + list(vin_t), writes=[bank_t[bo]])
            rb = st["g"] % 2
            ov = obank[:, 0:260].rearrange("p (q e) -> p q e", e=65)
            kb.op("dve", lambda rb=rb, ov=ov: nc.vector.reciprocal(out=rden[rb][:, 0:4], in_=ov[:, :, 64]),
                  reads=[bank_t[bo]], writes=[rden_t[rb]])
            for qs in range(4):
                tt = qc * 4 + qs
                kb.op("dve", lambda qs=qs, tt=tt, rb=rb: nc.vector.tensor_scalar(
                    out=y_tm[:, tt, h * 64:(h + 1) * 64], in0=obank[:, qs * 65:qs * 65 + 64],
                    scalar1=rden[rb][:, qs:qs + 1], scalar2=None, op0=ALU.mult),
                    reads=[rden_t[rb]], writes=[y_tm_t, bank_t[bo]])

        def tm_to_fm(y_tm, y_tm_t, ymT, ymT_t, c_base):
            for tt in range(NT):
                bi = tt % 4
                bkb = banks[bi][:, :].bitcast(BF16)
                fns = [(lambda k=k, bkb=bkb, tt=tt: nc.tensor.transpose(
                    out=bkb[:, k * 128:(k + 1) * 128], in_=y_tm[:, tt, k * 128:(k + 1) * 128], identity=ident_b[:]))
                    for k in range(4)]
                kb.group("pe", fns, reads=[y_tm_t, ident_b_t], writes=[bank_t[bi]])
                dst = ymT[:, c_base:c_base + 4, tt * 128:(tt + 1) * 128]
                src = bkb[:, 0:512].rearrange("p (k n) -> p k n", k=4)
                eng = alt(["dve", "act"])
                if eng == "dve":
                    kb.op("dve", lambda dst=dst, src=src: nc.vector.tensor_copy(out=dst, in_=src),
                          reads=[bank_t[bi]], writes=ymT_t[c_base:c_base + 4])
                else:
                    kb.op("act", lambda dst=dst, src=src: nc.scalar.copy(out=dst, in_=src),
                          reads=[bank_t[bi]], writes=ymT_t[c_base:c_base + 4])

        def mixer_out(ph, ymT, ymT_t, wo_dram, nm, ym_fn=None):
            wo = sb("wo_mix" + nm, [128, NC, D], BF16, ph)
            wo_t = kb.t("wo_mix", dma=True)
            kb.dma("pool", wo[:].rearrange("p c n -> p (c n)"), wo_dram, wo_t, writes=[wo_t])
            for dt in range(NC):
                for tc in range(4):
                    bi = (dt * 4 + tc) % 4
                    sl = slice(tc * 512, (tc + 1) * 512)
                    fns = [(lambda c=c, bi=bi, dt=dt, sl=sl: nc.tensor.matmul(
                        banks[bi][:, :], lhsT=wo[:, c, dt * 128:(dt + 1) * 128], rhs=(ym_fn(c, sl) if ym_fn else ymT[:, c, sl]),
                        start=(c == 0), stop=(c == NC - 1))) for c in range(NC)]
                    kb.group("pe", fns, reads=[wo_t] + ymT_t, writes=[bank_t[bi]])
                    kb.op("dve", lambda bi=bi, dt=dt, sl=sl: nc.vector.tensor_tensor(
                        out=hT[:, dt, sl], in0=hT[:, dt, sl], in1=banks[bi][:, :], op=ALU.add),
                        reads=[bank_t[bi], hT_t[dt]], writes=[hT_t[dt]])

        def odd_phase(l):
            with ExitStack() as ph:
                hn = sb("hn_o", [128, NC, S], BF16, ph)
                hn_t = [kb.t("hn%d" % c) for c in range(NC)]
                ymT = sb("ymT_o", [128, NC, S], BF16, ph)
                ymT_t = [kb.t("ymT%d" % c) for c in range(NC)]
                rmsnorm(ph, 3 * l, hn, hn_t)
                with ExitStack() as p2:
                    wu = sb("wu", [128, NC, 512], BF16, p2)
                    wu_t = kb.t("wu", dma=True)
                    wv = sb("wsv", [128, NC, 512], BF16, p2)
                    wv_t = kb.t("wsv", dma=True)
                    kb.dma("pool", wv[:].rearrange("p c n -> p (c n)"), odw_d[4], wv_t, writes=[wv_t])
                    kb.dma("pool", wu[:].rearrange("p c n -> p (c n)"), odw_d[3], wu_t, writes=[wu_t])
                    vn = sb("vn_tm", [128, NT, 512], BF16, p2)
                    vn_t = [kb.t("vn%d" % i) for i in range(NT)]
                    lng = sb("lng", [128, 512], F32, p2)
                    lnb = sb("lnb", [128, 512], F32, p2)
                    ln_t = kb.t("ln", dma=True)
                    kb.dma("sp", lng[:], lng_d[:, :], ln_t, writes=[ln_t])
                    kb.dma("sp", lnb[:], lnb_d[:, :], ln_t, writes=[ln_t])
                    swT = sb("swT", [128, 4, 128], F32, p2)
                    smk = sb("smk", [128, 128], F32, p2)
                    sw_t = kb.t("sw", dma=True)
                    kb.dma("sp", swT[:], sguw_d[:, :, :], sw_t, writes=[sw_t])
                    kb.dma("sp", smk[:], sgumask_d[:, :], sw_t, writes=[sw_t])
                    wm = sb("wm", [128, 4, 128], BF16, p2)
                    wm_t = kb.t("wm")
                    for g in range(4):
                        kb.op("dve", lambda g=g: nc.vector.tensor_tensor(out=wm[:, g, :], in0=swT[:, g, :], in1=smk[:, :], op=ALU.mult),
                              reads=[sw_t], writes=[wm_t])
                    brep = sb("brep", [128, 4, 512], F32, p2)
                    brep_t = kb.t("brep", dma=True)
                    kb.dma("sp", brep[:], sgub_d[:, :, :], brep_t, writes=[brep_t])
                    vg = [sb("vg%d" % i, [128, 512], F32, p2) for i in range(2)]
                    vg_t = [kb.t("vg%d" % i) for i in range(2)]
                    stt = [sb("stt%d" % i, [128, 8], F32, p2) for i in range(2)]
                    stt_t = [kb.t("stt%d" % i) for i in range(2)]
                    for tt in range(NT):
                        b = tt % 2
                        bi = tt % 4
                        fns = [(lambda c=c, bi=bi, tt=tt: nc.tensor.matmul(
                            banks[bi][:, :], lhsT=hn[:, c, tt * 128:(tt + 1) * 128], rhs=wv[:, c, :],
                            start=(c == 0), stop=(c == NC - 1))) for c in range(NC)]
                        kb.group("pe", fns, reads=[wv_t] + hn_t, writes=[bank_t[bi]])
                        kb.op("act", lambda b=b, bi=bi: nc.scalar.activation(out=vg[b][:, :], in_=banks[bi][:, :], func=AF.Gelu_apprx_tanh),
                              reads=[bank_t[bi]], writes=[vg_t[b]])
                        kb.op("dve", lambda b=b: nc.vector.bn_stats(out=stt[b][:, 0:6], in_=vg[b][:, :]),
                              reads=[vg_t[b]], writes=[stt_t[b]])
                        kb.op("dve", lambda b=b: nc.vector.bn_aggr(out=stt[b][:, 6:8], in_=stt[b][:, 0:6]),
                              reads=[stt_t[b]], writes=[stt_t[b]])
                        kb.op("act", lambda b=b: nc.scalar.activation(out=stt[b][:, 7:8], in_=stt[b][:, 7:8], func=AF.Sqrt, bias=eps_c[:]),
                              reads=[stt_t[b], eps_t], writes=[stt_t[b]])
                        kb.op("dve", lambda b=b: nc.vector.reciprocal(out=stt[b][:, 7:8], in_=stt[b][:, 7:8]),
                              reads=[stt_t[b]], writes=[stt_t[b]])
                        kb.op("dve", lambda b=b: nc.vector.tensor_scalar(
                            out=vg[b][:, :], in0=vg[b][:, :], scalar1=stt[b][:, 6:7], scalar2=stt[b][:, 7:8],
                            op0=ALU.subtract, op1=ALU.mult), reads=[vg_t[b], stt_t[b]], writes=[vg_t[b]])
                        kb.op("pool", lambda b=b: nc.gpsimd.tensor_tensor(out=vg[b][:, :], in0=vg[b][:, :], in1=lng[:, :], op=ALU.mult),
                              reads=[vg_t[b], ln_t], writes=[vg_t[b]])
                        kb.op("pool", lambda b=b, tt=tt: nc.gpsimd.tensor_tensor(out=vn[:, tt, :], in0=vg[b][:, :], in1=lnb[:, :], op=ALU.add),
                              reads=[vg_t[b], ln_t], writes=[vn_t[tt]])
                    us = [sb("us%d" % i, [128, 512], F32, p2) for i in range(2)]
                    us_t = [kb.t("us%d" % i) for i in range(2)]
                    ms = [sb("ms%d" % i, [128, 512], F32, p2) for i in range(2)]
                    ms_t = [kb.t("ms%d" % i) for i in range(2)]
                    it = 0
                    for g in range(4):
                        for tc in range(4):
                            b = it % 2
                            it += 1
                            bu = 4 + 2 * b
                            bm = 5 + 2 * b
                            sl = slice(tc * 512, (tc + 1) * 512)
                            fns = [(lambda c=c, bu=bu, g=g, sl=sl: nc.tensor.matmul(
                                banks[bu][:, :], lhsT=wu[:, c, g * 128:(g + 1) * 128], rhs=hn[:, c, sl],
                                start=(c == 0), stop=(c == NC - 1))) for c in range(NC)]
                            kb.group("pe", fns, reads=[wu_t] + hn_t, writes=[bank_t[bu]])
                            fns = [(lambda q=q, bm=bm, g=g, tc=tc: nc.tensor.matmul(
                                banks[bm][:, q * 128:(q + 1) * 128], lhsT=vn[:, tc * 4 + q, g * 128:(g + 1) * 128],
                                rhs=wm[:, g, :], start=True, stop=True)) for q in range(4)]
                            kb.group("pe", fns, reads=[wm_t] + vn_t[tc * 4:tc * 4 + 4], writes=[bank_t[bm]])
                            kb.op("act", lambda b=b, bu=bu: nc.scalar.activation(out=us[b][:, :], in_=banks[bu][:, :], func=AF.Gelu_apprx_tanh),
                                  reads=[bank_t[bu]], writes=[us_t[b]])
                            kb.op("dve", lambda b=b, bm=bm, g=g: nc.vector.tensor_tensor(
                                out=ms[b][:, :], in0=brep[:, g, :], in1=banks[bm][:, :], op=ALU.add),
                                reads=[bank_t[bm], brep_t], writes=[ms_t[b]])
                            kb.op("dve", lambda b=b, g=g, sl=sl: nc.vector.tensor_tensor(
                                out=ymT[:, 4 + g, sl], in0=ms[b][:, :], in1=us[b][:, :], op=ALU.mult),
                                reads=[ms_t[b], us_t[b]], writes=[ymT_t[4 + g]])
                    kb.barrier()
                    kb.release_phase_dsems()
                y_tm = sb("yfox_tm", [128, NT, 512], BF16, ph)
                y_tm_t = kb.t("yfox_tm")
                vaug = sb("vaug", [128, NT, 8, 65], BF16, ph)
                vaug_t = kb.t("vaug")
                Pp = [sb("Pp%d" % i, [8, S], BF16, ph) for i in range(3)]
                Pp_t = kb.t("Pp")
                with ExitStack() as p2:
                    wvf = sb("wvf", [128, NC, 640], BF16, p2)
                    wvf_t = kb.t("wvf", dma=True)
                    kb.dma("pool", wvf[:].rearrange("p c n -> p (c n)"), odw_d2[:, :], wvf_t, writes=[wvf_t])
                    bfc = sb("bfc", [8, 2], F32, p2)
                    bfc_t = kb.t("bfc", dma=True)
                    kb.dma("sp", bfc[:, 0:1], foxbf_d[:, :], bfc_t, writes=[bfc_t])
                    kb.op("dve", lambda: nc.vector.tensor_scalar(out=bfc[:, 1:2], in0=bfc[:, 0:1], scalar1=-1.0, scalar2=None, op0=ALU.mult),
                          reads=[bfc_t], writes=[bfc_t])
                    L8 = sb("L8", [8, S], F32, p2)
                    one8 = sb("one8", [8, S], BF16, p2)
                    one8_t = kb.t("one8")
                    kb.op("pool", lambda: nc.gpsimd.memset(one8[:, :], 1.0), writes=[one8_t])
                    kb.op("pool", lambda: nc.gpsimd.memset(vaug[:].rearrange("p a b c -> p (a b c)"), 1.0), writes=[vaug_t])
                    for tt in range(NT):
                        bi = tt % 4
                        fns = [(lambda c=c, bi=bi, tt=tt: nc.tensor.matmul(
                            banks[bi][:, :], lhsT=hn[:, c, tt * 128:(tt + 1) * 128], rhs=wvf[:, c, 0:512],
                            start=(c == 0), stop=(c == NC - 1))) for c in range(NC)]
                        kb.group("pe", fns, reads=[wvf_t] + hn_t, writes=[bank_t[bi]])
                        src = banks[bi][:, :].rearrange("p (h e) -> p h e", e=64)
                        eng = alt(["dve", "act"])
                        if eng == "dve":
                            kb.op("dve", lambda tt=tt, src=src: nc.vector.tensor_copy(out=vaug[:, tt, :, 0:64], in_=src),
                                  reads=[bank_t[bi]], writes=[vaug_t])
                        else:
                            kb.op("act", lambda tt=tt, src=src: nc.scalar.copy(out=vaug[:, tt, :, 0:64], in_=src),
                                  reads=[bank_t[bi]], writes=[vaug_t])
                    lf = sb("lf", [8, S], F32, p2)
                    lf_t = kb.t("lf")
                    for tc in range(4):
                        bi = 4 + tc
                        sl = slice(tc * 512, (tc + 1) * 512)
                        fns = [(lambda c=c, bi=bi, sl=sl: nc.tensor.matmul(
                            banks[bi][:, :], lhsT=wvf[:, c, 512:640], rhs=hn[:, c, sl],
                            start=(c == 0), stop=(c == NC - 1))) for c in range(NC)]
                        kb.group("pe", fns, reads=[wvf_t] + hn_t, writes=[bank_t[bi]])
                        kb.op("act", lambda bi=bi, sl=sl: nc.scalar.activation(
                            out=lf[:, sl], in_=banks[bi][0:8, :], func=AF.Exp, scale=-1.0, bias=bfc[:, 1:2]),
                            reads=[bank_t[bi], bfc_t], writes=[lf_t])
                    kb.op("act", lambda: nc.scalar.activation(out=lf[:, :], in_=lf[:, :], func=AF.Ln, bias=one_c[0:8, :]),
                          reads=[lf_t, onec_t], writes=[lf_t])
                    kb.op("dve", lambda: nc.vector.tensor_tensor_scan(out=L8[:, :], data0=one8[:, :], data1=lf[:, :], initial=0.0,
                                                                        op0=ALU.mult, op1=ALU.add),
                          reads=[lf_t, one8_t], writes=[Pp_t])
                    kb.op("dve", lambda: nc.vector.tensor_scalar(out=L8[:, :], in0=L8[:, :], scalar1=8.0, scalar2=None, op0=ALU.mult),
                          reads=[Pp_t], writes=[Pp_t])
                    for i in range(3):
                        kb.op("dve", lambda i=i: nc.vector.tensor_copy(out=Pp[i][:, :], in_=L8[:, :]), reads=[Pp_t], writes=[Pp_t])
                        if i < 2:
                            kb.op("dve", lambda i=i: nc.vector.tensor_tensor(out=L8[:, :], in0=L8[:, :], in1=Pp[i][:, :], op=ALU.subtract),
                                  reads=[Pp_t], writes=[Pp_t])
                    kb.barrier()
                    kb.release_phase_dsems()
                with ExitStack() as p2:
                    wqk = [sb("wqk%d" % i, [128, NC, 128], BF16, p2) for i in range(2)]
                    wqk_t = [kb.t("wqk%d" % i, dma=True) for i in range(2)]
                    qa = [sb("qa%d" % i, [128, S], BF16, p2) for i in range(2)]
                    ka = [sb("ka%d" % i, [128, S], BF16, p2) for i in range(2)]
                    qa_t = [kb.t("qa%d" % i, dma=True) for i in range(2)]
                    ka_t = [kb.t("ka%d" % i, dma=True) for i in range(2)]
                    PT = [sb("PT%d" % i, [128, 512], BF16, p2) for i in range(3)]
                    PT_t = [kb.t("PT%d" % i) for i in range(3)]
                    rden = [sb("rden%d" % i, [128, 4], F32, p2) for i in range(2)]
                    rden_t = [kb.t("rden%d" % i) for i in range(2)]
                    cmask = sb("cmask", [128, 128], BF16, p2)
                    cmask_t = kb.t("cmask", dma=True)
                    kb.dma("pool", cmask[:], cmask_d[:, :], cmask_t, writes=[cmask_t])
                    st = {"g": 0, "s": 0}
                    bufs = (PT, PT_t, rden, rden_t, st)
                    for h in (range(8) if "nofox" not in STAGES else []):
                        hb = h % 2
                        kb.dma("pool", wqk[hb][:].rearrange("p c n -> p (c n)"), odqk_d[h], wqk_t[hb], writes=[wqk_t[hb]])
                        for tc in range(4):
                            bi = tc % 4
                            sl = slice(tc * 512, (tc + 1) * 512)
                            fns = [(lambda c=c, bi=bi, sl=sl: nc.tensor.matmul(
                                banks[bi][:, :], lhsT=wqk[hb][:, c, :], rhs=hn[:, c, sl],
                                start=(c == 0), stop=(c == NC - 1))) for c in range(NC)]
                            kb.group("pe", fns, reads=[wqk_t[hb]] + hn_t, writes=[bank_t[bi]])
                            kb.op("dve", lambda bi=bi, sl=sl: nc.vector.tensor_copy(out=qa[hb][0:64, sl], in_=banks[bi][0:64, :]),
                                  reads=[], writes=[qa_t[hb], bank_t[bi]])
                            kb.op("act", lambda bi=bi, sl=sl: nc.scalar.copy(out=ka[hb][0:64, sl], in_=banks[bi][64:128, :]),
                                  reads=[], writes=[ka_t[hb], bank_t[bi]])
                        kb.op("pool", lambda: nc.gpsimd.memset(qa[hb][64:128, :], 0.0), writes=[qa_t[hb]])
                        kb.op("pool", lambda: nc.gpsimd.memset(ka[hb][64:128, :], 0.0), writes=[ka_t[hb]])
                        kb.op("pool", lambda: nc.gpsimd.memset(qa[hb][64:70, :], 1.0), writes=[qa_t[hb]])
                        kb.op("pool", lambda: nc.gpsimd.memset(ka[hb][64:70, :], -1.0), writes=[ka_t[hb]])
                        for i in range(3):
                            kb.dma("sp", qa[hb][64 + i:65 + i, :], Pp[i][h:h + 1, :], qa_t[hb], reads=[Pp_t], writes=[qa_t[hb]])
                            kb.dma("sp", ka[hb][67 + i:68 + i, :], Pp[i][h:h + 1, :], ka_t[hb], reads=[Pp_t], writes=[ka_t[hb]])

                        def mask_fn(kt, qc, r):
                            if kt >= 4 * qc:
                                return (cmask[:, :], 128, cmask_t)
                            return None
                        for qc in range(4):
                            attn_group(h, qc, bufs,
                                       k_lhsT=lambda kt, hb=hb: ka[hb][:, kt * 128:(kt + 1) * 128],
                                       q_rhs=lambda c0, c1, hb=hb: qa[hb][:, c0:c1],
                                       v_rhs=lambda kt, h=h: vaug[:, kt, h, :],
                                       mask_fn=mask_fn, y_tm=y_tm, y_tm_t=y_tm_t,
                                       kin_t=[qa_t[hb], ka_t[hb]], vin_t=[vaug_t])
                    kb.barrier()
                    kb.release_phase_dsems()
                with ExitStack() as p2:
                    if "nofox" in STAGES:
                        kb.op("pool", lambda: nc.gpsimd.memset(y_tm[:].rearrange("p a b -> p (a b)"), 0.0), writes=[y_tm_t])
                    tm_to_fm(y_tm, y_tm_t, ymT, ymT_t, 0)
                    mixer_out(p2, ymT, ymT_t, odwo_d[:, :], "o")
                    kb.barrier()
                    kb.release_phase_dsems()

        def even_phase(l):
            R0 = 8.0
            NIT = 26
            with ExitStack() as ph:
                ypT = sb("ypT", [128, 4, S], BF16, ph)
                ypT_t = [kb.t("ypT%d" % c) for c in range(4)]
                qr = sb("qr", [128, 4, S], BF16, ph)
                qr_t = [kb.t("qr%d" % c) for c in range(4)]
                kAB = sb("kAB", [128, 2, S], BF16, ph)
                kAB_t = [kb.t("kAB%d" % c) for c in range(2)]
                iqr = sb("iqr", [128, 2, S], BF16, ph)
                iqr_t = [kb.t("iqr%d" % c) for c in range(2)]
                ikAB = sb("ikAB", [128, 2, S], BF16, ph)
                ikAB_t = [kb.t("ikAB%d" % c) for c in range(2)]
                vaug = sb("vaug_e", [128, NT, 65], BF16, ph)
                vaug_t = kb.t("vaug_e")
                iwt = sb("iwt", [128, NT, 4], F32, ph)
                iwt_t = kb.t("iwt")
                with ExitStack() as p1:
                    hn = sb("hn_e", [128, NC, S], BF16, p1)
                    hn_t = [kb.t("hn%d" % c) for c in range(NC)]
                    wt = [sb("ewt%d" % i, [128, NC, 128], BF16, p1) for i in range(4)]
                    wt_t = [kb.t("ewt%d" % i, dma=True) for i in range(4)]
                    wst = {"i": 0}

                    def load_w(idx):
                        b = wst["i"] % 4
                        wst["i"] += 1
                        kb.dma("pool", wt[b][:].rearrange("p c n -> p (c n)"), evw_d[idx], wt_t[b], writes=[wt_t[b]])
                        return wt[b], wt_t[b]

                    def proj_fm(w, w_t, bi, tc):
                        sl = slice(tc * 512, (tc + 1) * 512)
                        fns = [(lambda c=c: nc.tensor.matmul(banks[bi][:, :], lhsT=w[:, c, :], rhs=hn[:, c, sl],
                                                             start=(c == 0), stop=(c == NC - 1))) for c in range(NC)]
                        kb.group("pe", fns, reads=[w_t] + hn_t, writes=[bank_t[bi]])

                    rmsnorm(p1, 3 * l, hn, hn_t)
                    wv_, wv_t_ = load_w(24)
                    kb.op("pool", lambda: nc.gpsimd.memset(vaug[:].rearrange("p a b -> p (a b)"), 1.0), writes=[vaug_t])
                    for tt in range(NT):
                        bi = tt % 4
                        fns = [(lambda c=c, bi=bi, tt=tt: nc.tensor.matmul(
                            banks[bi][:, 0:128], lhsT=hn[:, c, tt * 128:(tt + 1) * 128], rhs=wv_[:, c, :],
                            start=(c == 0), stop=(c == NC - 1))) for c in range(NC)]
                        kb.group("pe", fns, reads=[wv_t_] + hn_t, writes=[bank_t[bi]])
                        kb.op("dve", lambda bi=bi, tt=tt: nc.vector.tensor_copy(out=vaug[:, tt, 0:64], in_=banks[bi][:, 0:64]),
                              reads=[], writes=[vaug_t, bank_t[bi]])
                        kb.op("dve", lambda bi=bi, tt=tt: nc.vector.tensor_scalar(
                            out=iwt[:, tt, :], in0=banks[bi][:, 64:68], scalar1=0.0625, scalar2=None, op0=ALU.mult),
                            reads=[], writes=[iwt_t, bank_t[bi]])
                    with ExitStack() as p2:
                        zp = [sb("zp%d" % i, [128, 16 + S], F32, p2) for i in range(2)]
                        zp_t = [kb.t("zp%d" % i) for i in range(2)]
                        z0 = sb("z0", [128, S], F32, p2)
                        z0_t = kb.t("z0")
                        dd = sb("dd", [128, S], BF16, p2)
                        dd_t = kb.t("dd")
                        d16 = sb("d16", [128, 16], F32, p2)
                        d16_t = kb.t("d16")
                        pw = sb("pw", [128, 4, 128], BF16, p2)
                        pw_t = kb.t("pw", dma=True)
                        kb.dma("pool", pw[:], poolw_d[:, :, :], pw_t, writes=[pw_t])
                        pcs = sb("pcs", [128, 4 + 64], F32, p2)
                        pcs_t = kb.t("pcs", dma=True)
                        kb.dma("sp", pcs[:], poolc_d[:, :], pcs_t, writes=[pcs_t])
                        for i in range(2):
                            kb.op("pool", lambda i=i: nc.gpsimd.memset(zp[i][:, 0:16], 0.0), writes=[zp_t[i]])
                        for g in range(4):
                            w_, w_t_ = load_w(g)
                            for tc in range(4):
                                bi = 4 + tc
                                proj_fm(w_, w_t_, bi, tc)
                                kb.op("act", lambda bi=bi, tc=tc: nc.scalar.copy(out=z0[:, tc * 512:(tc + 1) * 512], in_=banks[bi][:, :]),
                                      reads=[], writes=[z0_t, bank_t[bi]])
                            cur = None
                            nsteps = g + 1
                            for sidx in range(nsteps):
                                sh = 1 << sidx
                                o = zp[sidx % 2]
                                ot = zp_t[sidx % 2]
                                if sidx == 0:
                                    kb.op("pool", lambda: nc.gpsimd.tensor_copy(out=zp[1][:, 16:16 + S], in_=z0[:, :]),
                                          reads=[z0_t], writes=[zp_t[1]])
                                    src, srct = zp[1], zp_t[1]
                                else:
                                    src, srct = zp[(sidx + 1) % 2], zp_t[(sidx + 1) % 2]
                                eng = "pool" if sidx % 2 == 0 else "dve"
                                kb.op(eng, lambda o=o, src=src, sh=sh, eng=eng: E[eng].tensor_tensor(
                                    out=o[:, 16:16 + S], in0=src[:, 16:16 + S], in1=src[:, 16 - sh:16 - sh + S], op=ALU.add),
                                    reads=[srct], writes=[ot])
                                cur, curt = o, ot
                            wwin = float(2 << g)
                            kb.op("dve", lambda cur=cur, wwin=wwin: nc.vector.scalar_tensor_tensor(
                                out=dd[:, :], in0=cur[:, 16:16 + S], scalar=1.0 / wwin, in1=z0[:, :], op0=ALU.mult, op1=ALU.subtract),
                                reads=[curt, z0_t], writes=[dd_t])
                            kb.op("dve", lambda cur=cur, g=g: nc.vector.tensor_tensor(
                                out=d16[:, :], in0=cur[:, 16:32], in1=pcs[:, 4 + g * 16:4 + (g + 1) * 16], op=ALU.mult),
                                reads=[curt, pcs_t], writes=[d16_t])
                            kb.op("dve", lambda: nc.vector.tensor_tensor(out=dd[:, 0:16], in0=d16[:, :], in1=z0[:, 0:16], op=ALU.subtract),
                                  reads=[d16_t, z0_t, dd_t], writes=[dd_t])
                            for tc in range(4):
                                bi = tc
                                sl = slice(tc * 512, (tc + 1) * 512)
                                kb.group("pe", [lambda bi=bi, g=g, sl=sl: nc.tensor.matmul(
                                    banks[bi][:, :], lhsT=pw[:, g, :], rhs=dd[:, sl], start=True, stop=True)],
                                    reads=[pw_t, dd_t], writes=[bank_t[bi]])
                                kb.op("act", lambda bi=bi, g=g, sl=sl: nc.scalar.activation(
                                    out=ypT[:, g, sl], in_=banks[bi][:, :], func=AF.Identity, scale=pcs[:, g:g + 1]),
                                    reads=[pcs_t], writes=[ypT_t[g], bank_t[bi]])
                        kb.barrier()
                    with ExitStack() as p2:
                        if "e_stop1" not in STAGES:
                          cosT = sb("cosT", [128, S], F32, p2)
                          sinT = sb("sinT", [128, S], F32, p2)
                          tab_t = kb.t("ropetab")
                          posi = sb("posi", [128, S], I32, p2)
                          posi_t = kb.t("posi", dma=True)
                          kb.dma("sp", posi[:], posr_d[:, :], posi_t, writes=[posi_t])
                          rc = sb("ropec", [128, 2], F32, p2)
                          rc_t = kb.t("ropec", dma=True)
                          kb.dma("sp", rc[:], ropec_d[:, :], rc_t, writes=[rc_t])
                          ang = sb("ang", [128, S], F32, p2)
                          ang_t = kb.t("ang")
                          t1 = [sb("rt1_%d" % i, [128, 512], F32, p2) for i in range(2)]
                          t1_t = [kb.t("rt1_%d" % i) for i in range(2)]
                          t2 = [sb("rt2_%d" % i, [128, 512], F32, p2) for i in range(2)]
                          t2_t = [kb.t("rt2_%d" % i) for i in range(2)]
                          ki = posi
                          TWO_PI = 6.283185307179586
                          PI = 3.141592653589793
                          kb.op("dve", lambda: nc.vector.tensor_copy(out=ang[:, :], in_=posi[:, :]), reads=[posi_t], writes=[ang_t])
                          kb.op("dve", lambda: nc.vector.tensor_scalar(out=ang[:, :], in0=ang[:, :], scalar1=rc[:, 0:1], scalar2=None, op0=ALU.mult),
                                reads=[ang_t, rc_t], writes=[ang_t])
                          for (tab, phi) in ((sinT, 0.0), (cosT, PI / 2)):
                              kb.op("dve", lambda tab=tab, phi=phi: nc.vector.tensor_scalar(
                                  out=tab[:, :], in0=ang[:, :], scalar1=phi + PI, scalar2=1.0 / TWO_PI, op0=ALU.add, op1=ALU.mult),
                                  reads=[ang_t], writes=[tab_t])
                              kb.op("dve", lambda tab=tab: nc.vector.tensor_copy(out=ki[:, :], in_=tab[:, :]), reads=[tab_t], writes=[posi_t])
                              kb.op("dve", lambda tab=tab: nc.vector.tensor_copy(out=tab[:, :], in_=ki[:, :]), reads=[posi_t], writes=[tab_t])
                              kb.op("dve", lambda tab=tab: nc.vector.scalar_tensor_tensor(
                                  out=tab[:, :], in0=tab[:, :], scalar=-TWO_PI, in1=ang[:, :], op0=ALU.mult, op1=ALU.add),
                                  reads=[tab_t, ang_t], writes=[tab_t])
                              if phi != 0.0:
                                  kb.op("dve", lambda tab=tab, phi=phi: nc.vector.tensor_scalar(
                                      out=tab[:, :], in0=tab[:, :], scalar1=phi, scalar2=None, op0=ALU.add), reads=[tab_t], writes=[tab_t])
                              for half in range(4):
                                  sl = slice(half * 512, (half + 1) * 512)
                                  for (cmp_op, thr, corr) in ((ALU.is_gt, PI, -TWO_PI), (ALU.is_lt, -PI, TWO_PI)):
                                      kb.op("dve", lambda cmp_op=cmp_op, thr=thr, corr=corr, tab=tab, sl=sl: nc.vector.tensor_scalar(
                                          out=t1[0][:, :], in0=tab[:, sl], scalar1=thr, scalar2=corr, op0=cmp_op, op1=ALU.mult),
                                          reads=[tab_t], writes=[t1_t[0]])
                                      kb.op("dve", lambda tab=tab, sl=sl: nc.vector.tensor_tensor(
                                          out=tab[:, sl], in0=tab[:, sl], in1=t1[0][:, :], op=ALU.add),
                                          reads=[tab_t, t1_t[0]], writes=[tab_t])
                              kb.op("dve", lambda tab=tab: nc.vector.tensor_scalar(
                                  out=tab[:, :], in0=tab[:, :], scalar1=-3.14159, scalar2=3.14159, op0=ALU.max, op1=ALU.min),
                                  reads=[tab_t], writes=[tab_t])
                              kb.op("act", lambda tab=tab: nc.scalar.activation(out=tab[:, :], in_=tab[:, :], func=AF.Sin),
                                    reads=[tab_t], writes=[tab_t])
                          kb.op("dve", lambda: nc.vector.tensor_scalar(out=sinT[:, :], in0=sinT[:, :], scalar1=rc[:, 1:2], scalar2=None, op0=ALU.mult),
                                reads=[tab_t, rc_t], writes=[tab_t])
                          jobs = []
                          for j in range(4):
                              jobs.append((qr, j, qr_t[j], 4 + j, 8 + j))
                          jobs.append((kAB, 0, kAB_t[0], 12, 14))
                          jobs.append((kAB, 1, kAB_t[1], 13, 15))
                          jobs.append((iqr, 0, iqr_t[0], 16, 18))
                          jobs.append((iqr, 1, iqr_t[1], 17, 19))
                          jobs.append((ikAB, 0, ikAB_t[0], 20, 22))
                          jobs.append((ikAB, 1, ikAB_t[1], 21, 23))
                          it = 0
                          for (dst, dj, dst_t, wi, wpi) in jobs:
                              w_, w_t_ = load_w(wi)
                              wp_, wp_t_ = load_w(wpi)
                              for tc in range(4):
                                  b = it % 2
                                  it += 1
                                  bx = 2 * b
                                  bp = 2 * b + 1
                                  sl = slice(tc * 512, (tc + 1) * 512)
                                  proj_fm(w_, w_t_, bx, tc)
                                  proj_fm(wp_, wp_t_, bp, tc)
                                  kb.op("dve", lambda b=b, bx=bx, sl=sl: nc.vector.tensor_tensor(
                                      out=t1[b][:, :], in0=banks[bx][:, :], in1=cosT[:, sl], op=ALU.mult),
                                      reads=[tab_t], writes=[t1_t[b], bank_t[bx]])
                                  kb.op("dve", lambda b=b, bp=bp, sl=sl: nc.vector.tensor_tensor(
                                      out=t2[b][:, :], in0=banks[bp][:, :], in1=sinT[:, sl], op=ALU.mult),
                                      reads=[tab_t], writes=[t2_t[b], bank_t[bp]])
                                  kb.op("pool", lambda b=b, dst=dst, dj=dj, sl=sl: nc.gpsimd.tensor_tensor(
                                      out=dst[:, dj, sl], in0=t1[b][:, :], in1=t2[b][:, :], op=ALU.add),
                                      reads=[t1_t[b], t2_t[b]], writes=[dst_t])
                        kb.barrier()
                    kb.release_phase_dsems()
                y_tm = sb("ydsa_tm", [128, NT, 512], BF16, ph)
                y_tm_t = kb.t("ydsa_tm")
                with ExitStack() as p1:
                    sc = [sb("sc%d" % i, [128, S], F32, p1) for i in range(2)]
                    sc_t = [kb.t("sc%d" % i) for i in range(2)]
                    rtmp = sb("rtmp", [128, S], F32, p1)
                    rtmp_t = kb.t("rtmp")
                    junk = sb("junk", [128, S], BF16, p1)
                    mb = [sb("mb%d" % i, [128, S], BF16, p1) for i in range(2)]
                    mb_t = [kb.t("mb%d" % i) for i in range(2)]
                    pert = sb("pert", [128, S], F32, p1)
                    pert_t = kb.t("pert", dma=True)
                    kb.dma("sp", pert[:], pert_d[:, :], pert_t, writes=[pert_t])
                    MBT = sb("MBT", [128, NT, 512], BF16, p1)
                    MBT_t = kb.t("MBT")
                    PT = [sb("PTe%d" % i, [128, 512], BF16, p1) for i in range(3)]
                    PT_t = [kb.t("PTe%d" % i) for i in range(3)]
                    rden = [sb("rdene%d" % i, [128, 4], F32, p1) for i in range(2)]
                    rden_t = [kb.t("rdene%d" % i) for i in range(2)]
                    bs_ = [sb("bis%d" % i, [128, 8], F32, p1) for i in range(2)]
                    bs_t = [kb.t("bis%d" % i) for i in range(2)]
                    ncol = sb("ncol", [128, NT], F32, p1)
                    ncol_t = kb.t("ncol")
                    for qt in range(NT):
                        kb.op("pool", lambda qt=qt: nc.gpsimd.memset(ncol[:, qt:qt + 1], float(128 * (qt + 1) - 513)), writes=[ncol_t])
                    st = {"g": 0, "s": 0}
                    bufs = (PT, PT_t, rden, rden_t, st)
                    for qc in (range(4) if ("e_stop1" not in STAGES and "e_stop2" not in STAGES) else []):
                        for qs in range(4):
                            qt = 4 * qc + qs
                            n = 128 * (qt + 1)
                            b = qt % 2
                            s_ = sc[b]
                            s_t = sc_t[b]
                            nch = (n + 511) // 512
                            for h4 in range(4):
                                par = h4 % 2
                                for ch in range(nch):
                                    c0 = ch * 512
                                    w = min(512, n - c0)
                                    kb.group("pe", [lambda ch=ch, c0=c0, w=w, h4=h4, par=par, qt=qt: nc.tensor.matmul(
                                        banks[ch][:, 0:w], lhsT=iqr[:, h4 // 2, qt * 128:(qt + 1) * 128], rhs=ikAB[:, par, c0:c0 + w],
                                        start=True, stop=True)],
                                        reads=[iqr_t[h4 // 2], ikAB_t[par]], writes=[bank_t[ch]])
                                    kb.op("act", lambda ch=ch, c0=c0, w=w: nc.scalar.activation(
                                        out=rtmp[:, c0:c0 + w], in_=banks[ch][:, 0:w], func=AF.Relu),
                                        reads=[], writes=[rtmp_t, bank_t[ch]])
                                in1 = pert if h4 == 0 else s_
                                in1_t = pert_t if h4 == 0 else s_t
                                kb.op("dve", lambda s_=s_, in1=in1, h4=h4, qt=qt, n=n: nc.vector.scalar_tensor_tensor(
                                    out=s_[:, 0:n], in0=rtmp[:, 0:n], scalar=iwt[:, qt, h4:h4 + 1], in1=in1[:, 0:n],
                                    op0=ALU.mult, op1=ALU.add), reads=[rtmp_t, iwt_t, in1_t], writes=[s_t])
                            kb.op("dve", lambda s_=s_, n=n: nc.vector.memset(s_[0:64, n - 64:n], -1e30), reads=[s_t], writes=[s_t])
                            bt = bs_[b]
                            btt = bs_t[b]
                            use_act = (qt % 2 == 1) and ("noacttopk" not in STAGES)
                            if qt < 2:
                                kb.op("dve", lambda bt=bt: nc.vector.memset(bt[:, 3:4], -1e29), writes=[btt])
                            elif not use_act:
                                kb.op("dve", lambda bt=bt: nc.vector.memset(bt[:, 0:1], 0.0), writes=[btt])
                                for i in range(NIT):
                                    dl = R0 / (2 ** (i + 1))
                                    kb.op("dve", lambda s_=s_, n=n, bt=bt: nc.vector.tensor_scalar(
                                        out=junk[:, 0:n], in0=s_[:, 0:n], scalar1=bt[:, 0:1], scalar2=None, op0=ALU.is_gt, op1=ALU.add,
                                        accum_out=bt[:, 1:2]), reads=[s_t, btt], writes=[btt])
                                    kb.op("dve", lambda bt=bt, dl=dl: nc.vector.tensor_scalar(
                                        out=bt[:, 2:3], in0=bt[:, 1:2], scalar1=256.5, scalar2=2.0 * dl, op0=ALU.is_ge, op1=ALU.mult),
                                        reads=[btt], writes=[btt])
                                    kb.op("dve", lambda bt=bt, dl=dl: nc.vector.scalar_tensor_tensor(
                                        out=bt[:, 0:1], in0=bt[:, 2:3], scalar=-dl, in1=bt[:, 0:1], op0=ALU.add, op1=ALU.add),
                                        reads=[btt], writes=[btt])
                                kb.op("dve", lambda bt=bt: nc.vector.tensor_scalar(
                                    out=bt[:, 3:4], in0=bt[:, 0:1], scalar1=R0 / (2 ** NIT), scalar2=None, op0=ALU.add),
                                    reads=[btt], writes=[btt])
                            else:
                                kb.op("act", lambda bt=bt: nc.scalar.activation(out=bt[:, 0:1], in_=zero_c[:, 0:1], func=AF.Identity),
                                      reads=[zeroc_t], writes=[btt])
                                for i in range(NIT):
                                    dl = R0 / (2 ** (i + 1))
                                    kb.op("act", lambda s_=s_, n=n, bt=bt: nc.scalar.activation(
                                        out=junk[:, 0:n], in_=s_[:, 0:n], func=AF.Sign, bias=bt[:, 0:1], accum_out=bt[:, 1:2]),
                                        reads=[s_t, btt], writes=[btt])
                                    kb.op("act", lambda bt=bt, qt=qt: nc.scalar.activation(
                                        out=bt[:, 2:3], in_=bt[:, 1:2], func=AF.Sign, bias=ncol[:, qt:qt + 1]),
                                        reads=[btt, ncol_t], writes=[btt])
                                    kb.op("act", lambda bt=bt, dl=dl: nc.scalar.activation(
                                        out=bt[:, 0:1], in_=bt[:, 2:3], func=AF.Identity, scale=-dl, bias=bt[:, 0:1]),
                                        reads=[btt], writes=[btt])
                                kb.op("act", lambda bt=bt: nc.scalar.activation(
                                    out=bt[:, 3:4], in_=bt[:, 0:1], func=AF.Identity, scale=-1.0, bias=tauc[:, 0:1]),
                                    reads=[btt, tauc_t], writes=[btt])
                            m_ = mb[b]
                            m_t = mb_t[b]
                            kb.op("dve", lambda m_=m_, s_=s_, n=n, bt=bt: nc.vector.tensor_scalar(
                                out=m_[:, 0:n], in0=s_[:, 0:n], scalar1=bt[:, 3:4], scalar2=-30000.0, op0=ALU.is_le, op1=ALU.mult),
                                reads=[s_t, btt], writes=[m_t])
                            for k0 in range(0, qt + 1, 4):
                                kn = min(4, qt + 1 - k0)
                                bi = (k0 // 4) % 4
                                bkb = banks[bi][:, :].bitcast(BF16)
                                fns = [(lambda k=k, bkb=bkb, m_=m_, k0=k0: nc.tensor.transpose(
                                    out=bkb[:, k * 128:(k + 1) * 128], in_=m_[:, (k0 + k) * 128:(k0 + k + 1) * 128], identity=ident_b[:]))
                                    for k in range(kn)]
                                kb.group("pe", fns, reads=[m_t, ident_b_t], writes=[bank_t[bi]])
                                dst = MBT[:, k0:k0 + kn, qs * 128:(qs + 1) * 128]
                                src = bkb[:, 0:kn * 128].rearrange("p (k n) -> p k n", k=kn)
                                kb.op("dve", lambda dst=dst, src=src: nc.vector.tensor_copy(out=dst, in_=src),
                                      reads=[], writes=[MBT_t, bank_t[bi]])

                        def mask_fn(kt, qc_, r):
                            return (MBT[:, kt, r * 128:512], 512 - 128 * r, MBT_t)
                        for h in (range(8) if "e_stop3" not in STAGES else []):
                            attn_group(h, qc, bufs,
                                       k_lhsT=lambda kt, h=h: kAB[:, h % 2, kt * 128:(kt + 1) * 128],
                                       q_rhs=lambda c0, c1, h=h: qr[:, h // 2, c0:c1],
                                       v_rhs=lambda kt: vaug[:, kt, :],
                                       mask_fn=mask_fn, y_tm=y_tm, y_tm_t=y_tm_t,
                                       kin_t=[qr_t[h // 2], kAB_t[h % 2]], vin_t=[vaug_t])
                    kb.barrier()
                    kb.release_phase_dsems()
                with ExitStack() as p1:
                    ydT = sb("ydT", [128, 4, S], BF16, p1)
                    ydT_t = [kb.t("ydT%d" % c) for c in range(4)]
                    tm_to_fm(y_tm, y_tm_t, ydT, ydT_t, 0)
                    mixer_out(p1, None, ypT_t + ydT_t, evwo_d[:, :], "e",
                              ym_fn=lambda c, sl: (ypT[:, c, sl] if c < 4 else ydT[:, c - 4, sl]))
                    kb.barrier()
                    kb.release_phase_dsems()

        for l in range(2):
            if "mix%d" % l in STAGES:
                if l == 1:
                    odd_phase(l)
                else:
                    even_phase(l)
            if "ffn%d" % l in STAGES:
                ffn_phase(l)
            if "ple%d" % l in STAGES:
                ple_phase(l)

        with ExitStack() as ph:
            rmsnorm(ph, 6, hT, hT_t)
            ost = [sb("ost%d" % i, [128, D], F32, ph) for i in range(2)]
            ost_t = [kb.t("ost%d" % i, dma=True) for i in range(2)]
            for tt in range(NT):
                b = tt % 2
                for half in range(2):
                    bi = 4 + (2 * tt + half) % 4
                    bk = banks[bi]
                    fns = []
                    for k in range(4):
                        c = half * 4 + k
                        fns.append(lambda k=k, c=c, bk=bk: nc.tensor.transpose(
                            out=bk[:, k * 128:(k + 1) * 128], in_=hT[:, c, tt * 128:(tt + 1) * 128], identity=ident_f[:]))
                    kb.group("pe", fns, reads=hT_t[half * 4:(half + 1) * 4] + [ident_f_t], writes=[bank_t[bi]])
                    eng = alt(["dve", "act"])
                    dst = ost[b][:, half * 512:(half + 1) * 512]
                    if eng == "dve":
                        kb.op("dve", lambda dst=dst, bk=bk: nc.vector.tensor_copy(out=dst, in_=bk[:, :]),
                              reads=[bank_t[bi]], writes=[ost_t[b]])
                    else:
                        kb.op("act", lambda dst=dst, bk=bk: nc.scalar.copy(out=dst, in_=bk[:, :]),
                              reads=[bank_t[bi]], writes=[ost_t[b]])
                kb.dma("sp", out_d[tt * 128:(tt + 1) * 128, :], ost[b][:], ost_t[b], reads=[ost_t[b]])
            kb.barrier()
    return nc


def prep_inputs(inputs):
    f = lambda a: np.ascontiguousarray(np.asarray(a, dtype=np.float32))
    x = f(inputs["x"])
    p = f(inputs["p"])
    pos = np.ascontiguousarray(np.asarray(inputs["positions"], dtype=np.int32))
    gl = []
    for l in range(2):
        for nm in ("norm_mix", "norm_ffn", "norm_ple"):
            gl.append(f(inputs[nm])[l])
    gl.append(f(inputs["norm_final"]))
    gains = np.ascontiguousarray(np.stack([g.reshape(8, 128).T for g in gl], axis=1))
    shared = {"gains": gains, "ident": np.eye(128, dtype=np.float32)}
    ow = f(inputs["od_w_in"])[0]

    def kcn(w):
        n = w.shape[1]
        return np.ascontiguousarray(w.reshape(8, 128, n).transpose(1, 0, 2)).reshape(128, 8 * n)
    shared["odw"] = np.stack([kcn(ow[:, 0:512]), kcn(ow[:, 512:1024]), kcn(ow[:, 1024:1536]),
                              kcn(ow[:, 1544:2056]), kcn(ow[:, 2056:2568])], axis=0)
    shared["odw2"] = kcn(np.concatenate([ow[:, 1024:1536], ow[:, 1536:1544], np.zeros((1024, 120), np.float32)], axis=1))
    shared["odwo"] = kcn(f(inputs["od_w_out"])[0])
    shared["odqk"] = np.stack([kcn(np.concatenate([ow[:, h * 64:(h + 1) * 64], ow[:, 512 + h * 64:512 + (h + 1) * 64]], axis=1))
                               for h in range(8)], axis=0)
    shared["lng"] = np.ascontiguousarray(np.broadcast_to(f(inputs["sgu_ln_g"])[0][None, :], (128, 512)))
    shared["lnb"] = np.ascontiguousarray(np.broadcast_to(f(inputs["sgu_ln_b"])[0][None, :], (128, 512)))
    shared["sguw"] = np.ascontiguousarray(f(inputs["sgu_w"])[0].transpose(2, 0, 1))
    ci = np.arange(128) // 64
    shared["sgumask"] = np.ascontiguousarray((ci[:, None] <= ci[None, :]).astype(np.float32))
    sbb = f(inputs["sgu_b"])[0]
    shared["sgub"] = np.ascontiguousarray(np.broadcast_to(np.tile(sbb, (1, 4))[None, :, :], (128, 4, 512)))
    shared["foxbf"] = np.ascontiguousarray(f(inputs["fox_b_f"])[0].reshape(8, 1))
    ar = np.arange(128)
    shared["cmask"] = np.where(ar[:, None] <= ar[None, :], 0.0, -30000.0).astype(np.float32)
    ew = f(inputs["ev_w_in"])[0]

    def permh(w):
        nh = w.shape[1] // 64
        idx = []
        for h_ in range(nh):
            for d_ in range(64):
                pd = d_ + 8 if d_ < 8 else (d_ - 8 if d_ < 16 else d_)
                idx.append(h_ * 64 + pd)
        return w[:, idx]
    z64 = np.zeros((1024, 64), np.float32)
    qW = ew[:, 512:1024]
    kW = ew[:, 1024:1088]
    iqW = ew[:, 1152:1408]
    ikW = ew[:, 1408:1472]
    tl = [ew[:, g_ * 128:(g_ + 1) * 128] for g_ in range(4)]
    tl += [qW[:, j_ * 128:(j_ + 1) * 128] for j_ in range(4)]
    tl += [permh(qW)[:, j_ * 128:(j_ + 1) * 128] for j_ in range(4)]
    tl += [np.concatenate([kW, z64], 1), np.concatenate([z64, kW], 1),
           np.concatenate([permh(kW), z64], 1), np.concatenate([z64, permh(kW)], 1)]
    tl += [iqW[:, j_ * 128:(j_ + 1) * 128] for j_ in range(2)]
    tl += [permh(iqW)[:, j_ * 128:(j_ + 1) * 128] for j_ in range(2)]
    tl += [np.concatenate([ikW, z64], 1), np.concatenate([z64, ikW], 1),
           np.concatenate([permh(ikW), z64], 1), np.concatenate([z64, permh(ikW)], 1)]
    tl += [np.concatenate([ew[:, 1088:1152], ew[:, 1472:1476], np.zeros((1024, 60), np.float32)], 1)]
    shared["evw"] = np.stack([kcn(t_) for t_ in tl], axis=0)
    shared["evwo"] = kcn(f(inputs["ev_w_out"])[0])
    shared["poolw"] = np.ascontiguousarray(f(inputs["pool_w"])[0].transpose(1, 0, 2))
    pc = np.zeros((128, 68), np.float32)
    pc[:, 0:4] = f(inputs["pool_scale"])[0].reshape(4, 128).T
    for g_, w_ in enumerate((2, 4, 8, 16)):
        pc[:, 4 + g_ * 16:4 + (g_ + 1) * 16] = (1.0 / np.minimum(np.arange(16) + 1, w_)).astype(np.float32)[None, :]
    shared["poolc"] = pc
    inv = (np.float32(500000.0) ** (-np.arange(8, dtype=np.float32) / np.float32(8))).astype(np.float32)
    rc_ = np.zeros((128, 2), np.float32)
    for p_ in range(128):
        d_ = p_ % 64
        if d_ < 16:
            rc_[p_, 0] = inv[d_ % 8]
            rc_[p_, 1] = -1.0 if d_ < 8 else 1.0
    shared["ropec"] = rc_
    shared["pert"] = np.ascontiguousarray(np.broadcast_to((-(2.0 ** -22) * np.arange(S, dtype=np.float64)).astype(np.float32)[None, :], (128, S)))
    cw = f(inputs["ffn_conv_w"])
    cb = f(inputs["ffn_conv_b"])
    cwb = np.concatenate([cw.reshape(2, 3, 44, 128).transpose(0, 3, 2, 1), cb.reshape(2, 44, 128).transpose(0, 2, 1)[..., None]], axis=-1)
    shared["cwb"] = np.ascontiguousarray(cwb)
    wi = f(inputs["ffn_w_in"]).reshape(2, 8, 128, 2, 22, 128)
    shared["win"] = np.ascontiguousarray(wi.transpose(0, 4, 2, 3, 1, 5)).reshape(2, 22, 128, 2 * 8 * 128)
    wo = f(inputs["ffn_w_out"]).reshape(2, 2, 11, 128, 8, 128)
    shared["wout"] = np.ascontiguousarray(wo.transpose(0, 1, 4, 3, 2, 5)).reshape(2, 2, 8, 128, 11 * 128)
    wgt = f(inputs["ple_w_gate"]).reshape(2, 8, 128, 8, 128)
    shared["wgate"] = np.ascontiguousarray(wgt.transpose(0, 3, 2, 1, 4)).reshape(2, 8, 128, 8 * 128)
    wpj = f(inputs["ple_w_proj"]).reshape(2, 2, 128, 8, 128)
    shared["wproj"] = np.ascontiguousarray(wpj.transpose(0, 2, 3, 1, 4)).reshape(2, 128, 8 * 2 * 128)
    in_maps = []
    for b in range(8):
        m = dict(shared)
        m["x"] = np.ascontiguousarray(x[b])
        m["p"] = np.ascontiguousarray(p[:, b])
        m["pos"] = np.ascontiguousarray(pos[b:b + 1])
        m["posr"] = np.ascontiguousarray(np.broadcast_to(pos[b:b + 1], (128, S)))
        in_maps.append(m)
    return in_maps


_CACHE = {}


def kernel(**inputs):
    in_maps = prep_inputs(inputs)
    if "nc" not in _CACHE:
        _CACHE["nc"] = build_program()
    nc = _CACHE["nc"]
    ncores = int(_CACHE.get("dev_cores", 8))
    res = run_bass_kernel_spmd(nc, in_maps[:ncores], core_ids=list(range(ncores)))
    _CACHE["last_res"] = res
    out = np.stack([np.asarray(r["out"], dtype=np.float32) for r in res.results], axis=0)
    return out
```

```python
import numpy as np
from contextlib import ExitStack
import concourse.bass as bass
import concourse.mybir as mybir
from concourse.bass_utils import run_bass_kernel_spmd

F32 = mybir.dt.float32
BF16 = mybir.dt.bfloat16
I32 = mybir.dt.int32
AF = mybir.ActivationFunctionType
ALU = mybir.AluOpType

S = 2048
D = 1024
NT = S // 128
NC = D // 128
EPS = 1e-6
STAGES = {"mix0", "ffn0", "ple0", "mix1", "ffn1", "ple1"}
DEBUG_OUT = None


class T:
    def __init__(self, name):
        self.name = name
        self.last_w = None
        self.readers = {}
        self.dsem = None


class KB:
    def __init__(self, nc, es):
        self.nc = nc
        self.es = es
        self.E = {"pe": nc.tensor, "act": nc.scalar, "dve": nc.vector, "pool": nc.gpsimd, "sp": nc.sync}
        self.sem = {}
        for e in ["pe", "act", "dve", "pool"]:
            self.sem[e] = es.enter_context(nc.semaphore("s_" + e))
        self.cnt = {e: 0 for e in self.sem}
        self.waited = {e: {} for e in self.E}
        self.semobj = {}
        self.dpool = []
        for i in range(90):
            s = es.enter_context(nc.semaphore("d%d" % i))
            self.dpool.append(s)
            self.semobj[s.num] = s
        for e in self.sem:
            self.semobj[self.sem[e].num] = self.sem[e]
        self.dcount = {s.num: 0 for s in self.dpool}
        self.dfree = list(self.dpool)
        self.phase_dsems = []

    def t(self, name, dma=False):
        o = T(name)
        if dma:
            o.dsem = self.dfree.pop()
            self.phase_dsems.append(o.dsem)
        return o

    def release_phase_dsems(self):
        for s in self.phase_dsems:
            self.dfree.append(s)
        self.phase_dsems = []

    def _wait(self, eng, ev):
        semnum, val = ev
        w = self.waited[eng]
        if w.get(semnum, 0) >= val:
            return
        self.E[eng].wait_ge(self.semobj[semnum], val)
        w[semnum] = val

    def _deps(self, eng, reads, writes):
        mysem = self.sem[eng].num if eng in self.sem else None
        for t in reads:
            if t.last_w is not None:
                if t.last_w[0] == mysem and eng == "pe":
                    continue
                self._wait(eng, t.last_w)
        for t in writes:
            if t.last_w is not None and t.last_w[0] != mysem:
                self._wait(eng, t.last_w)
            for sn, v in t.readers.items():
                if sn != mysem:
                    self._wait(eng, (sn, v))

    def _record(self, ev, reads, writes):
        for t in reads:
            if t.readers.get(ev[0], 0) < ev[1]:
                t.readers[ev[0]] = ev[1]
        for t in writes:
            t.last_w = ev
            t.readers = {}

    def op(self, eng, fn, reads=(), writes=()):
        self._deps(eng, reads, writes)
        ins = fn()
        self.cnt[eng] += 1
        ins.then_inc(self.sem[eng], 1)
        ev = (self.sem[eng].num, self.cnt[eng])
        self._record(ev, reads, writes)
        return ev

    def group(self, eng, fns, reads=(), writes=()):
        self._deps(eng, reads, writes)
        ins = None
        for fn in fns:
            ins = fn()
        self.cnt[eng] += 1
        ins.then_inc(self.sem[eng], 1)
        ev = (self.sem[eng].num, self.cnt[eng])
        self._record(ev, reads, writes)
        return ev

    def dma(self, q, out, in_, obj, reads=(), writes=()):
        self._deps(q, reads, writes)
        self.dcount[obj.dsem.num] += 1
        self.E[q].dma_start(out=out, in_=in_).then_inc(obj.dsem, 16)
        ev = (obj.dsem.num, 16 * self.dcount[obj.dsem.num])
        self._record(ev, reads, writes)
        return ev

    def barrier(self):
        evs = [(self.sem[e].num, self.cnt[e]) for e in self.sem if self.cnt[e] > 0]
        for s in self.dpool:
            if self.dcount[s.num] > 0:
                evs.append((s.num, 16 * self.dcount[s.num]))
        for e in self.E:
            for ev in evs:
                if e in self.sem and ev[0] == self.sem[e].num:
                    continue
                self._wait(e, ev)


def build_program():
    nc = bass.Bass("TRN2", target_bir_lowering=False)
    dr = {}

    def din(name, shape, dt=F32):
        dr[name] = nc.dram_tensor(name, list(shape), dt, kind="ExternalInput").ap()
        return dr[name]

    x_d = din("x", [S, D])
    p_d = din("p", [2, S, 256])
    pos_d = din("pos", [1, S], I32)
    gains_d = din("gains", [128, 7, 8])
    ident_d = din("ident", [128, 128])
    cwb_d = din("cwb", [2, 128, 44, 4])
    win_d = din("win", [2, 22, 128, 2 * NC * 128])
    wout_d = din("wout", [2, 2, NC, 128, 11 * 128])
    wgate_d = din("wgate", [2, NC, 128, NC * 128])
    wproj_d = din("wproj", [2, 128, NC * 2 * 128])
    odw_d = din("odw", [5, 128, NC * 512])
    odw_d2 = din("odw2", [128, NC * 640])
    odwo_d = din("odwo", [128, NC * D])
    odqk_d = din("odqk", [8, 128, NC * 128])
    lng_d = din("lng", [128, 512])
    lnb_d = din("lnb", [128, 512])
    sguw_d = din("sguw", [128, 4, 128])
    sgumask_d = din("sgumask", [128, 128])
    sgub_d = din("sgub", [128, 4, 512])
    foxbf_d = din("foxbf", [8, 1])
    cmask_d = din("cmask", [128, 128])
    evw_d = din("evw", [25, 128, NC * 128])
    evwo_d = din("evwo", [128, NC * D])
    poolw_d = din("poolw", [128, 4, 128])
    poolc_d = din("poolc", [128, 68])
    posr_d = din("posr", [128, S], I32)
    ropec_d = din("ropec", [128, 2])
    pert_d = din("pert", [128, S])
    out_d = nc.dram_tensor("out", [S, D], F32, kind="ExternalOutput").ap()

    with ExitStack() as es:
        kb = KB(nc, es)
        E = kb.E

        def sb(name, shape, dt, stack=es):
            return stack.enter_context(nc.sbuf_tensor("sb_" + name, list(shape), dt))

        hT = sb("hT", [128, NC, S], F32)
        hT_t = [kb.t("hT%d" % c) for c in range(NC)]
        gains = sb("gains", [128, 7, 8], F32)
        gains_t = kb.t("gains", dma=True)
        ident_f = sb("ident_f", [128, 128], F32)
        ident_f_t = kb.t("ident_f", dma=True)
        ident_b = sb("ident_b", [128, 128], BF16)
        ident_b_t = kb.t("ident_b", dma=True)
        ones_b = sb("ones_b", [128, 128], BF16)
        ones_b_t = kb.t("ones_b")
        one_c = sb("one_c", [128, 1], F32)
        onec_t = kb.t("onec")
        zeros_b = sb("zeros_b", [128, 512], BF16)
        zeros_t = kb.t("zeros")
        eps_c = sb("eps_c", [128, 1], F32)
        eps_t = kb.t("eps")
        banks = [es.enter_context(nc.psum_tensor("bank%d" % i, [128, 512], F32)) for i in range(8)]
        bank_t = [kb.t("bank%d" % i) for i in range(8)]

        kb.dma("sp", gains[:], gains_d[:, :, :], gains_t, writes=[gains_t])
        kb.dma("sp", ident_f[:], ident_d[:, :], ident_f_t, writes=[ident_f_t])
        kb.dma("pool", ident_b[:], ident_d[:, :], ident_b_t, writes=[ident_b_t])
        kb.op("dve", lambda: nc.vector.memset(ones_b[:], 1.0), writes=[ones_b_t])
        kb.op("dve", lambda: nc.vector.memset(eps_c[:], EPS), writes=[eps_t])
        kb.op("dve", lambda: nc.vector.memset(zeros_b[:], 0.0), writes=[zeros_t])
        kb.op("dve", lambda: nc.vector.memset(one_c[:], 1.0), writes=[onec_t])
        zero_c = sb("zero_c", [128, 1], F32)
        zeroc_t = kb.t("zeroc")
        tauc = sb("tauc", [128, 1], F32)
        tauc_t = kb.t("tauc")
        kb.op("dve", lambda: nc.vector.memset(zero_c[:], 0.0), writes=[zeroc_t])
        kb.op("dve", lambda: nc.vector.memset(tauc[:], 8.0 / (2 ** 26)), writes=[tauc_t])

        rr = {"i": 0}

        def alt(engs):
            rr["i"] += 1
            return engs[rr["i"] % len(engs)]

        with ExitStack() as ph:
            xs = [sb("xs%d" % i, [128, D], F32, ph) for i in range(2)]
            xs_t = [kb.t("xs%d" % i, dma=True) for i in range(2)]
            for tt in range(NT):
                b = tt % 2
                kb.dma("sp", xs[b][:], x_d[tt * 128:(tt + 1) * 128, :], xs_t[b], writes=[xs_t[b]])
                for half in range(2):
                    bi = (2 * tt + half) % 4
                    bk = banks[bi]
                    fns = []
                    for k in range(4):
                        c = half * 4 + k
                        fns.append(lambda k=k, c=c, bk=bk, b=b: nc.tensor.transpose(
                            out=bk[:, k * 128:(k + 1) * 128], in_=xs[b][:, c * 128:(c + 1) * 128], identity=ident_f[:]))
                    kb.group("pe", fns, reads=[xs_t[b], ident_f_t], writes=[bank_t[bi]])
                    eng = alt(["dve", "act"])
                    dst = hT[:, half * 4:(half + 1) * 4, tt * 128:(tt + 1) * 128]
                    src = bk[:, :].rearrange("p (k n) -> p k n", k=4)
                    if eng == "dve":
                        kb.op("dve", lambda dst=dst, src=src: nc.vector.tensor_copy(out=dst, in_=src),
                              reads=[bank_t[bi]], writes=hT_t[half * 4:(half + 1) * 4])
                    else:
                        kb.op("act", lambda dst=dst, src=src: nc.scalar.copy(out=dst, in_=src),
                              reads=[bank_t[bi]], writes=hT_t[half * 4:(half + 1) * 4])
            kb.barrier()
            kb.release_phase_dsems()

        def rmsnorm(ph_unused, gidx, outT, outT_t):
            with ExitStack() as ph:
                sqb = [sb("sqb%d_%d" % (gidx, i), [128, S], BF16, ph) for i in range(2)]
                sqb_t = [kb.t("sqb%d" % i) for i in range(2)]
                rstd = sb("rstd%d" % gidx, [128, S], F32, ph)
                rstd_t = [kb.t("rstd%d" % i) for i in range(4)]
                for c in range(NC):
                    b = c % 2
                    kb.op("act", lambda c=c, b=b: nc.scalar.activation(out=sqb[b][:], in_=hT[:, c, :], func=AF.Square),
                          reads=[hT_t[c]], writes=[sqb_t[b]])
                    fns = []
                    for tc in range(4):
                        fns.append(lambda tc=tc, c=c, b=b: nc.tensor.matmul(
                            banks[tc][:, :], lhsT=ones_b[:], rhs=sqb[b][:, tc * 512:(tc + 1) * 512],
                            start=(c == 0), stop=(c == NC - 1)))
                    kb.group("pe", fns, reads=[sqb_t[b], ones_b_t], writes=bank_t[0:4])
                for tc in range(4):
                    sl = slice(tc * 512, (tc + 1) * 512)
                    kb.op("act", lambda tc=tc, sl=sl: nc.scalar.activation(
                        out=rstd[:, sl], in_=banks[tc][:, :], func=AF.Sqrt, scale=1.0 / D, bias=eps_c[:]),
                        reads=[bank_t[tc], eps_t], writes=[rstd_t[tc]])
                    kb.op("dve", lambda sl=sl: nc.vector.reciprocal(out=rstd[:, sl], in_=rstd[:, sl]),
                          reads=[rstd_t[tc]], writes=[rstd_t[tc]])
                for c in range(NC):
                    kb.op("dve", lambda c=c: nc.vector.scalar_tensor_tensor(
                        out=outT[:, c, :], in0=hT[:, c, :], scalar=gains[:, gidx, c:c + 1], in1=rstd[:, :],
                        op0=ALU.mult, op1=ALU.mult),
                        reads=[hT_t[c], gains_t] + rstd_t, writes=[outT_t[c]])
                kb.barrier()

        def ffn_phase(l):
            with ExitStack() as ph:
                hn = sb("hn_f%d" % l, [128, NC, S], BF16, ph)
                hn_t = [kb.t("hn%d" % c) for c in range(NC)]
                cwb = sb("cwb%d" % l, [128, 44, 4], F32, ph)
                cwb_t = kb.t("cwb", dma=True)
                kb.dma("sp", cwb[:], cwb_d[l], cwb_t, writes=[cwb_t])
                win = [sb("win%d_%d" % (l, i), [128, 2, NC, 128], BF16, ph) for i in range(3)]
                win_t = [kb.t("win%d" % i, dma=True) for i in range(3)]
                wob = [sb("wob%d_%d" % (l, i), [128, 11, 128], BF16, ph) for i in range(2)]
                wob_t = [kb.t("wob%d" % i, dma=True) for i in range(2)]
                act = sb("act%d" % l, [128, 11, S], BF16, ph)
                act_t = [kb.t("act%d" % i) for i in range(11)]

                def load_win(j):
                    b = j % 3
                    kb.dma("pool", win[b][:].rearrange("p s c n -> p (s c n)"), win_d[l, j], win_t[b], writes=[win_t[b]])

                load_win(0)
                load_win(1)
                rmsnorm(ph, 3 * l + 1, hn, hn_t)
                A = [[sb("A%d_%d_%d" % (l, st, s_), [128, 1026], F32, ph) for s_ in range(2)] for st in range(2)]
                A_t = [[kb.t("A%d%d" % (st, s_)) for s_ in range(2)] for st in range(2)]
                C = [[sb("C%d_%d_%d" % (l, st, s_), [128, 1024], F32, ph) for s_ in range(2)] for st in range(2)]
                C_t = [[kb.t("C%d%d" % (st, s_)) for s_ in range(2)] for st in range(2)]
                step = 0
                for grp in range(2):
                    for jj in range(11):
                        j = grp * 11 + jj
                        if j + 2 < 22:
                            load_win(j + 2)
                        wb = win[j % 3]
                        wt = win_t[j % 3]
                        for half in range(2):
                            st = step % 2
                            step += 1
                            for s_ in range(2):
                                ft = s_ * 22 + j
                                for tcl in range(2):
                                    bi = 4 * st + 2 * s_ + tcl
                                    t0 = half * 1024 + tcl * 512
                                    fns = [(lambda c=c, bi=bi, s_=s_, t0=t0, wb=wb: nc.tensor.matmul(
                                        banks[bi][:, :], lhsT=wb[:, s_, c, :], rhs=hn[:, c, t0:t0 + 512],
                                        start=(c == 0), stop=(c == NC - 1))) for c in range(NC)]
                                    kb.group("pe", fns, reads=[wt] + hn_t, writes=[bank_t[bi]])
                            for s_ in range(2):
                                ft = s_ * 22 + j
                                a = A[st][s_]
                                at = A_t[st][s_]
                                cc = C[st][s_]
                                ct = C_t[st][s_]
                                if half == 0:
                                    kb.op("pool", lambda a=a: nc.gpsimd.memset(a[:, 0:2], 0.0), writes=[at])
                                else:
                                    ap_ = A[1 - st][s_]
                                    kb.op("pool", lambda a=a, ap_=ap_: nc.gpsimd.tensor_copy(out=a[:, 0:2], in_=ap_[:, 1024:1026]),
                                          reads=[A_t[1 - st][s_]], writes=[at])
                                for tcl in range(2):
                                    bi = 4 * st + 2 * s_ + tcl
                                    kb.op("act", lambda a=a, bi=bi, tcl=tcl: nc.scalar.copy(
                                        out=a[:, 2 + tcl * 512:2 + (tcl + 1) * 512], in_=banks[bi][:, :]),
                                        reads=[bank_t[bi]], writes=[at])
                                    kb.op("act", lambda cc=cc, bi=bi, tcl=tcl, ft=ft: nc.scalar.activation(
                                        out=cc[:, tcl * 512:(tcl + 1) * 512], in_=banks[bi][:, :], func=AF.Identity,
                                        scale=cwb[:, ft, 2:3], bias=cwb[:, ft, 3:4]),
                                        reads=[bank_t[bi], cwb_t], writes=[ct])
                                kb.op("dve", lambda a=a, cc=cc, ft=ft: nc.vector.scalar_tensor_tensor(
                                    out=cc[:, :], in0=a[:, 1:1025], scalar=cwb[:, ft, 1:2], in1=cc[:, :],
                                    op0=ALU.mult, op1=ALU.add), reads=[at, ct, cwb_t], writes=[ct])
                                kb.op("dve", lambda a=a, cc=cc, ft=ft: nc.vector.scalar_tensor_tensor(
                                    out=cc[:, :], in0=a[:, 0:1024], scalar=cwb[:, ft, 0:1], in1=cc[:, :],
                                    op0=ALU.mult, op1=ALU.add), reads=[at, ct, cwb_t], writes=[ct])
                            cg = C[st][0]
                            cu = C[st][1]
                            kb.op("act", lambda cg=cg: nc.scalar.activation(out=cg[:, :], in_=cg[:, :], func=AF.Gelu_apprx_tanh),
                                  reads=[C_t[st][0]], writes=[C_t[st][0]])
                            kb.op("dve", lambda cg=cg, cu=cu, jj=jj, half=half: nc.vector.tensor_tensor(
                                out=act[:, jj, half * 1024:(half + 1) * 1024], in0=cg[:, :], in1=cu[:, :], op=ALU.mult),
                                reads=[C_t[st][0], C_t[st][1]], writes=[act_t[jj]])
                    for dt in range(NC):
                        b = dt % 2
                        kb.dma("pool", wob[b][:].rearrange("p j n -> p (j n)"), wout_d[l, grp, dt], wob_t[b], writes=[wob_t[b]])
                        for tc in range(4):
                            bi = (dt * 4 + tc) % 8
                            fns = [(lambda q=q, bi=bi, b=b, tc=tc, dt=dt: nc.tensor.matmul(
                                banks[bi][:, :], lhsT=wob[b][:, q, :], rhs=act[:, q, tc * 512:(tc + 1) * 512],
                                start=(q == 0), stop=(q == 10))) for q in range(11)]
                            kb.group("pe", fns, reads=[wob_t[b]] + act_t, writes=[bank_t[bi]])
                            kb.op("dve", lambda bi=bi, dt=dt, tc=tc: nc.vector.tensor_tensor(
                                out=hT[:, dt, tc * 512:(tc + 1) * 512], in0=hT[:, dt, tc * 512:(tc + 1) * 512],
                                in1=banks[bi][:, :], op=ALU.add), reads=[bank_t[bi], hT_t[dt]], writes=[hT_t[dt]])
                kb.barrier()
                kb.release_phase_dsems()

        def ple_phase(l):
            with ExitStack() as ph:
                hn = sb("hn_p%d" % l, [128, NC, S], BF16, ph)
                hn_t = [kb.t("hn%d" % c) for c in range(NC)]
                wg = sb("wg%d" % l, [128, NC, NC, 128], BF16, ph)
                wg_t = [kb.t("wg%d" % i, dma=True) for i in range(NC)]
                wp = sb("wp%d" % l, [128, NC, 2, 128], BF16, ph)
                wp_t = kb.t("wp", dma=True)
                pb = sb("pb%d" % l, [128, NT, 256], BF16, ph)
                pb_t = kb.t("pb", dma=True)
                pT = sb("pT%d" % l, [128, 2, S], BF16, ph)
                pT_t = kb.t("pT")
                gs = [sb("gs%d_%d" % (l, i), [128, 512], F32, ph) for i in range(2)]
                gs_t = [kb.t("gs%d" % i) for i in range(2)]
                kb.dma("pool", pb[:], p_d[l].rearrange("(tt p) f -> p tt f", p=128), pb_t, writes=[pb_t])
                kb.dma("pool", wp[:].rearrange("p d c n -> p (d c n)"), wproj_d[l], wp_t, writes=[wp_t])
                for dt in range(NC):
                    kb.dma("pool", wg[:, dt].rearrange("p c n -> p (c n)"), wgate_d[l, dt], wg_t[dt], writes=[wg_t[dt]])
                rmsnorm(ph, 3 * l + 2, hn, hn_t)
                for tt in range(NT):
                    bi = 4 + tt % 4
                    bkb = banks[bi][:, :].bitcast(BF16)
                    fns = [(lambda c2=c2, bkb=bkb, tt=tt: nc.tensor.transpose(
                        out=bkb[:, c2 * 128:(c2 + 1) * 128], in_=pb[:, tt, c2 * 128:(c2 + 1) * 128], identity=ident_b[:]))
                        for c2 in range(2)]
                    kb.group("pe", fns, reads=[pb_t, ident_b_t], writes=[bank_t[bi]])
                    eng = alt(["dve", "act"])
                    dst = pT[:, :, tt * 128:(tt + 1) * 128]
                    src = bkb[:, 0:256].rearrange("p (k n) -> p k n", k=2)
                    if eng == "dve":
                        kb.op("dve", lambda dst=dst, src=src: nc.vector.tensor_copy(out=dst, in_=src),
                              reads=[bank_t[bi]], writes=[pT_t])
                    else:
                        kb.op("act", lambda dst=dst, src=src: nc.scalar.copy(out=dst, in_=src),
                              reads=[bank_t[bi]], writes=[pT_t])
                it = 0
                for dt in range(NC):
                    for tc in range(4):
                        b = it % 2
                        it += 1
                        bg = 2 * b
                        bp = 2 * b + 1
                        sl = slice(tc * 512, (tc + 1) * 512)
                        fns = [(lambda c=c, bg=bg, dt=dt, sl=sl: nc.tensor.matmul(
                            banks[bg][:, :], lhsT=wg[:, dt, c, :], rhs=hn[:, c, sl], start=(c == 0), stop=(c == NC - 1)))
                            for c in range(NC)]
                        kb.group("pe", fns, reads=[wg_t[dt]] + hn_t, writes=[bank_t[bg]])
                        fns = [(lambda c2=c2, bp=bp, dt=dt, sl=sl: nc.tensor.matmul(
                            banks[bp][:, :], lhsT=wp[:, dt, c2, :], rhs=pT[:, c2, sl], start=(c2 == 0), stop=(c2 == 1)))
                            for c2 in range(2)]
                        kb.group("pe", fns, reads=[wp_t, pT_t], writes=[bank_t[bp]])
                        kb.op("act", lambda b=b, bg=bg: nc.scalar.activation(out=gs[b][:, :], in_=banks[bg][:, :], func=AF.Sigmoid),
                              reads=[bank_t[bg]], writes=[gs_t[b]])
                        kb.op("dve", lambda b=b, bp=bp: nc.vector.tensor_tensor(
                            out=gs[b][:, :], in0=gs[b][:, :], in1=banks[bp][:, :], op=ALU.mult),
                            reads=[gs_t[b], bank_t[bp]], writes=[gs_t[b]])
                        kb.op("pool", lambda b=b, dt=dt, sl=sl: nc.gpsimd.tensor_tensor(
                            out=hT[:, dt, sl], in0=hT[:, dt, sl], in1=gs[b][:, :], op=ALU.add),
                            reads=[gs_t[b], hT_t[dt]], writes=[hT_t[dt]])
                kb.barrier()
                kb.release_phase_dsems()

        def attn_group(h, qc, ph_bufs, k_lhsT, q_rhs, v_rhs, mask_fn, y_tm, y_tm_t, kin_t, vin_t):
            PT, PT_t, rden, rden_t, st = ph_bufs
            bo = 4 + (st["g"] % 2)
            st["g"] += 1
            obank = banks[bo]
            kb.group("pe", [lambda: nc.tensor.matmul(obank[:, 0:260], lhsT=zeros_b[:, 0:128], rhs=zeros_b[:, 0:260],
                                                      start=True, stop=False, skip_group_check=True)],
                     reads=[zeros_t], writes=[bank_t[bo]])
            nk = 4 * qc + 4
            for kt in range(nk):
                r = max(0, kt - 4 * qc)
                w = 512 - 128 * r
                c0 = qc * 512 + r * 128
                bs = 6 + (st["s"] % 2)
                pb_ = st["s"] % 3
                st["s"] += 1
                sbank = banks[bs]
                mk = mask_fn(kt, qc, r)
                fns = [lambda kt=kt, c0=c0, w=w, sbank=sbank, mk=mk: nc.tensor.matmul(
                    sbank[:, 0:w], lhsT=k_lhsT(kt), rhs=q_rhs(c0, c0 + w), start=True, stop=(mk is None))]
                rds = list(kin_t)
                if mk is not None:
                    mrhs, mw, mt = mk
                    fns.append(lambda sbank=sbank, mrhs=mrhs, mw=mw: nc.tensor.matmul(
                        sbank[:, 0:mw], lhsT=ident_b[:], rhs=mrhs, start=False, stop=True))
                    rds += [mt, ident_b_t]
                kb.group("pe", fns, reads=rds, writes=[bank_t[bs]])
                kb.op("act", lambda pb_=pb_, w=w, sbank=sbank: nc.scalar.activation(
                    out=PT[pb_][:, 0:w], in_=sbank[:, 0:w], func=AF.Exp, scale=0.125),
                    reads=[bank_t[bs]], writes=[PT_t[pb_]])
                fns = []
                for qs in range(r, 4):
                    last = (kt == nk - 1 and qs == 3)
                    fns.append(lambda qs=qs, r=r, pb_=pb_, kt=kt, last=last: nc.tensor.matmul(
                        obank[:, qs * 65:qs * 65 + 65], lhsT=PT[pb_][:, (qs - r) * 128:(qs - r + 1) * 128],
                        rhs=v_rhs(kt), start=False, stop=last, skip_group_check=True))
                kb.group("pe", fns, reads=[PT_t[pb_]] + list(vin_t), writes=[bank_t[bo]])
            rb = st["g"] % 2
            ov = obank[:, 0:260].rearrange("p (q e) -> p q e", e=65)
            kb.op("dve", lambda rb=rb, ov=ov: nc.vector.reciprocal(out=rden[rb][:, 0:4], in_=ov[:, :, 64]),
                  reads=[bank_t[bo]], writes=[rden_t[rb]])
            for qs in range(4):
                tt = qc * 4 + qs
                kb.op("dve", lambda qs=qs, tt=tt, rb=rb: nc.vector.tensor_scalar(
                    out=y_tm[:, tt, h * 64:(h + 1) * 64], in0=obank[:, qs * 65:qs * 65 + 64],
                    scalar1=rden[rb][:, qs:qs + 1], scalar2=None, op0=ALU.mult),
                    reads=[rden_t[rb]], writes=[y_tm_t, bank_t[bo]])

        def tm_to_fm(y_tm, y_tm_t, ymT, ymT_t, c_base):
            for tt in range(NT):
                bi = tt % 4
                bkb = banks[bi][:, :].bitcast(BF16)
                fns = [(lambda k=k, bkb=bkb, tt=tt: nc.tensor.transpose(
                    out=bkb[:, k * 128:(k + 1) * 128], in_=y_tm[:, tt, k * 128:(k + 1) * 128], identity=ident_b[:]))
                    for k in range(4)]
                kb.group("pe", fns, reads=[y_tm_t, ident_b_t], writes=[bank_t[bi]])
                dst = ymT[:, c_base:c_base + 4, tt * 128:(tt + 1) * 128]
                src = bkb[:, 0:512].rearrange("p (k n) -> p k n", k=4)
                eng = alt(["dve", "act"])
                if eng == "dve":
                    kb.op("dve", lambda dst=dst, src=src: nc.vector.tensor_copy(out=dst, in_=src),
                          reads=[bank_t[bi]], writes=ymT_t[c_base:c_base + 4])
                else:
                    kb.op("act", lambda dst=dst, src=src: nc.scalar.copy(out=dst, in_=src),
                          reads=[bank_t[bi]], writes=ymT_t[c_base:c_base + 4])

        def mixer_out(ph, ymT, ymT_t, wo_dram, nm, ym_fn=None):
            wo = sb("wo_mix" + nm, [128, NC, D], BF16, ph)
            wo_t = kb.t("wo_mix", dma=True)
            kb.dma("pool", wo[:].rearrange("p c n -> p (c n)"), wo_dram, wo_t, writes=[wo_t])
            for dt in range(NC):
                for tc in range(4):
                    bi = (dt * 4 + tc) % 4
                    sl = slice(tc * 512, (tc + 1) * 512)
                    fns = [(lambda c=c, bi=bi, dt=dt, sl=sl: nc.tensor.matmul(
                        banks[bi][:, :], lhsT=wo[:, c, dt * 128:(dt + 1) * 128], rhs=(ym_fn(c, sl) if ym_fn else ymT[:, c, sl]),
                        start=(c == 0), stop=(c == NC - 1))) for c in range(NC)]
                    kb.group("pe", fns, reads=[wo_t] + ymT_t, writes=[bank_t[bi]])
                    kb.op("dve", lambda bi=bi, dt=dt, sl=sl: nc.vector.tensor_tensor(
                        out=hT[:, dt, sl], in0=hT[:, dt, sl], in1=banks[bi][:, :], op=ALU.add),
                        reads=[bank_t[bi], hT_t[dt]], writes=[hT_t[dt]])

        def odd_phase(l):
            with ExitStack() as ph:
                hn = sb("hn_o", [128, NC, S], BF16, ph)
                hn_t = [kb.t("hn%d" % c) for c in range(NC)]
                ymT = sb("ymT_o", [128, NC, S], BF16, ph)
                ymT_t = [kb.t("ymT%d" % c) for c in range(NC)]
                rmsnorm(ph, 3 * l, hn, hn_t)
                with ExitStack() as p2:
                    wu = sb("wu", [128, NC, 512], BF16, p2)
                    wu_t = kb.t("wu", dma=True)
                    wv = sb("wsv", [128, NC, 512], BF16, p2)
                    wv_t = kb.t("wsv", dma=True)
                    kb.dma("pool", wv[:].rearrange("p c n -> p (c n)"), odw_d[4], wv_t, writes=[wv_t])
                    kb.dma("pool", wu[:].rearrange("p c n -> p (c n)"), odw_d[3], wu_t, writes=[wu_t])
                    vn = sb("vn_tm", [128, NT, 512], BF16, p2)
                    vn_t = [kb.t("vn%d" % i) for i in range(NT)]
                    lng = sb("lng", [128, 512], F32, p2)
                    lnb = sb("lnb", [128, 512], F32, p2)
                    ln_t = kb.t("ln", dma=True)
                    kb.dma("sp", lng[:], lng_d[:, :], ln_t, writes=[ln_t])
                    kb.dma("sp", lnb[:], lnb_d[:, :], ln_t, writes=[ln_t])
                    swT = sb("swT", [128, 4, 128], F32, p2)
                    smk = sb("smk", [128, 128], F32, p2)
                    sw_t = kb.t("sw", dma=True)
                    kb.dma("sp", swT[:], sguw_d[:, :, :], sw_t, writes=[sw_t])
                    kb.dma("sp", smk[:], sgumask_d[:, :], sw_t, writes=[sw_t])
                    wm = sb("wm", [128, 4, 128], BF16, p2)
                    wm_t = kb.t("wm")
                    for g in range(4):
                        kb.op("dve", lambda g=g: nc.vector.tensor_tensor(out=wm[:, g, :], in0=swT[:, g, :], in1=smk[:, :], op=ALU.mult),
                              reads=[sw_t], writes=[wm_t])
                    brep = sb("brep", [128, 4, 512], F32, p2)
                    brep_t = kb.t("brep", dma=True)
                    kb.dma("sp", brep[:], sgub_d[:, :, :], brep_t, writes=[brep_t])
                    vg = [sb("vg%d" % i, [128, 512], F32, p2) for i in range(2)]
                    vg_t = [kb.t("vg%d" % i) for i in range(2)]
                    stt = [sb("stt%d" % i, [128, 8], F32, p2) for i in range(2)]
                    stt_t = [kb.t("stt%d" % i) for i in range(2)]
                    for tt in range(NT):
                        b = tt % 2
                        bi = tt % 4
                        fns = [(lambda c=c, bi=bi, tt=tt: nc.tensor.matmul(
                            banks[bi][:, :], lhsT=hn[:, c, tt * 128:(tt + 1) * 128], rhs=wv[:, c, :],
                            start=(c == 0), stop=(c == NC - 1))) for c in range(NC)]
                        kb.group("pe", fns, reads=[wv_t] + hn_t, writes=[bank_t[bi]])
                        kb.op("act", lambda b=b, bi=bi: nc.scalar.activation(out=vg[b][:, :], in_=banks[bi][:, :], func=AF.Gelu_apprx_tanh),
                              reads=[bank_t[bi]], writes=[vg_t[b]])
                        kb.op("dve", lambda b=b: nc.vector.bn_stats(out=stt[b][:, 0:6], in_=vg[b][:, :]),
                              reads=[vg_t[b]], writes=[stt_t[b]])
                        kb.op("dve", lambda b=b: nc.vector.bn_aggr(out=stt[b][:, 6:8], in_=stt[b][:, 0:6]),
                              reads=[stt_t[b]], writes=[stt_t[b]])
                        kb.op("act", lambda b=b: nc.scalar.activation(out=stt[b][:, 7:8], in_=stt[b][:, 7:8], func=AF.Sqrt, bias=eps_c[:]),
                              reads=[stt_t[b], eps_t], writes=[stt_t[b]])
                        kb.op("dve", lambda b=b: nc.vector.reciprocal(out=stt[b][:, 7:8], in_=stt[b][:, 7:8]),
                              reads=[stt_t[b]], writes=[stt_t[b]])
                        kb.op("dve", lambda b=b: nc.vector.tensor_scalar(
                            out=vg[b][:, :], in0=vg[b][:, :], scalar1=stt[b][:, 6:7], scalar2=stt[b][:, 7:8],
                            op0=ALU.subtract, op1=ALU.mult), reads=[vg_t[b], stt_t[b]], writes=[vg_t[b]])
                        kb.op("pool", lambda b=b: nc.gpsimd.tensor_tensor(out=vg[b][:, :], in0=vg[b][:, :], in1=lng[:, :], op=ALU.mult),
                              reads=[vg_t[b], ln_t], writes=[vg_t[b]])
                        kb.op("pool", lambda b=b, tt=tt: nc.gpsimd.tensor_tensor(out=vn[:, tt, :], in0=vg[b][:, :], in1=lnb[:, :], op=ALU.add),
                              reads=[vg_t[b], ln_t], writes=[vn_t[tt]])
                    us = [sb("us%d" % i, [128, 512], F32, p2) for i in range(2)]
                    us_t = [kb.t("us%d" % i) for i in range(2)]
                    ms = [sb("ms%d" % i, [128, 512], F32, p2) for i in range(2)]
                    ms_t = [kb.t("ms%d" % i) for i in range(2)]
                    it = 0
                    for g in range(4):
                        for tc in range(4):
                            b = it % 2
                            it += 1
                            bu = 4 + 2 * b
                            bm = 5 + 2 * b
                            sl = slice(tc * 512, (tc + 1) * 512)
                            fns = [(lambda c=c, bu=bu, g=g, sl=sl: nc.tensor.matmul(
                                banks[bu][:, :], lhsT=wu[:, c, g * 128:(g + 1) * 128], rhs=hn[:, c, sl],
                                start=(c == 0), stop=(c == NC - 1))) for c in range(NC)]
                            kb.group("pe", fns, reads=[wu_t] + hn_t, writes=[bank_t[bu]])
                            fns = [(lambda q=q, bm=bm, g=g, tc=tc: nc.tensor.matmul(
                                banks[bm][:, q * 128:(q + 1) * 128], lhsT=vn[:, tc * 4 + q, g * 128:(g + 1) * 128],
                                rhs=wm[:, g, :], start=True, stop=True)) for q in range(4)]
                            kb.group("pe", fns, reads=[wm_t] + vn_t[tc * 4:tc * 4 + 4], writes=[bank_t[bm]])
                            kb.op("act", lambda b=b, bu=bu: nc.scalar.activation(out=us[b][:, :], in_=banks[bu][:, :], func=AF.Gelu_apprx_tanh),
                                  reads=[bank_t[bu]], writes=[us_t[b]])
                            kb.op("dve", lambda b=b, bm=bm, g=g: nc.vector.tensor_tensor(
                                out=ms[b][:, :], in0=brep[:, g, :], in1=banks[bm][:, :], op=ALU.add),
                                reads=[bank_t[bm], brep_t], writes=[ms_t[b]])
                            kb.op("dve", lambda b=b, g=g, sl=sl: nc.vector.tensor_tensor(
                                out=ymT[:, 4 + g, sl], in0=ms[b][:, :], in1=us[b][:, :], op=ALU.mult),
                                reads=[ms_t[b], us_t[b]], writes=[ymT_t[4 + g]])
                    kb.barrier()
                    kb.release_phase_dsems()
                y_tm = sb("yfox_tm", [128, NT, 512], BF16, ph)
                y_tm_t = kb.t("yfox_tm")
                vaug = sb("vaug", [128, NT, 8, 65], BF16, ph)
                vaug_t = kb.t("vaug")
                Pp = [sb("Pp%d" % i, [8, S], BF16, ph) for i in range(3)]
                Pp_t = kb.t("Pp")
                with ExitStack() as p2:
                    wvf = sb("wvf", [128, NC, 640], BF16, p2)
                    wvf_t = kb.t("wvf", dma=True)
                    kb.dma("pool", wvf[:].rearrange("p c n -> p (c n)"), odw_d2[:, :], wvf_t, writes=[wvf_t])
                    bfc = sb("bfc", [8, 2], F32, p2)
                    bfc_t = kb.t("bfc", dma=True)
                    kb.dma("sp", bfc[:, 0:1], foxbf_d[:, :], bfc_t, writes=[bfc_t])
                    kb.op("dve", lambda: nc.vector.tensor_scalar(out=bfc[:, 1:2], in0=bfc[:, 0:1], scalar1=-1.0, scalar2=None, op0=ALU.mult),
                          reads=[bfc_t], writes=[bfc_t])
                    L8 = sb("L8", [8, S], F32, p2)
                    one8 = sb("one8", [8, S], BF16, p2)
                    one8_t = kb.t("one8")
                    kb.op("pool", lambda: nc.gpsimd.memset(one8[:, :], 1.0), writes=[one8_t])
                    kb.op("pool", lambda: nc.gpsimd.memset(vaug[:].rearrange("p a b c -> p (a b c)"), 1.0), writes=[vaug_t])
                    for tt in range(NT):
                        bi = tt % 4
                        fns = [(lambda c=c, bi=bi, tt=tt: nc.tensor.matmul(
                            banks[bi][:, :], lhsT=hn[:, c, tt * 128:(tt + 1) * 128], rhs=wvf[:, c, 0:512],
                            start=(c == 0), stop=(c == NC - 1))) for c in range(NC)]
                        kb.group("pe", fns, reads=[wvf_t] + hn_t, writes=[bank_t[bi]])
                        src = banks[bi][:, :].rearrange("p (h e) -> p h e", e=64)
                        eng = alt(["dve", "act"])
                        if eng == "dve":
                            kb.op("dve", lambda tt=tt, src=src: nc.vector.tensor_copy(out=vaug[:, tt, :, 0:64], in_=src),
                                  reads=[bank_t[bi]], writes=[vaug_t])
                        else:
                            kb.op("act", lambda tt=tt, src=src: nc.scalar.copy(out=vaug[:, tt, :, 0:64], in_=src),
                                  reads=[bank_t[bi]], writes=[vaug_t])
                    lf = sb("lf", [8, S], F32, p2)
                    lf_t = kb.t("lf")
                    for tc in range(4):
                        bi = 4 + tc
                        sl = slice(tc * 512, (tc + 1) * 512)
                        fns = [(lambda c=c, bi=bi, sl=sl: nc.tensor.matmul(
                            banks[bi][:, :], lhsT=wvf[:, c, 512:640], rhs=hn[:, c, sl],
                            start=(c == 0), stop=(c == NC - 1))) for c in range(NC)]
                        kb.group("pe", fns, reads=[wvf_t] + hn_t, writes=[bank_t[bi]])
                        kb.op("act", lambda bi=bi, sl=sl: nc.scalar.activation(
                            out=lf[:, sl], in_=banks[bi][0:8, :], func=AF.Exp, scale=-1.0, bias=bfc[:, 1:2]),
                            reads=[bank_t[bi], bfc_t], writes=[lf_t])
                    kb.op("act", lambda: nc.scalar.activation(out=lf[:, :], in_=lf[:, :], func=AF.Ln, bias=one_c[0:8, :]),
                          reads=[lf_t, onec_t], writes=[lf_t])
                    kb.op("dve", lambda: nc.vector.tensor_tensor_scan(out=L8[:, :], data0=one8[:, :], data1=lf[:, :], initial=0.0,
                                                                        op0=ALU.mult, op1=ALU.add),
                          reads=[lf_t, one8_t], writes=[Pp_t])
                    kb.op("dve", lambda: nc.vector.tensor_scalar(out=L8[:, :], in0=L8[:, :], scalar1=8.0, scalar2=None, op0=ALU.mult),
                          reads=[Pp_t], writes=[Pp_t])
                    for i in range(3):
                        kb.op("dve", lambda i=i: nc.vector.tensor_copy(out=Pp[i][:, :], in_=L8[:, :]), reads=[Pp_t], writes=[Pp_t])
                        if i < 2:
                            kb.op("dve", lambda i=i: nc.vector.tensor_tensor(out=L8[:, :], in0=L8[:, :], in1=Pp[i][:, :], op=ALU.subtract),
                                  reads=[Pp_t], writes=[Pp_t])
                    kb.barrier()
                    kb.release_phase_dsems()
                with ExitStack() as p2:
                    wqk = [sb("wqk%d" % i, [128, NC, 128], BF16, p2) for i in range(2)]
                    wqk_t = [kb.t("wqk%d" % i, dma=True) for i in range(2)]
                    qa = [sb("qa%d" % i, [128, S], BF16, p2) for i in range(2)]
                    ka = [sb("ka%d" % i, [128, S], BF16, p2) for i in range(2)]
                    qa_t = [kb.t("qa%d" % i, dma=True) for i in range(2)]
                    ka_t = [kb.t("ka%d" % i, dma=True) for i in range(2)]
                    PT = [sb("PT%d" % i, [128, 512], BF16, p2) for i in range(3)]
                    PT_t = [kb.t("PT%d" % i) for i in range(3)]
                    rden = [sb("rden%d" % i, [128, 4], F32, p2) for i in range(2)]
                    rden_t = [kb.t("rden%d" % i) for i in range(2)]
                    cmask = sb("cmask", [128, 128], BF16, p2)
                    cmask_t = kb.t("cmask", dma=True)
                    kb.dma("pool", cmask[:], cmask_d[:, :], cmask_t, writes=[cmask_t])
                    st = {"g": 0, "s": 0}
                    bufs = (PT, PT_t, rden, rden_t, st)
                    for h in (range(8) if "nofox" not in STAGES else []):
                        hb = h % 2
                        kb.dma("pool", wqk[hb][:].rearrange("p c n -> p (c n)"), odqk_d[h], wqk_t[hb], writes=[wqk_t[hb]])
                        for tc in range(4):
                            bi = tc % 4
                            sl = slice(tc * 512, (tc + 1) * 512)
                            fns = [(lambda c=c, bi=bi, sl=sl: nc.tensor.matmul(
                                banks[bi][:, :], lhsT=wqk[hb][:, c, :], rhs=hn[:, c, sl],
                                start=(c == 0), stop=(c == NC - 1))) for c in range(NC)]
                            kb.group("pe", fns, reads=[wqk_t[hb]] + hn_t, writes=[bank_t[bi]])
                            kb.op("dve", lambda bi=bi, sl=sl: nc.vector.tensor_copy(out=qa[hb][0:64, sl], in_=banks[bi][0:64, :]),
                                  reads=[], writes=[qa_t[hb], bank_t[bi]])
                            kb.op("act", lambda bi=bi, sl=sl: nc.scalar.copy(out=ka[hb][0:64, sl], in_=banks[bi][64:128, :]),
                                  reads=[], writes=[ka_t[hb], bank_t[bi]])
                        kb.op("pool", lambda: nc.gpsimd.memset(qa[hb][64:128, :], 0.0), writes=[qa_t[hb]])
                        kb.op("pool", lambda: nc.gpsimd.memset(ka[hb][64:128, :], 0.0), writes=[ka_t[hb]])
                        kb.op("pool", lambda: nc.gpsimd.memset(qa[hb][64:70, :], 1.0), writes=[qa_t[hb]])
                        kb.op("pool", lambda: nc.gpsimd.memset(ka[hb][64:70, :], -1.0), writes=[ka_t[hb]])
                        for i in range(3):
                            kb.dma("sp", qa[hb][64 + i:65 + i, :], Pp[i][h:h + 1, :], qa_t[hb], reads=[Pp_t], writes=[qa_t[hb]])
                            kb.dma("sp", ka[hb][67 + i:68 + i, :], Pp[i][h:h + 1, :], ka_t[hb], reads=[Pp_t], writes=[ka_t[hb]])

                        def mask_fn(kt, qc, r):
                            if kt >= 4 * qc:
                                return (cmask[:, :], 128, cmask_t)
                            return None
                        for qc in range(4):
                            attn_group(h, qc, bufs,
                                       k_lhsT=lambda kt, hb=hb: ka[hb][:, kt * 128:(kt + 1) * 128],
                                       q_rhs=lambda c0, c1, hb=hb: qa[hb][:, c0:c1],
                                       v_rhs=lambda kt, h=h: vaug[:, kt, h, :],
                                       mask_fn=mask_fn, y_tm=y_tm, y_tm_t=y_tm_t,
                                       kin_t=[qa_t[hb], ka_t[hb]], vin_t=[vaug_t])
                    kb.barrier()
                    kb.release_phase_dsems()
                with ExitStack() as p2:
                    if "nofox" in STAGES:
                        kb.op("pool", lambda: nc.gpsimd.memset(y_tm[:].rearrange("p a b -> p (a b)"), 0.0), writes=[y_tm_t])
                    tm_to_fm(y_tm, y_tm_t, ymT, ymT_t, 0)
                    mixer_out(p2, ymT, ymT_t, odwo_d[:, :], "o")
                    kb.barrier()
                    kb.release_phase_dsems()

        def even_phase(l):
            R0 = 8.0
            NIT = 26
            with ExitStack() as ph:
                ypT = sb("ypT", [128, 4, S], BF16, ph)
                ypT_t = [kb.t("ypT%d" % c) for c in range(4)]
                qr = sb("qr", [128, 4, S], BF16, ph)
                qr_t = [kb.t("qr%d" % c) for c in range(4)]
                kAB = sb("kAB", [128, 2, S], BF16, ph)
                kAB_t = [kb.t("kAB%d" % c) for c in range(2)]
                iqr = sb("iqr", [128, 2, S], BF16, ph)
                iqr_t = [kb.t("iqr%d" % c) for c in range(2)]
                ikAB = sb("ikAB", [128, 2, S], BF16, ph)
                ikAB_t = [kb.t("ikAB%d" % c) for c in range(2)]
                vaug = sb("vaug_e", [128, NT, 65], BF16, ph)
                vaug_t = kb.t("vaug_e")
                iwt = sb("iwt", [128, NT, 4], F32, ph)
                iwt_t = kb.t("iwt")
                with ExitStack() as p1:
                    hn = sb("hn_e", [128, NC, S], BF16, p1)
                    hn_t = [kb.t("hn%d" % c) for c in range(NC)]
                    wt = [sb("ewt%d" % i, [128, NC, 128], BF16, p1) for i in range(4)]
                    wt_t = [kb.t("ewt%d" % i, dma=True) for i in range(4)]
                    wst = {"i": 0}

                    def load_w(idx):
                        b = wst["i"] % 4
                        wst["i"] += 1
                        kb.dma("pool", wt[b][:].rearrange("p c n -> p (c n)"), evw_d[idx], wt_t[b], writes=[wt_t[b]])
                        return wt[b], wt_t[b]

                    def proj_fm(w, w_t, bi, tc):
                        sl = slice(tc * 512, (tc + 1) * 512)
                        fns = [(lambda c=c: nc.tensor.matmul(banks[bi][:, :], lhsT=w[:, c, :], rhs=hn[:, c, sl],
                                                             start=(c == 0), stop=(c == NC - 1))) for c in range(NC)]
                        kb.group("pe", fns, reads=[w_t] + hn_t, writes=[bank_t[bi]])

                    rmsnorm(p1, 3 * l, hn, hn_t)
                    wv_, wv_t_ = load_w(24)
                    kb.op("pool", lambda: nc.gpsimd.memset(vaug[:].rearrange("p a b -> p (a b)"), 1.0), writes=[vaug_t])
                    for tt in range(NT):
                        bi = tt % 4
                        fns = [(lambda c=c, bi=bi, tt=tt: nc.tensor.matmul(
                            banks[bi][:, 0:128], lhsT=hn[:, c, tt * 128:(tt + 1) * 128], rhs=wv_[:, c, :],
                            start=(c == 0), stop=(c == NC - 1))) for c in range(NC)]
                        kb.group("pe", fns, reads=[wv_t_] + hn_t, writes=[bank_t[bi]])
                        kb.op("dve", lambda bi=bi, tt=tt: nc.vector.tensor_copy(out=vaug[:, tt, 0:64], in_=banks[bi][:, 0:64]),
                              reads=[], writes=[vaug_t, bank_t[bi]])
                        kb.op("dve", lambda bi=bi, tt=tt: nc.vector.tensor_scalar(
                            out=iwt[:, tt, :], in0=banks[bi][:, 64:68], scalar1=0.0625, scalar2=None, op0=ALU.mult),
                            reads=[], writes=[iwt_t, bank_t[bi]])
                    with ExitStack() as p2:
                        zp = [sb("zp%d" % i, [128, 16 + S], F32, p2) for i in range(2)]
                        zp_t = [kb.t("zp%d" % i) for i in range(2)]
                        z0 = sb("z0", [128, S], F32, p2)
                        z0_t = kb.t("z0")
                        dd = sb("dd", [128, S], BF16, p2)
                        dd_t = kb.t("dd")
                        d16 = sb("d16", [128, 16], F32, p2)
                        d16_t = kb.t("d16")
                        pw = sb("pw", [128, 4, 128], BF16, p2)
                        pw_t = kb.t("pw", dma=True)
                        kb.dma("pool", pw[:], poolw_d[:, :, :], pw_t, writes=[pw_t])
                        pcs = sb("pcs", [128, 4 + 64], F32, p2)
                        pcs_t = kb.t("pcs", dma=True)
                        kb.dma("sp", pcs[:], poolc_d[:, :], pcs_t, writes=[pcs_t])
                        for i in range(2):
                            kb.op("pool", lambda i=i: nc.gpsimd.memset(zp[i][:, 0:16], 0.0), writes=[zp_t[i]])
                        for g in range(4):
                            w_, w_t_ = load_w(g)
                            for tc in range(4):
                                bi = 4 + tc
                                proj_fm(w_, w_t_, bi, tc)
                                kb.op("act", lambda bi=bi, tc=tc: nc.scalar.copy(out=z0[:, tc * 512:(tc + 1) * 512], in_=banks[bi][:, :]),
                                      reads=[], writes=[z0_t, bank_t[bi]])
                            cur = None
                            nsteps = g + 1
                            for sidx in range(nsteps):
                                sh = 1 << sidx
                                o = zp[sidx % 2]
                                ot = zp_t[sidx % 2]
                                if sidx == 0:
                                    kb.op("pool", lambda: nc.gpsimd.tensor_copy(out=zp[1][:, 16:16 + S], in_=z0[:, :]),
                                          reads=[z0_t], writes=[zp_t[1]])
                                    src, srct = zp[1], zp_t[1]
                                else:
                                    src, srct = zp[(sidx + 1) % 2], zp_t[(sidx + 1) % 2]
                                eng = "pool" if sidx % 2 == 0 else "dve"
                                kb.op(eng, lambda o=o, src=src, sh=sh, eng=eng: E[eng].tensor_tensor(
                                    out=o[:, 16:16 + S], in0=src[:, 16:16 + S], in1=src[:, 16 - sh:16 - sh + S], op=ALU.add),
                                    reads=[srct], writes=[ot])
                                cur, curt = o, ot
                            wwin = float(2 << g)
                            kb.op("dve", lambda cur=cur, wwin=wwin: nc.vector.scalar_tensor_tensor(
                                out=dd[:, :], in0=cur[:, 16:16 + S], scalar=1.0 / wwin, in1=z0[:, :], op0=ALU.mult, op1=ALU.subtract),
                                reads=[curt, z0_t], writes=[dd_t])
                            kb.op("dve", lambda cur=cur, g=g: nc.vector.tensor_tensor(
                                out=d16[:, :], in0=cur[:, 16:32], in1=pcs[:, 4 + g * 16:4 + (g + 1) * 16], op=ALU.mult),
                                reads=[curt, pcs_t], writes=[d16_t])
                            kb.op("dve", lambda: nc.vector.tensor_tensor(out=dd[:, 0:16], in0=d16[:, :], in1=z0[:, 0:16], op=ALU.subtract),
                                  reads=[d16_t, z0_t, dd_t], writes=[dd_t])
                            for tc in range(4):
                                bi = tc
                                sl = slice(tc * 512, (tc + 1) * 512)
                                kb.group("pe", [lambda bi=bi, g=g, sl=sl: nc.tensor.matmul(
                                    banks[bi][:, :], lhsT=pw[:, g, :], rhs=dd[:, sl], start=True, stop=True)],
                                    reads=[pw_t, dd_t], writes=[bank_t[bi]])
                                kb.op("act", lambda bi=bi, g=g, sl=sl: nc.scalar.activation(
                                    out=ypT[:, g, sl], in_=banks[bi][:, :], func=AF.Identity, scale=pcs[:, g:g + 1]),
                                    reads=[pcs_t], writes=[ypT_t[g], bank_t[bi]])
                        kb.barrier()
                    with ExitStack() as p2:
                        if "e_stop1" not in STAGES:
                          cosT = sb("cosT", [128, S], F32, p2)
                          sinT = sb("sinT", [128, S], F32, p2)
                          tab_t = kb.t("ropetab")
                          posi = sb("posi", [128, S], I32, p2)
                          posi_t = kb.t("posi", dma=True)
                          kb.dma("sp", posi[:], posr_d[:, :], posi_t, writes=[posi_t])
                          rc = sb("ropec", [128, 2], F32, p2)
                          rc_t = kb.t("ropec", dma=True)
                          kb.dma("sp", rc[:], ropec_d[:, :], rc_t, writes=[rc_t])
                          ang = sb("ang", [128, S], F32, p2)
                          ang_t = kb.t("ang")
                          t1 = [sb("rt1_%d" % i, [128, 512], F32, p2) for i in range(2)]
                          t1_t = [kb.t("rt1_%d" % i) for i in range(2)]
                          t2 = [sb("rt2_%d" % i, [128, 512], F32, p2) for i in range(2)]
                          t2_t = [kb.t("rt2_%d" % i) for i in range(2)]
                          ki = posi
                          TWO_PI = 6.283185307179586
                          PI = 3.141592653589793
                          kb.op("dve", lambda: nc.vector.tensor_copy(out=ang[:, :], in_=posi[:, :]), reads=[posi_t], writes=[ang_t])
                          kb.op("dve", lambda: nc.vector.tensor_scalar(out=ang[:, :], in0=ang[:, :], scalar1=rc[:, 0:1], scalar2=None, op0=ALU.mult),
                                reads=[ang_t, rc_t], writes=[ang_t])
                          for (tab, phi) in ((sinT, 0.0), (cosT, PI / 2)):
                              kb.op("dve", lambda tab=tab, phi=phi: nc.vector.tensor_scalar(
                                  out=tab[:, :], in0=ang[:, :], scalar1=phi + PI, scalar2=1.0 / TWO_PI, op0=ALU.add, op1=ALU.mult),
                                  reads=[ang_t], writes=[tab_t])
                              kb.op("dve", lambda tab=tab: nc.vector.tensor_copy(out=ki[:, :], in_=tab[:, :]), reads=[tab_t], writes=[posi_t])
                              kb.op("dve", lambda tab=tab: nc.vector.tensor_copy(out=tab[:, :], in_=ki[:, :]), reads=[posi_t], writes=[tab_t])
                              kb.op("dve", lambda tab=tab: nc.vector.scalar_tensor_tensor(
                                  out=tab[:, :], in0=tab[:, :], scalar=-TWO_PI, in1=ang[:, :], op0=ALU.mult, op1=ALU.add),
                                  reads=[tab_t, ang_t], writes=[tab_t])
                              if phi != 0.0:
                                  kb.op("dve", lambda tab=tab, phi=phi: nc.vector.tensor_scalar(
                                      out=tab[:, :], in0=tab[:, :], scalar1=phi, scalar2=None, op0=ALU.add), reads=[tab_t], writes=[tab_t])
                              for half in range(4):
                                  sl = slice(half * 512, (half + 1) * 512)
                                  for (cmp_op, thr, corr) in ((ALU.is_gt, PI, -TWO_PI), (ALU.is_lt, -PI, TWO_PI)):
                                      kb.op("dve", lambda cmp_op=cmp_op, thr=thr, corr=corr, tab=tab, sl=sl: nc.vector.tensor_scalar(
                                          out=t1[0][:, :], in0=tab[:, sl], scalar1=thr, scalar2=corr, op0=cmp_op, op1=ALU.mult),
                                          reads=[tab_t], writes=[t1_t[0]])
                                      kb.op("dve", lambda tab=tab, sl=sl: nc.vector.tensor_tensor(
                                          out=tab[:, sl], in0=tab[:, sl], in1=t1[0][:, :], op=ALU.add),
                                          reads=[tab_t, t1_t[0]], writes=[tab_t])
                              kb.op("dve", lambda tab=tab: nc.vector.tensor_scalar(
                                  out=tab[:, :], in0=tab[:, :], scalar1=-3.14159, scalar2=3.14159, op0=ALU.max, op1=ALU.min),
                                  reads=[tab_t], writes=[tab_t])
                              kb.op("act", lambda tab=tab: nc.scalar.activation(out=tab[:, :], in_=tab[:, :], func=AF.Sin),
                                    reads=[tab_t], writes=[tab_t])
                          kb.op("dve", lambda: nc.vector.tensor_scalar(out=sinT[:, :], in0=sinT[:, :], scalar1=rc[:, 1:2], scalar2=None, op0=ALU.mult),
                                reads=[tab_t, rc_t], writes=[tab_t])
                          jobs = []
                          for j in range(4):
                              jobs.append((qr, j, qr_t[j], 4 + j, 8 + j))
                          jobs.append((kAB, 0, kAB_t[0], 12, 14))
                          jobs.append((kAB, 1, kAB_t[1], 13, 15))
                          jobs.append((iqr, 0, iqr_t[0], 16, 18))
                          jobs.append((iqr, 1, iqr_t[1], 17, 19))
                          jobs.append((ikAB, 0, ikAB_t[0], 20, 22))
                          jobs.append((ikAB, 1, ikAB_t[1], 21, 23))
                          it = 0
                          for (dst, dj, dst_t, wi, wpi) in jobs:
                              w_, w_t_ = load_w(wi)
                              wp_, wp_t_ = load_w(wpi)
                              for tc in range(4):
                                  b = it % 2
                                  it += 1
                                  bx = 2 * b
                                  bp = 2 * b + 1
                                  sl = slice(tc * 512, (tc + 1) * 512)
                                  proj_fm(w_, w_t_, bx, tc)
                                  proj_fm(wp_, wp_t_, bp, tc)
                                  kb.op("dve", lambda b=b, bx=bx, sl=sl: nc.vector.tensor_tensor(
                                      out=t1[b][:, :], in0=banks[bx][:, :], in1=cosT[:, sl], op=ALU.mult),
                                      reads=[tab_t], writes=[t1_t[b], bank_t[bx]])
                                  kb.op("dve", lambda b=b, bp=bp, sl=sl: nc.vector.tensor_tensor(
                                      out=t2[b][:, :], in0=banks[bp][:, :], in1=sinT[:, sl], op=ALU.mult),
                                      reads=[tab_t], writes=[t2_t[b], bank_t[bp]])
                                  kb.op("pool", lambda b=b, dst=dst, dj=dj, sl=sl: nc.gpsimd.tensor_tensor(
                                      out=dst[:, dj, sl], in0=t1[b][:, :], in1=t2[b][:, :], op=ALU.add),
                                      reads=[t1_t[b], t2_t[b]], writes=[dst_t])
                        kb.barrier()
                    kb.release_phase_dsems()
                y_tm = sb("ydsa_tm", [128, NT, 512], BF16, ph)
                y_tm_t = kb.t("ydsa_tm")
                with ExitStack() as p1:
                    sc = [sb("sc%d" % i, [128, S], F32, p1) for i in range(2)]
                    sc_t = [kb.t("sc%d" % i) for i in range(2)]
                    rtmp = sb("rtmp", [128, S], F32, p1)
                    rtmp_t = kb.t("rtmp")
                    junk = sb("junk", [128, S], BF16, p1)
                    mb = [sb("mb%d" % i, [128, S], BF16, p1) for i in range(2)]
                    mb_t = [kb.t("mb%d" % i) for i in range(2)]
                    pert = sb("pert", [128, S], F32, p1)
                    pert_t = kb.t("pert", dma=True)
                    kb.dma("sp", pert[:], pert_d[:, :], pert_t, writes=[pert_t])
                    MBT = sb("MBT", [128, NT, 512], BF16, p1)
                    MBT_t = kb.t("MBT")
                    PT = [sb("PTe%d" % i, [128, 512], BF16, p1) for i in range(3)]
                    PT_t = [kb.t("PTe%d" % i) for i in range(3)]
                    rden = [sb("rdene%d" % i, [128, 4], F32, p1) for i in range(2)]
                    rden_t = [kb.t("rdene%d" % i) for i in range(2)]
                    bs_ = [sb("bis%d" % i, [128, 8], F32, p1) for i in range(2)]
                    bs_t = [kb.t("bis%d" % i) for i in range(2)]
                    ncol = sb("ncol", [128, NT], F32, p1)
                    ncol_t = kb.t("ncol")
                    for qt in range(NT):
                        kb.op("pool", lambda qt=qt: nc.gpsimd.memset(ncol[:, qt:qt + 1], float(128 * (qt + 1) - 513)), writes=[ncol_t])
                    st = {"g": 0, "s": 0}
                    bufs = (PT, PT_t, rden, rden_t, st)
                    for qc in (range(4) if ("e_stop1" not in STAGES and "e_stop2" not in STAGES) else []):
                        if True:
                            def emit_scores(qs):
                                qt = 4 * qc + qs
                                n = 128 * (qt + 1)
                                b = qt % 2
                                s_ = sc[b]
                                s_t = sc_t[b]
                                nch = (n + 511) // 512
                                bt = bs_[b]
                                btt = bs_t[b]
                                m_ = mb[b]
                                m_t = mb_t[b]
                                for h4 in range(4):
                                    par = h4 % 2
                                    for ch in range(nch):
                                        c0 = ch * 512
                                        w = min(512, n - c0)
                                        kb.group("pe", [lambda ch=ch, c0=c0, w=w, h4=h4, par=par, qt=qt: nc.tensor.matmul(
                                            banks[ch][:, 0:w], lhsT=iqr[:, h4 // 2, qt * 128:(qt + 1) * 128], rhs=ikAB[:, par, c0:c0 + w],
                                            start=True, stop=True)],
                                            reads=[iqr_t[h4 // 2], ikAB_t[par]], writes=[bank_t[ch]])
                                        kb.op("act", lambda ch=ch, c0=c0, w=w: nc.scalar.activation(
                                            out=rtmp[:, c0:c0 + w], in_=banks[ch][:, 0:w], func=AF.Relu),
                                            reads=[], writes=[rtmp_t, bank_t[ch]])
                                    in1 = pert if h4 == 0 else s_
                                    in1_t = pert_t if h4 == 0 else s_t
                                    kb.op("dve", lambda s_=s_, in1=in1, h4=h4, qt=qt, n=n: nc.vector.scalar_tensor_tensor(
                                        out=s_[:, 0:n], in0=rtmp[:, 0:n], scalar=iwt[:, qt, h4:h4 + 1], in1=in1[:, 0:n],
                                        op0=ALU.mult, op1=ALU.add), reads=[rtmp_t, iwt_t, in1_t], writes=[s_t])
                                kb.op("dve", lambda s_=s_, n=n: nc.vector.memset(s_[0:64, n - 64:n], -1e30), reads=[s_t], writes=[s_t])
                            def emit_bisect(qs):
                                qt = 4 * qc + qs
                                n = 128 * (qt + 1)
                                b = qt % 2
                                s_ = sc[b]
                                s_t = sc_t[b]
                                nch = (n + 511) // 512
                                bt = bs_[b]
                                btt = bs_t[b]
                                m_ = mb[b]
                                m_t = mb_t[b]
                                bt = bs_[b]
                                btt = bs_t[b]
                                use_act = (qt % 2 == 1) and ("noacttopk" not in STAGES)
                                if qt < 2:
                                    kb.op("dve", lambda bt=bt: nc.vector.memset(bt[:, 3:4], -1e29), writes=[btt])
                                elif not use_act:
                                    kb.op("dve", lambda bt=bt: nc.vector.memset(bt[:, 0:1], 0.0), writes=[btt])
                                    for i in range(NIT):
                                        dl = R0 / (2 ** (i + 1))
                                        kb.op("dve", lambda s_=s_, n=n, bt=bt: nc.vector.tensor_scalar(
                                            out=junk[:, 0:n], in0=s_[:, 0:n], scalar1=bt[:, 0:1], scalar2=None, op0=ALU.is_gt, op1=ALU.add,
                                            accum_out=bt[:, 1:2]), reads=[s_t, btt], writes=[btt])
                                        kb.op("dve", lambda bt=bt, dl=dl: nc.vector.tensor_scalar(
                                            out=bt[:, 2:3], in0=bt[:, 1:2], scalar1=256.5, scalar2=2.0 * dl, op0=ALU.is_ge, op1=ALU.mult),
                                            reads=[btt], writes=[btt])
                                        kb.op("dve", lambda bt=bt, dl=dl: nc.vector.scalar_tensor_tensor(
                                            out=bt[:, 0:1], in0=bt[:, 2:3], scalar=-dl, in1=bt[:, 0:1], op0=ALU.add, op1=ALU.add),
                                            reads=[btt], writes=[btt])
                                    kb.op("dve", lambda bt=bt: nc.vector.tensor_scalar(
                                        out=bt[:, 3:4], in0=bt[:, 0:1], scalar1=R0 / (2 ** NIT), scalar2=None, op0=ALU.add),
                                        reads=[btt], writes=[btt])
                                else:
                                    kb.op("act", lambda bt=bt: nc.scalar.activation(out=bt[:, 0:1], in_=zero_c[:, 0:1], func=AF.Identity),
                                          reads=[zeroc_t], writes=[btt])
                                    for i in range(NIT):
                                        dl = R0 / (2 ** (i + 1))
                                        kb.op("act", lambda s_=s_, n=n, bt=bt: nc.scalar.activation(
                                            out=junk[:, 0:n], in_=s_[:, 0:n], func=AF.Sign, bias=bt[:, 0:1], accum_out=bt[:, 1:2]),
                                            reads=[s_t, btt], writes=[btt])
                                        kb.op("act", lambda bt=bt, qt=qt: nc.scalar.activation(
                                            out=bt[:, 2:3], in_=bt[:, 1:2], func=AF.Sign, bias=ncol[:, qt:qt + 1]),
                                            reads=[btt, ncol_t], writes=[btt])
                                        kb.op("act", lambda bt=bt, dl=dl: nc.scalar.activation(
                                            out=bt[:, 0:1], in_=bt[:, 2:3], func=AF.Identity, scale=-dl, bias=bt[:, 0:1]),
                                            reads=[btt], writes=[btt])
                                    kb.op("act", lambda bt=bt: nc.scalar.activation(
                                        out=bt[:, 3:4], in_=bt[:, 0:1], func=AF.Identity, scale=-1.0, bias=tauc[:, 0:1]),
                                        reads=[btt, tauc_t], writes=[btt])
                            def emit_mask(qs):
                                qt = 4 * qc + qs
                                n = 128 * (qt + 1)
                                b = qt % 2
                                s_ = sc[b]
                                s_t = sc_t[b]
                                nch = (n + 511) // 512
                                bt = bs_[b]
                                btt = bs_t[b]
                                m_ = mb[b]
                                m_t = mb_t[b]
                                m_ = mb[b]
                                m_t = mb_t[b]
                                kb.op("dve", lambda m_=m_, s_=s_, n=n, bt=bt: nc.vector.tensor_scalar(
                                    out=m_[:, 0:n], in0=s_[:, 0:n], scalar1=bt[:, 3:4], scalar2=-30000.0, op0=ALU.is_le, op1=ALU.mult),
                                    reads=[s_t, btt], writes=[m_t])
                                for k0 in range(0, qt + 1, 4):
                                    kn = min(4, qt + 1 - k0)
                                    bi = (k0 // 4) % 4
                                    bkb = banks[bi][:, :].bitcast(BF16)
                                    fns = [(lambda k=k, bkb=bkb, m_=m_, k0=k0: nc.tensor.transpose(
                                        out=bkb[:, k * 128:(k + 1) * 128], in_=m_[:, (k0 + k) * 128:(k0 + k + 1) * 128], identity=ident_b[:]))
                                        for k in range(kn)]
                                    kb.group("pe", fns, reads=[m_t, ident_b_t], writes=[bank_t[bi]])
                                    dst = MBT[:, k0:k0 + kn, qs * 128:(qs + 1) * 128]
                                    src = bkb[:, 0:kn * 128].rearrange("p (k n) -> p k n", k=kn)
                                    kb.op("dve", lambda dst=dst, src=src: nc.vector.tensor_copy(out=dst, in_=src),
                                          reads=[], writes=[MBT_t, bank_t[bi]])

                            for pr in range(2):
                                emit_scores(2 * pr)
                                emit_scores(2 * pr + 1)
                                emit_bisect(2 * pr)
                                emit_bisect(2 * pr + 1)
                                emit_mask(2 * pr)
                                emit_mask(2 * pr + 1)
                        def mask_fn(kt, qc_, r):
                            return (MBT[:, kt, r * 128:512], 512 - 128 * r, MBT_t)
                        for h in (range(8) if "e_stop3" not in STAGES else []):
                            attn_group(h, qc, bufs,
                                       k_lhsT=lambda kt, h=h: kAB[:, h % 2, kt * 128:(kt + 1) * 128],
                                       q_rhs=lambda c0, c1, h=h: qr[:, h // 2, c0:c1],
                                       v_rhs=lambda kt: vaug[:, kt, :],
                                       mask_fn=mask_fn, y_tm=y_tm, y_tm_t=y_tm_t,
                                       kin_t=[qr_t[h // 2], kAB_t[h % 2]], vin_t=[vaug_t])
                    kb.barrier()
                    kb.release_phase_dsems()
                with ExitStack() as p1:
                    ydT = sb("ydT", [128, 4, S], BF16, p1)
                    ydT_t = [kb.t("ydT%d" % c) for c in range(4)]
                    tm_to_fm(y_tm, y_tm_t, ydT, ydT_t, 0)
                    mixer_out(p1, None, ypT_t + ydT_t, evwo_d[:, :], "e",
                              ym_fn=lambda c, sl: (ypT[:, c, sl] if c < 4 else ydT[:, c - 4, sl]))
                    kb.barrier()
                    kb.release_phase_dsems()

        for l in range(2):
            if "mix%d" % l in STAGES:
                if l == 1:
                    odd_phase(l)
                else:
                    even_phase(l)
            if "ffn%d" % l in STAGES:
                ffn_phase(l)
            if "ple%d" % l in STAGES:
                ple_phase(l)

        with ExitStack() as ph:
            rmsnorm(ph, 6, hT, hT_t)
            ost = [sb("ost%d" % i, [128, D], F32, ph) for i in range(2)]
            ost_t = [kb.t("ost%d" % i, dma=True) for i in range(2)]
            for tt in range(NT):
                b = tt % 2
                for half in range(2):
                    bi = 4 + (2 * tt + half) % 4
                    bk = banks[bi]
                    fns = []
                    for k in range(4):
                        c = half * 4 + k
                        fns.append(lambda k=k, c=c, bk=bk: nc.tensor.transpose(
                            out=bk[:, k * 128:(k + 1) * 128], in_=hT[:, c, tt * 128:(tt + 1) * 128], identity=ident_f[:]))
                    kb.group("pe", fns, reads=hT_t[half * 4:(half + 1) * 4] + [ident_f_t], writes=[bank_t[bi]])
                    eng = alt(["dve", "act"])
                    dst = ost[b][:, half * 512:(half + 1) * 512]
                    if eng == "dve":
                        kb.op("dve", lambda dst=dst, bk=bk: nc.vector.tensor_copy(out=dst, in_=bk[:, :]),
                              reads=[bank_t[bi]], writes=[ost_t[b]])
                    else:
                        kb.op("act", lambda dst=dst, bk=bk: nc.scalar.copy(out=dst, in_=bk[:, :]),
                              reads=[bank_t[bi]], writes=[ost_t[b]])
                kb.dma("sp", out_d[tt * 128:(tt + 1) * 128, :], ost[b][:], ost_t[b], reads=[ost_t[b]])
            kb.barrier()
    return nc


def prep_inputs(inputs):
    f = lambda a: np.ascontiguousarray(np.asarray(a, dtype=np.float32))
    x = f(inputs["x"])
    p = f(inputs["p"])
    pos = np.ascontiguousarray(np.asarray(inputs["positions"], dtype=np.int32))
    gl = []
    for l in range(2):
        for nm in ("norm_mix", "norm_ffn", "norm_ple"):
            gl.append(f(inputs[nm])[l])
    gl.append(f(inputs["norm_final"]))
    gains = np.ascontiguousarray(np.stack([g.reshape(8, 128).T for g in gl], axis=1))
    shared = {"gains": gains, "ident": np.eye(128, dtype=np.float32)}
    ow = f(inputs["od_w_in"])[0]

    def kcn(w):
        n = w.shape[1]
        return np.ascontiguousarray(w.reshape(8, 128, n).transpose(1, 0, 2)).reshape(128, 8 * n)
    shared["odw"] = np.stack([kcn(ow[:, 0:512]), kcn(ow[:, 512:1024]), kcn(ow[:, 1024:1536]),
                              kcn(ow[:, 1544:2056]), kcn(ow[:, 2056:2568])], axis=0)
    shared["odw2"] = kcn(np.concatenate([ow[:, 1024:1536], ow[:, 1536:1544], np.zeros((1024, 120), np.float32)], axis=1))
    shared["odwo"] = kcn(f(inputs["od_w_out"])[0])
    shared["odqk"] = np.stack([kcn(np.concatenate([ow[:, h * 64:(h + 1) * 64], ow[:, 512 + h * 64:512 + (h + 1) * 64]], axis=1))
                               for h in range(8)], axis=0)
    shared["lng"] = np.ascontiguousarray(np.broadcast_to(f(inputs["sgu_ln_g"])[0][None, :], (128, 512)))
    shared["lnb"] = np.ascontiguousarray(np.broadcast_to(f(inputs["sgu_ln_b"])[0][None, :], (128, 512)))
    shared["sguw"] = np.ascontiguousarray(f(inputs["sgu_w"])[0].transpose(2, 0, 1))
    ci = np.arange(128) // 64
    shared["sgumask"] = np.ascontiguousarray((ci[:, None] <= ci[None, :]).astype(np.float32))
    sbb = f(inputs["sgu_b"])[0]
    shared["sgub"] = np.ascontiguousarray(np.broadcast_to(np.tile(sbb, (1, 4))[None, :, :], (128, 4, 512)))
    shared["foxbf"] = np.ascontiguousarray(f(inputs["fox_b_f"])[0].reshape(8, 1))
    ar = np.arange(128)
    shared["cmask"] = np.where(ar[:, None] <= ar[None, :], 0.0, -30000.0).astype(np.float32)
    ew = f(inputs["ev_w_in"])[0]

    def permh(w):
        nh = w.shape[1] // 64
        idx = []
        for h_ in range(nh):
            for d_ in range(64):
                pd = d_ + 8 if d_ < 8 else (d_ - 8 if d_ < 16 else d_)
                idx.append(h_ * 64 + pd)
        return w[:, idx]
    z64 = np.zeros((1024, 64), np.float32)
    qW = ew[:, 512:1024]
    kW = ew[:, 1024:1088]
    iqW = ew[:, 1152:1408]
    ikW = ew[:, 1408:1472]
    tl = [ew[:, g_ * 128:(g_ + 1) * 128] for g_ in range(4)]
    tl += [qW[:, j_ * 128:(j_ + 1) * 128] for j_ in range(4)]
    tl += [permh(qW)[:, j_ * 128:(j_ + 1) * 128] for j_ in range(4)]
    tl += [np.concatenate([kW, z64], 1), np.concatenate([z64, kW], 1),
           np.concatenate([permh(kW), z64], 1), np.concatenate([z64, permh(kW)], 1)]
    tl += [iqW[:, j_ * 128:(j_ + 1) * 128] for j_ in range(2)]
    tl += [permh(iqW)[:, j_ * 128:(j_ + 1) * 128] for j_ in range(2)]
    tl += [np.concatenate([ikW, z64], 1), np.concatenate([z64, ikW], 1),
           np.concatenate([permh(ikW), z64], 1), np.concatenate([z64, permh(ikW)], 1)]
    tl += [np.concatenate([ew[:, 1088:1152], ew[:, 1472:1476], np.zeros((1024, 60), np.float32)], 1)]
    shared["evw"] = np.stack([kcn(t_) for t_ in tl], axis=0)
    shared["evwo"] = kcn(f(inputs["ev_w_out"])[0])
    shared["poolw"] = np.ascontiguousarray(f(inputs["pool_w"])[0].transpose(1, 0, 2))
    pc = np.zeros((128, 68), np.float32)
    pc[:, 0:4] = f(inputs["pool_scale"])[0].reshape(4, 128).T
    for g_, w_ in enumerate((2, 4, 8, 16)):
        pc[:, 4 + g_ * 16:4 + (g_ + 1) * 16] = (1.0 / np.minimum(np.arange(16) + 1, w_)).astype(np.float32)[None, :]
    shared["poolc"] = pc
    inv = (np.float32(500000.0) ** (-np.arange(8, dtype=np.float32) / np.float32(8))).astype(np.float32)
    rc_ = np.zeros((128, 2), np.float32)
    for p_ in range(128):
        d_ = p_ % 64
        if d_ < 16:
            rc_[p_, 0] = inv[d_ % 8]
            rc_[p_, 1] = -1.0 if d_ < 8 else 1.0
    shared["ropec"] = rc_
    shared["pert"] = np.ascontiguousarray(np.broadcast_to((-(2.0 ** -22) * np.arange(S, dtype=np.float64)).astype(np.float32)[None, :], (128, S)))
    cw = f(inputs["ffn_conv_w"])
    cb = f(inputs["ffn_conv_b"])
    cwb = np.concatenate([cw.reshape(2, 3, 44, 128).transpose(0, 3, 2, 1), cb.reshape(2, 44, 128).transpose(0, 2, 1)[..., None]], axis=-1)
    shared["cwb"] = np.ascontiguousarray(cwb)
    wi = f(inputs["ffn_w_in"]).reshape(2, 8, 128, 2, 22, 128)
    shared["win"] = np.ascontiguousarray(wi.transpose(0, 4, 2, 3, 1, 5)).reshape(2, 22, 128, 2 * 8 * 128)
    wo = f(inputs["ffn_w_out"]).reshape(2, 2, 11, 128, 8, 128)
    shared["wout"] = np.ascontiguousarray(wo.transpose(0, 1, 4, 3, 2, 5)).reshape(2, 2, 8, 128, 11 * 128)
    wgt = f(inputs["ple_w_gate"]).reshape(2, 8, 128, 8, 128)
    shared["wgate"] = np.ascontiguousarray(wgt.transpose(0, 3, 2, 1, 4)).reshape(2, 8, 128, 8 * 128)
    wpj = f(inputs["ple_w_proj"]).reshape(2, 2, 128, 8, 128)
    shared["wproj"] = np.ascontiguousarray(wpj.transpose(0, 2, 3, 1, 4)).reshape(2, 128, 8 * 2 * 128)
    in_maps = []
    for b in range(8):
        m = dict(shared)
        m["x"] = np.ascontiguousarray(x[b])
        m["p"] = np.ascontiguousarray(p[:, b])
        m["pos"] = np.ascontiguousarray(pos[b:b + 1])
        m["posr"] = np.ascontiguousarray(np.broadcast_to(pos[b:b + 1], (128, S)))
        in_maps.append(m)
    return in_maps


_CACHE = {}


def kernel(**inputs):
    in_maps = prep_inputs(inputs)
    if "nc" not in _CACHE:
        _CACHE["nc"] = build_program()
    nc = _CACHE["nc"]
    ncores = int(_CACHE.get("dev_cores", 8))
    res = run_bass_kernel_spmd(nc, in_maps[:ncores], core_ids=list(range(ncores)))
    _CACHE["last_res"] = res
    out = np.stack([np.asarray(r["out"], dtype=np.float32) for r in res.results], axis=0)
    return out
```
